# Optimizing a Trainium2 kernel written in Bass

```python
import math
import jax, jax.numpy as jnp
from jax import lax
import numpy as np

D_MODEL = 2048
BATCH = 16
SEQ = 256
DEPTH = 4
DEC_BATCH = 4
DEC_SEQ = 1024
PAST_LEN = 512

GRID_W = 64
N_MIXERS = 4
N_LAYERS_A = (DEPTH + N_MIXERS - 1) // N_MIXERS
N_LAYERS_B = (DEPTH + N_MIXERS - 2) // N_MIXERS
N_LAYERS_C = (DEPTH + N_MIXERS - 3) // N_MIXERS
N_LAYERS_D = (DEPTH + N_MIXERS - 4) // N_MIXERS
D_FF = 4 * D_MODEL
N_MOD = 6
RMS_EPS = 1e-6
ROPE_THETA = 10000.0
Q_BLOCK = 128

A_HEADS = 16
A_HALF_DIM = D_MODEL // A_HEADS // 2
A_V_DIM = 2 * A_HALF_DIM

B_HEADS = 8
B_DK = D_MODEL // 2 // B_HEADS
B_DV = D_MODEL // B_HEADS
B_CHUNK = 64
B_IN = 2 * B_HEADS * B_DK + 2 * B_HEADS * B_DV + 4 * B_HEADS

C_HEADS = 16
C_HEAD_DIM = D_MODEL // C_HEADS
NA_ROWS_MAX = 8
NA_COLS = 16
NA_QCOLS = 16
NA_KCOLS = 32

D_HEADS = 32
D_KV_HEADS = 8
D_GROUP = D_HEADS // D_KV_HEADS
D_HEAD_DIM = D_MODEL // D_HEADS
WINDOW = 128

kernel_name = 'hybrid_diffusion_trunk_step'


def rmsnorm(x, g):
    xf = x.astype(jnp.float32)
    y = xf * lax.rsqrt(jnp.mean(xf * xf, axis=-1, keepdims=True) + RMS_EPS)
    return (y * g.astype(jnp.float32)).astype(x.dtype)


def adaln_in(x, g, shift, scale):
    return rmsnorm(x, g) * (1.0 + scale) + shift


def modulation(cvec, w, b):
    m = (jax.nn.silu(cvec) @ w + b)[:, None, :]
    return jnp.split(m, N_MOD, axis=-1)


def sqrelu_mlp(h, w1, w2):
    return jnp.square(jax.nn.relu(h @ w1)) @ w2


def rope_half(x, pos):
    half = x.shape[-1] // 2
    inv = ROPE_THETA ** (-jnp.arange(half, dtype=jnp.float32) / half)
    ang = pos.astype(jnp.float32)[:, None] * inv
    ang = ang.reshape((pos.shape[0],) + (1,) * (x.ndim - 3) + (half,))
    cos, sin = jnp.cos(ang).astype(x.dtype), jnp.sin(ang).astype(x.dtype)
    x1, x2 = x[..., :half], x[..., half:]
    return jnp.concatenate([x1 * cos - x2 * sin, x1 * sin + x2 * cos], axis=-1)


def axial_rope(x):
    t = jnp.arange(x.shape[1])
    d = x.shape[-1] // 2
    return jnp.concatenate([rope_half(x[..., :d], t // GRID_W), rope_half(x[..., d:], t % GRID_W)], axis=-1)


def map_query_blocks(fn, q):
    B, T = q.shape[:2]
    nb = T // Q_BLOCK
    qb = jnp.swapaxes(q.reshape((B, nb, Q_BLOCK) + q.shape[2:]), 0, 1)
    out = lax.map(fn, (qb, jnp.arange(nb)))
    return jnp.swapaxes(out, 0, 1).reshape((B, T) + out.shape[3:])


def softmax_attn_block(qb, k, v):
    s = jnp.einsum('bqhd,bkhd->bhqk', qb, k, preferred_element_type=jnp.float32) * (qb.shape[-1] ** -0.5)
    p = jax.nn.softmax(s, axis=-1)
    return jnp.einsum('bhqk,bkhd->bqhd', p.astype(v.dtype), v)


def diff_split(h, w_in):
    B, T, _ = h.shape
    q, k, v = jnp.split(h @ w_in, 3, axis=-1)
    return (q.reshape(B, T, A_HEADS, 2, A_HALF_DIM), k.reshape(B, T, A_HEADS, 2, A_HALF_DIM),
            v.reshape(B, T, A_HEADS, A_V_DIM))


def diff_lambda(lam_p, lam_init):
    lp = lam_p.astype(jnp.float32)
    return jnp.exp(jnp.sum(lp[0] * lp[1], -1)) - jnp.exp(jnp.sum(lp[2] * lp[3], -1)) + lam_init


def diff_core(qb, k, v, lam):
    s = jnp.einsum('bqhjd,bkhjd->bhjqk', qb, k, preferred_element_type=jnp.float32) * (A_HALF_DIM ** -0.5)
    p = jax.nn.softmax(s, axis=-1)
    a = p[:, :, 0] - lam[None, :, None, None] * p[:, :, 1]
    return jnp.einsum('bhqk,bkhd->bqhd', a.astype(v.dtype), v)


def diff_out(o, subln, lam_init, w_out):
    B, T = o.shape[:2]
    return (rmsnorm(o, subln) * (1.0 - lam_init)).reshape(B, T, -1) @ w_out


def diff_attn_context(h, w_in, w_out, lam_p, subln, lam_init):
    B, T, _ = h.shape
    q, k, v = diff_split(h, w_in)
    lam = diff_lambda(lam_p, lam_init)
    o = map_query_blocks(lambda a: diff_core(a[0], k, v, lam), q)
    return diff_out(o, subln, lam_init, w_out), k.reshape(B, T, A_HEADS, A_V_DIM), v


def diff_attn_latent(h, ck, cv, w_in, w_out, lam_p, subln, lam_init):
    B, T, _ = h.shape
    q, k, v = diff_split(h, w_in)
    q, k = axial_rope(q), axial_rope(k)
    kk = jnp.concatenate([ck.reshape(B, ck.shape[1], A_HEADS, 2, A_HALF_DIM).astype(k.dtype), k], axis=1)
    vv = jnp.concatenate([cv.astype(v.dtype), v], axis=1)
    lam = diff_lambda(lam_p, lam_init)
    o = map_query_blocks(lambda a: diff_core(a[0], kk, vv, lam), q)
    return diff_out(o, subln, lam_init, w_out)


def mlstm_chunked(q, k, v, ig, lf, C0, n0, m0):
    B, T = q.shape[:2]
    L = B_CHUNK
    nc = T // L

    def to_chunks(a):
        return jnp.swapaxes(a.reshape((B, nc, L) + a.shape[2:]), 0, 1)

    xs = (to_chunks(q), to_chunks(k), to_chunks(v), to_chunks(ig), to_chunks(lf))
    causal = jnp.tril(jnp.ones((L, L), dtype=bool))

    def step(carry, xc):
        C, n, m = carry
        qc, kc, vc, ic, fc = xc
        b = jnp.cumsum(fc, axis=1)
        dm = b[:, :, None, :] - b[:, None, :, :] + ic[:, None, :, :]
        dm = jnp.where(causal[None, :, :, None], dm, -jnp.inf)
        inter = b + m[:, None, :]
        m_t = jnp.maximum(inter, jnp.max(dm, axis=2))
        w = jnp.exp(dm - m_t[:, :, None, :])
        a_in = jnp.exp(inter - m_t)
        sw = jnp.einsum('bthd,bshd->btsh', qc, kc, preferred_element_type=jnp.float32) * w
        num = (jnp.einsum('btsh,bshv->bthv', sw, vc, preferred_element_type=jnp.float32)
               + a_in[..., None] * jnp.einsum('bhvd,bthd->bthv', C, qc, preferred_element_type=jnp.float32))
        den = jnp.sum(sw, axis=2) + a_in * jnp.einsum('bhd,bthd->bth', n, qc, preferred_element_type=jnp.float32)
        h = num / jnp.maximum(jnp.abs(den), jnp.exp(-m_t))[..., None]
        wl, al = w[:, -1], a_in[:, -1]
        C_new = al[..., None, None] * C + jnp.einsum('bsh,bshv,bshd->bhvd', wl, vc, kc, preferred_element_type=jnp.float32)
        n_new = al[..., None] * n + jnp.einsum('bsh,bshd->bhd', wl, kc, preferred_element_type=jnp.float32)
        return (C_new, n_new, m_t[:, -1]), h

    init = (C0.astype(jnp.float32), n0.astype(jnp.float32), m0.astype(jnp.float32))
    (C, n, m), hs = lax.scan(step, init, xs)
    hs = jnp.swapaxes(hs, 0, 1).reshape(B, T, B_HEADS, B_DV)
    return hs, C, n, m


def mlstm_proj(h, w_in, gate_bias):
    B, T, _ = h.shape
    qk, vd = B_HEADS * B_DK, B_HEADS * B_DV
    q, k, v, o, g = jnp.split(h @ w_in, [qk, 2 * qk, 2 * qk + vd, 2 * qk + 2 * vd], axis=-1)
    q = q.reshape(B, T, B_HEADS, B_DK) * (B_DK ** -0.5)
    k = k.reshape(B, T, B_HEADS, B_DK)
    v = v.reshape(B, T, B_HEADS, B_DV)
    g = g.astype(jnp.float32).reshape(B, T, 4, B_HEADS) + gate_bias.astype(jnp.float32)
    return q, k, v, o, g


def _rev(a, d):
    return a if d == 0 else jnp.flip(a, axis=1)


def mlstm_bidir(q, k, v, g, C0, n0, m0):
    hsum, Cs, ns, ms = 0.0, [], [], []
    for d in range(2):
        ig = g[:, :, 2 * d]
        lf = jax.nn.log_sigmoid(g[:, :, 2 * d + 1])
        hd, Cd, nd, md = mlstm_chunked(_rev(q, d), _rev(k, d), _rev(v, d), _rev(ig, d), _rev(lf, d),
                                       C0[:, d], n0[:, d], m0[:, d])
        hsum = hsum + _rev(hd, d)
        Cs.append(Cd); ns.append(nd); ms.append(md)
    return hsum, jnp.stack(Cs, axis=1), jnp.stack(ns, axis=1), jnp.stack(ms, axis=1)


def mlstm_out(hsum, o, norm_w, w_out):
    B, T = hsum.shape[:2]
    hn = rmsnorm(hsum.astype(o.dtype), norm_w.reshape(B_HEADS, B_DV))
    return (hn.reshape(B, T, -1) * jax.nn.sigmoid(o)) @ w_out


def mlstm_context(h, w_in, gate_bias, norm_w, w_out):
    B = h.shape[0]
    q, k, v, o, g = mlstm_proj(h, w_in, gate_bias)
    C0 = jnp.zeros((B, 2, B_HEADS, B_DV, B_DK), jnp.float32)
    n0 = jnp.zeros((B, 2, B_HEADS, B_DK), jnp.float32)
    m0 = jnp.zeros((B, 2, B_HEADS), jnp.float32)
    hsum, C, n, m = mlstm_bidir(q, k, v, g, C0, n0, m0)
    return mlstm_out(hsum, o, norm_w, w_out), C, n, m


def mlstm_latent(h, C0, n0, m0, w_in, gate_bias, norm_w, w_out):
    q, k, v, o, g = mlstm_proj(h, w_in, gate_bias)
    hsum, _, _, _ = mlstm_bidir(q, k, v, g, C0, n0, m0)
    return mlstm_out(hsum, o, norm_w, w_out)


def na_split(h, w_in):
    B, T, _ = h.shape
    q, k, v = jnp.split(h @ w_in, 3, axis=-1)
    return tuple(a.reshape(B, T, C_HEADS, C_HEAD_DIM) for a in (q, k, v))


def na_context(h, w_in, w_out):
    B, T, _ = h.shape
    q, k, v = na_split(h, w_in)
    o = map_query_blocks(lambda a: softmax_attn_block(a[0], k, v), q)
    return o.reshape(B, T, -1) @ w_out, k, v


def na_latent(h, ck, cv, w_in, w_out, rpb):
    B, T, _ = h.shape
    rows = T // GRID_W
    kh = min(NA_ROWS_MAX, rows)
    q, k, v = na_split(h, w_in)
    ck, cv = ck.astype(k.dtype), cv.astype(v.dtype)
    qg = q.reshape(B, rows, GRID_W, C_HEADS, C_HEAD_DIM)
    kg = k.reshape(B, rows, GRID_W, C_HEADS, C_HEAD_DIM)
    vg = v.reshape(B, rows, GRID_W, C_HEADS, C_HEAD_DIM)
    ncb = GRID_W // NA_QCOLS
    col0 = np.clip(np.arange(ncb) * NA_QCOLS - NA_COLS // 2, 0, GRID_W - NA_KCOLS)
    kcols = col0[:, None] + np.arange(NA_KCOLS)
    qcols = np.arange(ncb)[:, None] * NA_QCOLS + np.arange(NA_QCOLS)
    cstart = np.clip(qcols - NA_COLS // 2, 0, GRID_W - NA_COLS)
    col_ok = (kcols[:, None, :] >= cstart[..., None]) & (kcols[:, None, :] < cstart[..., None] + NA_COLS)
    dcol = np.clip(kcols[:, None, :] - qcols[..., None], -(NA_COLS - 1), NA_COLS - 1) + NA_COLS - 1
    bias_col = rpb.astype(jnp.float32)[:, :, dcol]
    n_nb = kh * NA_KCOLS
    scale = C_HEAD_DIM ** -0.5

    def row_fn(a):
        qr, r = a
        r0 = jnp.clip(r - kh // 2, 0, rows - kh)
        kr = lax.dynamic_slice_in_dim(kg, r0, kh, axis=1)[:, :, kcols]
        vr = lax.dynamic_slice_in_dim(vg, r0, kh, axis=1)[:, :, kcols]
        qb = qr.reshape(B, ncb, NA_QCOLS, C_HEADS, C_HEAD_DIM)
        s_nb = jnp.einsum('bjqhd,bijkhd->bhjqik', qb, kr, preferred_element_type=jnp.float32) * scale
        drow = r0 + jnp.arange(kh) - r + NA_ROWS_MAX - 1
        bias = jnp.transpose(jnp.take(bias_col, drow, axis=1), (0, 2, 3, 1, 4))
        s_nb = jnp.where(col_ok[:, :, None, :], s_nb + bias, -jnp.inf)
        s_ctx = jnp.einsum('bjqhd,bkhd->bhjqk', qb, ck, preferred_element_type=jnp.float32) * scale
        p = jax.nn.softmax(jnp.concatenate([s_nb.reshape(B, C_HEADS, ncb, NA_QCOLS, n_nb), s_ctx], axis=-1), axis=-1)
        p = p.astype(v.dtype)
        p_nb = p[..., :n_nb].reshape(B, C_HEADS, ncb, NA_QCOLS, kh, NA_KCOLS)
        o = (jnp.einsum('bhjqik,bijkhd->bjqhd', p_nb, vr)
             + jnp.einsum('bhjqk,bkhd->bjqhd', p[..., n_nb:], cv))
        return o.reshape(B, GRID_W, C_HEADS, C_HEAD_DIM)

    o = lax.map(row_fn, (jnp.swapaxes(qg, 0, 1), jnp.arange(rows)))
    return jnp.swapaxes(o, 0, 1).reshape(B, T, -1) @ w_out


def gqa_split(h, w_in):
    B, T, _ = h.shape
    q, k, v = jnp.split(h @ w_in, [D_HEADS * D_HEAD_DIM, (D_HEADS + D_KV_HEADS) * D_HEAD_DIM], axis=-1)
    return (q.reshape(B, T, D_KV_HEADS, D_GROUP, D_HEAD_DIM), k.reshape(B, T, D_KV_HEADS, D_HEAD_DIM),
            v.reshape(B, T, D_KV_HEADS, D_HEAD_DIM))


def sink_probs(s, sk):
    skb = sk[None, :, :, None, None]
    m = jnp.maximum(jnp.max(s, axis=-1, keepdims=True), skb)
    e = jnp.exp(s - m)
    return e / (jnp.sum(e, axis=-1, keepdims=True) + jnp.exp(skb - m))


def gqa_context(h, w_in, w_out, sink):
    B, T, _ = h.shape
    q, k, v = gqa_split(h, w_in)
    sk = sink.astype(jnp.float32).reshape(D_KV_HEADS, D_GROUP)

    def blk(a):
        s = jnp.einsum('bqhgd,bkhd->bhgqk', a[0], k, preferred_element_type=jnp.float32) * (D_HEAD_DIM ** -0.5)
        return jnp.einsum('bhgqk,bkhd->bqhgd', sink_probs(s, sk).astype(v.dtype), v)

    o = map_query_blocks(blk, q)
    return o.reshape(B, T, -1) @ w_out, k, v


def gqa_latent(h, ck, cv, w_in, w_out, sink):
    B, T, _ = h.shape
    q, k, v = gqa_split(h, w_in)
    q, k = axial_rope(q), axial_rope(k)
    ck, cv = ck.astype(k.dtype), cv.astype(v.dtype)
    pad = ((0, 0), (Q_BLOCK, Q_BLOCK), (0, 0), (0, 0))
    kp, vp = jnp.pad(k, pad), jnp.pad(v, pad)
    band = 3 * Q_BLOCK
    sk = sink.astype(jnp.float32).reshape(D_KV_HEADS, D_GROUP)
    scale = D_HEAD_DIM ** -0.5

    def blk(a):
        qb, bi = a
        kb = lax.dynamic_slice_in_dim(kp, bi * Q_BLOCK, band, axis=1)
        vb = lax.dynamic_slice_in_dim(vp, bi * Q_BLOCK, band, axis=1)
        qpos = bi * Q_BLOCK + jnp.arange(Q_BLOCK)
        kpos = (bi - 1) * Q_BLOCK + jnp.arange(band)
        valid = (kpos[None, :] >= 0) & (kpos[None, :] < T) & (jnp.abs(qpos[:, None] - kpos[None, :]) <= WINDOW)
        s_loc = jnp.einsum('bqhgd,bkhd->bhgqk', qb, kb, preferred_element_type=jnp.float32) * scale
        s_loc = jnp.where(valid, s_loc, -jnp.inf)
        s_ctx = jnp.einsum('bqhgd,bkhd->bhgqk', qb, ck, preferred_element_type=jnp.float32) * scale
        p = sink_probs(jnp.concatenate([s_loc, s_ctx], axis=-1), sk).astype(v.dtype)
        return (jnp.einsum('bhgqk,bkhd->bqhgd', p[..., :band], vb)
                + jnp.einsum('bhgqk,bkhd->bqhgd', p[..., band:], cv))

    o = map_query_blocks(blk, q)
    return o.reshape(B, T, -1) @ w_out


def setup_inputs(seed: int = 0) -> dict:
    key = jax.random.key(seed)
    ks = jax.random.split(key, 40)
    D = D_MODEL

    def nrm(i, shape, s=1.0):
        return s * jax.random.normal(ks[i], shape, jnp.float32)

    return {
        'x_prompt': nrm(0, (BATCH, SEQ, D)),
        'x_sample': nrm(1, (DEC_BATCH, DEC_SEQ, D)),
        'cache_a_k': nrm(2, (DEC_BATCH, N_LAYERS_A, PAST_LEN, A_HEADS, A_V_DIM)),
        'cache_a_v': nrm(3, (DEC_BATCH, N_LAYERS_A, PAST_LEN, A_HEADS, A_V_DIM)),
        'state_b_C': nrm(4, (DEC_BATCH, N_LAYERS_B, 2, B_HEADS, B_DV, B_DK)),
        'state_b_n': nrm(5, (DEC_BATCH, N_LAYERS_B, 2, B_HEADS, B_DK)),
        'state_b_m': nrm(6, (DEC_BATCH, N_LAYERS_B, 2, B_HEADS)),
        'cache_c_k': nrm(7, (DEC_BATCH, N_LAYERS_C, PAST_LEN, C_HEADS, C_HEAD_DIM)),
        'cache_c_v': nrm(8, (DEC_BATCH, N_LAYERS_C, PAST_LEN, C_HEADS, C_HEAD_DIM)),
        'cache_d_k': nrm(9, (DEC_BATCH, N_LAYERS_D, PAST_LEN, D_KV_HEADS, D_HEAD_DIM)),
        'cache_d_v': nrm(10, (DEC_BATCH, N_LAYERS_D, PAST_LEN, D_KV_HEADS, D_HEAD_DIM)),
        'c': nrm(11, (DEC_BATCH, D)),
        'c_ctx': nrm(12, (D,)),
        'w_mod': nrm(13, (DEPTH, D, N_MOD * D), 0.5 * D ** -0.5),
        'b_mod': nrm(14, (DEPTH, N_MOD * D), 0.01),
        'g_norm': 1.0 + nrm(15, (DEPTH, 4, D), 0.05),
        'w_ff1': nrm(16, (DEPTH, D, D_FF), D ** -0.5),
        'w_ff2': nrm(17, (DEPTH, D_FF, D), D_FF ** -0.5),
        'a_w_in': nrm(18, (N_LAYERS_A, D, 3 * A_HEADS * A_V_DIM), D ** -0.5),
        'a_w_out': nrm(19, (N_LAYERS_A, A_HEADS * A_V_DIM, D), (A_HEADS * A_V_DIM) ** -0.5),
        'a_lambda': nrm(20, (N_LAYERS_A, 4, A_HEADS, A_HALF_DIM), 0.1),
        'a_subln': 1.0 + nrm(21, (N_LAYERS_A, A_V_DIM), 0.05),
        'b_w_in': nrm(22, (N_LAYERS_B, D, B_IN), D ** -0.5),
        'b_gate_bias': nrm(23, (N_LAYERS_B, 4, B_HEADS), 0.1) + jnp.array([0.0, 3.0, 0.0, 3.0], jnp.float32)[None, :, None],
        'b_w_out': nrm(24, (N_LAYERS_B, B_HEADS * B_DV, D), (B_HEADS * B_DV) ** -0.5),
        'b_norm': 1.0 + nrm(25, (N_LAYERS_B, B_HEADS * B_DV), 0.05),
        'c_w_in': nrm(26, (N_LAYERS_C, D, 3 * C_HEADS * C_HEAD_DIM), D ** -0.5),
        'c_w_out': nrm(27, (N_LAYERS_C, C_HEADS * C_HEAD_DIM, D), (C_HEADS * C_HEAD_DIM) ** -0.5),
        'c_rpb': nrm(28, (N_LAYERS_C, C_HEADS, 2 * NA_ROWS_MAX - 1, 2 * NA_COLS - 1), 0.5),
        'd_w_in': nrm(29, (N_LAYERS_D, D, (D_HEADS + 2 * D_KV_HEADS) * D_HEAD_DIM), D ** -0.5),
        'd_w_out': nrm(30, (N_LAYERS_D, D_HEADS * D_HEAD_DIM, D), (D_HEADS * D_HEAD_DIM) ** -0.5),
        'd_sink': nrm(31, (N_LAYERS_D, D_HEADS)),
    }


def reference(x_prompt, x_sample, cache_a_k, cache_a_v, state_b_C, state_b_n, state_b_m,
              cache_c_k, cache_c_v, cache_d_k, cache_d_v, c, c_ctx, w_mod, b_mod, g_norm,
              w_ff1, w_ff2, a_w_in, a_w_out, a_lambda, a_subln, b_w_in, b_gate_bias, b_w_out,
              b_norm, c_w_in, c_w_out, c_rpb, d_w_in, d_w_out, d_sink):
    xp, xs = x_prompt, x_sample
    a_k, a_v, b_C, b_n, b_m, c_k, c_v, d_k, d_v = [], [], [], [], [], [], [], [], []
    for i in range(DEPTH):
        kind, j = i % N_MIXERS, i // N_MIXERS
        mod_p = modulation(c_ctx[None, :], w_mod[i], b_mod[i])
        mod_s = modulation(c, w_mod[i], b_mod[i])
        hp = adaln_in(xp, g_norm[i, 0], mod_p[0], mod_p[1])
        hs = adaln_in(xs, g_norm[i, 0], mod_s[0], mod_s[1])
        if kind == 0:
            lam_init = 0.8 - 0.6 * math.exp(-0.3 * i)
            yp, kc, vc = diff_attn_context(hp, a_w_in[j], a_w_out[j], a_lambda[j], a_subln[j], lam_init)
            ys = diff_attn_latent(hs, cache_a_k[:, j], cache_a_v[:, j], a_w_in[j], a_w_out[j],
                                  a_lambda[j], a_subln[j], lam_init)
            a_k.append(kc); a_v.append(vc)
        elif kind == 1:
            yp, Cc, nc_, mc = mlstm_context(hp, b_w_in[j], b_gate_bias[j], b_norm[j], b_w_out[j])
            ys = mlstm_latent(hs, state_b_C[:, j], state_b_n[:, j], state_b_m[:, j],
                              b_w_in[j], b_gate_bias[j], b_norm[j], b_w_out[j])
            b_C.append(Cc); b_n.append(nc_); b_m.append(mc)
        elif kind == 2:
            yp, kc, vc = na_context(hp, c_w_in[j], c_w_out[j])
            ys = na_latent(hs, cache_c_k[:, j], cache_c_v[:, j], c_w_in[j], c_w_out[j], c_rpb[j])
            c_k.append(kc); c_v.append(vc)
        else:
            yp, kc, vc = gqa_context(hp, d_w_in[j], d_w_out[j], d_sink[j])
            ys = gqa_latent(hs, cache_d_k[:, j], cache_d_v[:, j], d_w_in[j], d_w_out[j], d_sink[j])
            d_k.append(kc); d_v.append(vc)
        xp = xp + mod_p[2] * rmsnorm(yp, g_norm[i, 1])
        xs = xs + mod_s[2] * rmsnorm(ys, g_norm[i, 1])
        hp = adaln_in(xp, g_norm[i, 2], mod_p[3], mod_p[4])
        hs = adaln_in(xs, g_norm[i, 2], mod_s[3], mod_s[4])
        xp = xp + mod_p[5] * rmsnorm(sqrelu_mlp(hp, w_ff1[i], w_ff2[i]), g_norm[i, 3])
        xs = xs + mod_s[5] * rmsnorm(sqrelu_mlp(hs, w_ff1[i], w_ff2[i]), g_norm[i, 3])
    return (xp, xs, jnp.stack(a_k, axis=1), jnp.stack(a_v, axis=1), jnp.stack(b_C, axis=1),
            jnp.stack(b_n, axis=1), jnp.stack(b_m, axis=1), jnp.stack(c_k, axis=1),
            jnp.stack(c_v, axis=1), jnp.stack(d_k, axis=1), jnp.stack(d_v, axis=1))
```

```python
import math
import numpy as np
import concourse.bass as bass
import concourse.mybir as mybir
from concourse.bass_utils import run_bass_kernel_spmd
from contextlib import ExitStack

F32 = mybir.dt.float32
BF16 = mybir.dt.bfloat16
AF = mybir.ActivationFunctionType
ALU = mybir.AluOpType

D = 2048
NT = 1024
DFF = 8192
EPS = 1e-6
NEG = -30000.0


class _Rec:
    def __init__(self):
        self.call = None

    def __getattr__(self, name):
        def f(*a, **k):
            self.call = (name, a, k)
            return None
        return f


class Sched:
    COMPUTE = ('pe', 'act', 'dve', 'pool')

    def __init__(self, nc, es):
        self.nc = nc
        self.es = es
        self.ops = []
        self.last_w = {}
        self.readers = {}
        self.engs = ('pe', 'act', 'dve', 'pool', 'sp')
        self.psem = {e: es.enter_context(nc.semaphore("P_" + e)) for e in self.COMPUTE}
        self.dma_last = {}
        self.dsems = {}
        self.last_on = {}
        self.bar = None
        self.bar_done = set()

    def barrier(self):
        self.bar = set(self.last_on.values()) | set(self.dma_last.values())
        self.bar_done = set()

    def op(self, eng, fn, reads=(), writes=(), dma=None):
        deps = set()
        for r in reads:
            if r in self.last_w:
                deps.add(self.last_w[r])
            if isinstance(r, tuple) and r[0] == 'bank':
                for k_, v_ in (self.readers.get(r) or {}).items():
                    if k_ != eng:
                        deps.add(v_)
        for w in writes:
            if w in self.last_w:
                deps.add(self.last_w[w])
            rd = self.readers.get(w)
            if rd:
                deps.update(rd.values())
        if dma is not None:
            if dma not in self.dsems:
                self.dsems[dma] = self.es.enter_context(self.nc.semaphore("D_" + dma))
            if dma in self.dma_last:
                deps.add(self.dma_last[dma])
        if self.bar is not None and eng not in self.bar_done:
            deps.update(self.bar)
            self.bar_done.add(eng)
        oid = len(self.ops)
        rec = _Rec()
        fn(rec)
        name_, a_, k_ = rec.call
        fn = (lambda E, name_=name_, a_=a_, k_=k_: getattr(E, name_)(*a_, **k_))
        self.ops.append([eng, fn, deps, dma, False, None])
        if dma is not None:
            self.dma_last[dma] = oid
        else:
            self.last_on[eng] = oid
        for w in writes:
            self.last_w[w] = oid
            self.readers[w] = {}
        for r in reads:
            d = self.readers.setdefault(r, {})
            d[eng if dma is None else ('dma', oid)] = oid
        return oid

    def truncate(self, n):
        self.ops = self.ops[:n]

    def emit(self, final_wait_eng='sp'):
        ops = self.ops
        for o in ops:
            for d in o[2]:
                do = ops[d]
                if do[3] is None and not (do[0] == 'pe' and o[0] == 'pe' and o[3] is None):
                    do[4] = True
        cnt = {e: 0 for e in self.COMPUTE}
        dcnt = {}
        for o in ops:
            if o[3] is not None:
                dcnt[o[3]] = dcnt.get(o[3], 0) + 16
                o[5] = (o[3], dcnt[o[3]])
            elif o[4]:
                cnt[o[0]] += 1
                o[5] = (o[0], cnt[o[0]])
        waited = {e: {} for e in self.engs}
        streams = {e: [] for e in self.engs}
        for o in ops:
            eng, fn, deps, dma, sig, tok = o
            need = {}
            for d in deps:
                do = ops[d]
                if do[3] is None and do[0] == 'pe' and eng == 'pe' and dma is None:
                    continue
                s, v = do[5]
                if v > need.get(s, 0):
                    need[s] = v
            for s, v in need.items():
                if waited[eng].get(s, 0) >= v:
                    continue
                sem = self.psem[s] if s in self.psem else self.dsems[s]
                streams[eng].append(('w', sem, v))
                waited[eng][s] = v
            if dma is not None:
                streams[eng].append(('i', fn, self.dsems[dma], 16))
            elif sig:
                streams[eng].append(('i', fn, self.psem[eng], 1))
            else:
                streams[eng].append(('i', fn, None, 0))
        for s, v in dcnt.items():
            streams[final_wait_eng].append(('w', self.dsems[s], v))

        def runner(eng):
            def f(E):
                for it in streams[eng]:
                    if it[0] == 'w':
                        E.wait_ge(it[1], it[2])
                    else:
                        ins = it[1](E)
                        if it[2] is not None:
                            ins.then_inc(it[2], it[3])
            return f
        with self.nc.Block() as block:
            block.sync(runner('sp'))
            block.scalar(runner('act'))
            block.vector(runner('dve'))
            block.gpsimd(runner('pool'))
            block.tensor(runner('pe'))
        return dict(n_ops=len(ops), counts=cnt)


def rope_tables(sample, qscale):
    t = np.arange(NT)
    cos = np.ones((64, NT), np.float64)
    sin = np.zeros((64, NT), np.float64)
    if sample:
        inv = 10000.0 ** (-np.arange(16, dtype=np.float32) / 16)
        for grp, pos in ((0, t // 64), (1, t % 64)):
            ang = pos.astype(np.float32)[None, :] * inv[:, None].astype(np.float32)
            c, s = np.cos(ang), np.sin(ang)
            cos[grp * 32:grp * 32 + 16] = c
            cos[grp * 32 + 16:grp * 32 + 32] = c
            sin[grp * 32:grp * 32 + 16] = -s
            sin[grp * 32 + 16:grp * 32 + 32] = s
    cos = np.concatenate([cos, cos], 0) * qscale
    sin = np.concatenate([sin, sin], 0) * qscale
    return cos.astype(np.float32), sin.astype(np.float32)


def rot_matrix():
    P = np.zeros((128, 128), np.float32)
    for m in range(128):
        g, i = divmod(m, 32)
        partner = g * 32 + (i + 16) % 32
        P[partner, m] = 1.0
    return P


def build(layers=(0, 1, 2, 3), dbg=None, max_ops=None):
    NLW = max(layers) + 1
    nc = bass.Bass("TRN2", target_bir_lowering=False)
    es = ExitStack()
    S = Sched(nc, es)

    def din(name, shape):
        return nc.dram_tensor(name, list(shape), F32, kind="ExternalInput").ap()

    def dout(name, shape):
        return nc.dram_tensor(name, list(shape), F32, kind="ExternalOutput").ap()

    x_in = din("x", [NT, D])
    cvec = din("cvec", [128, 16])
    w_mod = din("w_mod", [NLW * D, 6 * D])
    bmod = din("bmod", [128, 4 * 96])
    gfm = din("gfm", [128, 4 * 4 * 16])
    w_ff1 = din("w_ff1", [NLW * D, DFF])
    w_ff2 = din("w_ff2", [NLW * DFF, D])
    ident_d = din("ident", [128, 128])
    a_w_in = din("a_w_in", [D, 3 * D])
    a_w_out = din("a_w_out", [D, D])
    a_ck = din("a_ck", [512, D])
    a_cv = din("a_cv", [512, D])
    a_lam = din("a_lam", [128, 4096])
    a_sub = din("a_sub", [128, 128])
    a_cosq = din("a_cosq", [128, NT])
    a_sinq = din("a_sinq", [128, NT])
    a_cosk = din("a_cosk", [128, NT])
    a_sink = din("a_sink", [128, NT])
    a_mask = din("a_mask", [128, 48])
    rotm = din("rotm", [128, 128])

    c_w_in = din("c_w_in", [D, 3 * D])
    c_w_out = din("c_w_out", [D, D])
    c_ck = din("c_ck", [512, D])
    c_cv = din("c_cv", [512, D])
    c_bias = din("c_bias", [16 * 8 * 128, 5 * 128])
    d_w_in = din("d_w_in", [D, 3072])
    d_w_out = din("d_w_out", [D, D])
    d_ck = din("d_ck", [512, 512])
    d_cv = din("d_cv", [512, 512])
    d_mask = din("d_mask", [128, 2 * 8 * 128])
    d_snk = din("d_snk", [128, 32])
    ctxb_d = din("ctxb", [128, 1])
    b_w_in = din("b_w_in", [D, 6176])
    b_w_out = din("b_w_out", [D, D])
    b_gb = din("b_gb", [128, 32])
    b_nrm = din("b_nrm", [128, D])
    b_keep = din("b_keep", [128, 1])
    b_m0 = din("b_m0", [128, 16])
    b_S0 = din("b_S0", [16 * 128, 257])
    b_um = din("b_um", [128, 256])
    b_mk = din("b_mk", [128, 256])
    b_So = dout("b_So", [4 * 16 * 128, 257])
    b_mo = dout("b_mo", [8, 8])
    c_ko = dout("c_k", [NT, D])
    c_vo = dout("c_v", [NT, D])
    d_ko = dout("d_k", [NT, 512])
    d_vo = dout("d_v", [NT, 512])
    y_out = dout("y", [NT, D])
    a_ko = dout("a_k", [NT, D])
    a_vo = dout("a_v", [NT, D])
    dbg_out = dout("dbg", [NT, D]) if dbg else None

    sb_ctr = [0]

    def sb(name, shape, dt, stack=es):
        sb_ctr[0] += 1
        return stack.enter_context(nc.sbuf_tensor("%s_%d" % (name, sb_ctr[0]), list(shape), dt))

    banks = [es.enter_context(nc.psum_tensor("bank%d" % i, [128, 512], F32)) for i in range(8)]
    bank_rr = [0]

    def next_bank():
        b = bank_rr[0]
        bank_rr[0] = (b + 1) % 8
        return b

    def bk(b):
        return ('bank', b)

    xres = sb("xres", [128, 8, D], F32)
    ident_f = sb("ident_f", [128, 128], F32)
    ident_b = sb("ident_b", [128, 128], BF16)
    ones_f = sb("ones_f", [128, 128], F32)
    cv_f = sb("cv_f", [128, 16], F32)
    cv_b = sb("cv_b", [128, 16], BF16)
    bmod_s = sb("bmod_s", [128, 4 * 96], F32)
    gfm_s = sb("gfm_s", [128, 4 * 4 * 16], F32)
    modv = sb("modv", [128, 96], F32)
    Acoef = sb("Acoef", [128, 2, 16], F32)
    Gfm = sb("Gfm", [128, 2, 16], F32)
    Gbc = sb("Gbc", [128, 2, D], F32)
    diag = sb("diag", [128, 2, 128], F32)
    stat = sb("stat", [128, 64], F32)
    sq_junk = sb("sq_junk", [128, 512], BF16)
    xn_b = sb("xn_b", [128, 1, D], BF16)
    tmp_f = sb("tmp_f", [128, 2, 512], F32)

    n_dma = [0]

    def dma(eng, out, in_, reads=(), writes=(), pool=None):
        if pool is None:
            pool = 'ld' if eng == 'sp' else 'wq'
        k = n_dma[0]
        n_dma[0] += 1
        name = "%s%d" % (pool, k % 6)
        S.op(eng, lambda e: e.dma_start(out=out, in_=in_), reads=reads, writes=writes, dma=name)

    for tt in range(8):
        dma('sp', xres[:, tt, :], x_in[tt * 128:(tt + 1) * 128, :], writes=[('x', tt)])
    dma('sp', ident_f[:], ident_d[:, :], writes=['ident_f'])
    dma('sp', cv_f[:], cvec[:, :], writes=['cv_f'])
    dma('sp', bmod_s[:], bmod[:, :], writes=['bmod_s'])
    dma('sp', gfm_s[:], gfm[:, :], writes=['gfm_s'])
    S.op('dve', lambda e: e.tensor_copy(out=ident_b[:], in_=ident_f[:]), reads=['ident_f'], writes=['ident_b'])
    S.op('dve', lambda e: e.memset(ones_f[:], 1.0), writes=['ones_f'])
    S.op('act', lambda e: e.activation(out=cv_b[:], in_=cv_f[:], func=AF.Silu), reads=['cv_f'], writes=['cv_b'])

    wctr = [0]

    def modulation(l, wst):
        wblk = [sb("wm%d" % i, [128, 16, 512], BF16, wst) for i in range(3)]
        b = next_bank()
        for nb in range(24):
            w = wblk[nb % 3]
            wk = ('wm', nb % 3)
            src = w_mod[l * D:(l + 1) * D, nb * 512:(nb + 1) * 512].rearrange("(c p) n -> p c n", p=128)
            dma('pool', w[:], src, writes=[wk])
            for j in range(4):
                col = nb * 4 + j
                for kc in range(16):
                    S.op('pe', lambda e, w=w, j=j, kc=kc, col=col, b=b: e.matmul(
                        banks[b][:, col:col + 1], lhsT=w[:, kc, j * 128:(j + 1) * 128], rhs=cv_b[:, kc:kc + 1],
                        start=(kc == 0), stop=(kc == 15)), reads=[wk, 'cv_b'], writes=[bk(b)])
        S.op('dve', lambda e, b=b: e.tensor_tensor(out=modv[:], in0=banks[b][:, 0:96], in1=bmod_s[:, l * 96:(l + 1) * 96],
                                                    op=ALU.add), reads=[bk(b), 'bmod_s'], writes=['modv'])
        g = lambda i: gfm_s[:, (l * 4 + i) * 16:(l * 4 + i + 1) * 16]
        for half in range(2):
            sc = modv[:, (3 * half + 1) * 16:(3 * half + 2) * 16]
            gt = modv[:, (3 * half + 2) * 16:(3 * half + 3) * 16]
            S.op('dve', lambda e, half=half, sc=sc: e.scalar_tensor_tensor(
                out=Acoef[:, half, :], in0=sc, scalar=1.0, in1=g(2 * half), op0=ALU.add, op1=ALU.mult),
                reads=['modv', 'gfm_s'], writes=[('Acoef', half)])
            S.op('dve', lambda e, half=half, gt=gt: e.tensor_tensor(
                out=Gfm[:, half, :], in0=gt, in1=g(2 * half + 1), op=ALU.mult),
                reads=['modv', 'gfm_s'], writes=[('Gfm', half)])
            for c4 in range(4):
                b2 = next_bank()
                for cc in range(4):
                    c = c4 * 4 + cc
                    S.op('dve', lambda e, c=c, half=half: e.tensor_scalar(
                        out=diag[:, c % 2, :], in0=ident_f[:], scalar1=Gfm[:, half, c:c + 1], scalar2=None, op0=ALU.mult),
                        reads=['ident_f', ('Gfm', half)], writes=[('diag', c % 2)])
                    S.op('pe', lambda e, c=c, cc=cc, b2=b2: e.matmul(
                        banks[b2][:, cc * 128:(cc + 1) * 128], lhsT=ones_f[:], rhs=diag[:, c % 2, :], start=True, stop=True),
                        reads=['ones_f', ('diag', c % 2)], writes=[bk(b2)])
                S.op('act', lambda e, c4=c4, half=half, b2=b2: e.copy(
                    out=Gbc[:, half, c4 * 512:(c4 + 1) * 512], in_=banks[b2][:]),
                    reads=[bk(b2)], writes=[('Gbc', half)])

    def adaln_in(half, tiles, hT, hkey, col0=0):
        shift = modv[:, (3 * half) * 16:(3 * half + 1) * 16]
        for i, tt in enumerate(tiles):
            s = tt % 2
            S.op('act', lambda e, tt=tt, s=s: e.activation(out=xn_b[:, 0, :], in_=xres[:, tt, :], func=AF.Square,
                                                            accum_out=stat[:, s:s + 1]),
                 reads=[('x', tt)], writes=[('xn_b', 0), ('stat', s)])
            S.op('dve', lambda e, s=s: e.tensor_scalar(out=stat[:, 2 + s:3 + s], in0=stat[:, s:s + 1], scalar1=1.0 / D,
                                                        scalar2=EPS, op0=ALU.mult, op1=ALU.add),
                 reads=[('stat', s)], writes=[('stat', 2 + s)])
            S.op('act', lambda e, s=s: e.activation(out=stat[:, 6 + s:7 + s], in_=stat[:, 2 + s:3 + s], func=AF.Ln),
                 reads=[('stat', 2 + s)], writes=[('stat', 6 + s)])
            S.op('act', lambda e, s=s: e.activation(out=stat[:, 4 + s:5 + s], in_=stat[:, 6 + s:7 + s], func=AF.Exp, scale=-0.5),
                 reads=[('stat', 6 + s)], writes=[('stat', 4 + s)])
            S.op('dve', lambda e, tt=tt, s=s: e.tensor_scalar(out=xn_b[:, 0, :], in0=xres[:, tt, :],
                                                               scalar1=stat[:, 4 + s:5 + s], scalar2=None, op0=ALU.mult),
                 reads=[('x', tt), ('stat', 4 + s)], writes=[('xn_b', 0)])
            for c8 in range(2):
                b = next_bank()
                pb = banks[b][:].bitcast(BF16)
                for cc in range(8):
                    c = c8 * 8 + cc
                    S.op('pe', lambda e, c=c, cc=cc, s=s, pb=pb: e.transpose(
                        out=pb[:, cc * 128:(cc + 1) * 128], in_=xn_b[:, 0, c * 128:(c + 1) * 128], identity=ident_b[:]),
                        reads=[('xn_b', 0), 'ident_b'], writes=[bk(b)])
                for cc in range(8):
                    c = c8 * 8 + cc
                    dst = hT[:, c, col0 + i * 128:col0 + (i + 1) * 128]
                    if True:
                        S.op('act', lambda e, c=c, cc=cc, pb=pb, dst=dst, half=half: e.activation(
                            out=dst, in_=pb[:, cc * 128:(cc + 1) * 128], func=AF.Identity,
                            bias=shift[:, c:c + 1], scale=Acoef[:, half, c:c + 1]),
                            reads=[bk(b), ('Acoef', half), 'modv'], writes=[(hkey, c)])
                    else:
                        S.op('dve', lambda e, c=c, cc=cc, pb=pb, dst=dst, half=half: e.tensor_scalar(
                            out=dst, in0=pb[:, cc * 128:(cc + 1) * 128], scalar1=Acoef[:, half, c:c + 1],
                            scalar2=shift[:, c:c + 1], op0=ALU.mult, op1=ALU.add),
                            reads=[bk(b), ('Acoef', half), 'modv'], writes=[(hkey, c)])

    def post_norm(half, tt, yb):
        for q in range(4):
            S.op('act', lambda e, q=q: e.activation(out=sq_junk[:, :], in_=banks[yb[q]][:],
                                                     func=AF.Square, accum_out=stat[:, 8 + q:9 + q]),
                 reads=[bk(yb[q])], writes=['sq_junk', ('stat', 8 + q)])
        S.op('dve', lambda e: e.tensor_reduce(out=stat[:, 12:13], in_=stat[:, 8:12], axis=mybir.AxisListType.X, op=ALU.add),
             reads=[('stat', 8 + q) for q in range(4)], writes=[('stat', 12)])
        S.op('dve', lambda e: e.tensor_scalar(out=stat[:, 13:14], in0=stat[:, 12:13], scalar1=1.0 / D, scalar2=EPS,
                                               op0=ALU.mult, op1=ALU.add), reads=[('stat', 12)], writes=[('stat', 13)])
        S.op('act', lambda e: e.activation(out=stat[:, 15:16], in_=stat[:, 13:14], func=AF.Ln),
             reads=[('stat', 13)], writes=[('stat', 15)])
        S.op('act', lambda e: e.activation(out=stat[:, 14:15], in_=stat[:, 15:16], func=AF.Exp, scale=-0.5),
             reads=[('stat', 15)], writes=[('stat', 14)])
        for q in range(4):
            s = q % 2
            S.op('dve', lambda e, q=q, s=s: e.scalar_tensor_tensor(
                out=tmp_f[:, s, :], in0=banks[yb[q]][:], scalar=stat[:, 14:15], in1=Gbc[:, half, q * 512:(q + 1) * 512],
                op0=ALU.mult, op1=ALU.mult), reads=[bk(yb[q]), ('stat', 14), ('Gbc', half)], writes=[('tmp_f', s)])
            S.op('dve', lambda e, q=q, s=s, tt=tt: e.tensor_tensor(
                out=xres[:, tt, q * 512:(q + 1) * 512], in0=xres[:, tt, q * 512:(q + 1) * 512], in1=tmp_f[:, s, :],
                op=ALU.add), reads=[('x', tt), ('tmp_f', s)], writes=[('x', tt)])

    def mlp(l):
        with ExitStack() as st:
            hTq = sb("hTq", [128, 16, 256], BF16, st)
            uTq = sb("uTq", [128, 64, 256], BF16, st)
            w1 = [sb("w1_%d" % i, [128, 16, 512], BF16, st) for i in range(3)]
            w2 = [sb("w2_%d" % i, [128, D], BF16, st) for i in range(5)]
            rl = sb("rl", [128, 2, 256], F32, st)
            for q in range(4):
                adaln_in(1, [2 * q, 2 * q + 1], hTq, 'hTq')
                for nb in range(16):
                    w = w1[nb % 3]
                    wk = ('w1', nb % 3)
                    src = w_ff1[l * D:(l + 1) * D, nb * 512:(nb + 1) * 512].rearrange("(c p) n -> p c n", p=128)
                    dma('pool', w[:], src, writes=[wk])
                    for j in range(4):
                        b = next_bank()
                        fc = nb * 4 + j
                        for kc in range(16):
                            S.op('pe', lambda e, w=w, j=j, kc=kc, b=b: e.matmul(
                                banks[b][:, 0:256], lhsT=w[:, kc, j * 128:(j + 1) * 128], rhs=hTq[:, kc, :],
                                start=(kc == 0), stop=(kc == 15)), reads=[wk, ('hTq', kc)], writes=[bk(b)])
                        s = fc % 2
                        S.op('act', lambda e, b=b, s=s: e.activation(out=rl[:, s, :], in_=banks[b][:, 0:256], func=AF.Relu),
                             reads=[bk(b)], writes=[('rl', s)])
                        S.op('dve', lambda e, s=s, fc=fc: e.tensor_tensor(out=uTq[:, fc, :], in0=rl[:, s, :], in1=rl[:, s, :],
                                                                           op=ALU.mult),
                             reads=[('rl', s)], writes=[('uTq', fc)])
                yb = [[next_bank() for _ in range(4)] for _ in range(2)]
                for kc in range(64):
                    w = w2[kc % 5]
                    wk = ("w2", kc % 5)
                    dma('pool', w[:], w_ff2[l * DFF + kc * 128:l * DFF + (kc + 1) * 128, :], writes=[wk])
                    for t in range(2):
                        for nq in range(4):
                            S.op('pe', lambda e, w=w, kc=kc, t=t, nq=nq: e.matmul(
                                banks[yb[t][nq]][:], lhsT=uTq[:, kc, t * 128:(t + 1) * 128], rhs=w[:, nq * 512:(nq + 1) * 512],
                                start=(kc == 0), stop=(kc == 63)), reads=[wk, ('uTq', kc)], writes=[bk(yb[t][nq])])
                for t in range(2):
                    post_norm(1, 2 * q + t, yb[t])
        S.barrier()

    def mixer_a(l):
        lam_init = 0.8 - 0.6 * math.exp(-0.3 * l)
        with ExitStack() as st:
            oT = sb("oT", [128, 16, NT], BF16, st)
            with ExitStack() as st2:
                lam_s = sb("lam_s", [128, 4, 16], F32, st2)
                gsub = sb("gsub", [128, 128], F32, st2)
                st_lam = ExitStack()
                lam_t = sb("lam_t", [128, 4096], F32, st_lam)
                lam_p = sb("lam_p", [128, 2, 1024], F32, st_lam)
                dma('sp', lam_t[:], a_lam[:, :], writes=['lam_t'])
                dma('sp', gsub[:], a_sub[:, :], writes=['gsub0'])
                for i in range(2):
                    S.op('dve', lambda e, i=i: e.tensor_tensor(out=lam_p[:, i, :], in0=lam_t[:, (2 * i) * 1024:(2 * i + 1) * 1024],
                                                                in1=lam_t[:, (2 * i + 1) * 1024:(2 * i + 2) * 1024], op=ALU.mult),
                         reads=['lam_t'], writes=[('lam_p', i)])
                    S.op('dve', lambda e, i=i: e.tensor_reduce(out=lam_s[:, i, :], in_=lam_p[:, i, :].rearrange("p (h d) -> p h d", d=64),
                                                                axis=mybir.AxisListType.X, op=ALU.add),
                         reads=[('lam_p', i)], writes=[('lam_s', i)])
                    S.op('act', lambda e, i=i: e.activation(out=lam_s[:, i, :], in_=lam_s[:, i, :], func=AF.Exp),
                         reads=[('lam_s', i)], writes=[('lam_s', i)])
                S.op('dve', lambda e: e.tensor_tensor(out=lam_s[:, 2, :], in0=lam_s[:, 0, :], in1=lam_s[:, 1, :], op=ALU.subtract),
                     reads=[('lam_s', 0), ('lam_s', 1)], writes=[('lam_s', 2)])
                S.op('dve', lambda e: e.tensor_scalar(out=lam_s[:, 3, :], in0=lam_s[:, 2, :], scalar1=lam_init, scalar2=-1.0,
                                                       op0=ALU.add, op1=ALU.mult), reads=[('lam_s', 2)], writes=[('lam_s', 3)])
                S.op('dve', lambda e: e.tensor_scalar(out=gsub[:], in0=gsub[:], scalar1=1.0 - lam_init, scalar2=None, op0=ALU.mult),
                     reads=['gsub0'], writes=['gsub'])
                S.barrier()
                st_lam.close()
                hT = sb("hT", [128, 16, NT], BF16, st2)
                wh = [sb("wh%d" % i, [128, 16, 3, 128], BF16, st2) for i in range(1)]
                cosq = sb("cosq", [128, NT], F32, st2)
                sinq = sb("sinq", [128, NT], F32, st2)
                cosk = sb("cosk", [128, NT], F32, st2)
                sink_ = sb("sink", [128, NT], F32, st2)
                rot_f = sb("rot_f", [128, 128], F32, st2)
                rot_b = sb("rot_b", [128, 128], BF16, st2)
                maskA = sb("maskA", [128, 48], F32, st2)
                QT = sb("QT", [128, NT], BF16, st2)
                KT = sb("KT", [128, NT + 512], BF16, st2)
                raw = sb("raw", [128, 1, 512], BF16, st2)
                t1 = sb("t1", [128, 1, 512], F32, st2)
                t2 = sb("t2", [128, 1, 512], F32, st2)
                Vaug = sb("Vaug", [128, 12, 132], BF16, st2)
                kvo = sb("kvo", [128, 2, 2, 128], F32, st2)
                cks = sb("cks", [128, 4, 128], BF16, st2)
                Et = sb("Et", [128, 2, 256], BF16, st2)
                osb = sb("osb", [128, 2, 128], F32, st2)
                onb = sb("onb", [128, 2, 128], BF16, st2)
                sst = sb("sst", [128, 16], F32, st2)

                tabk = ['cosq', 'sinq', 'cosk', 'sink']
                for t_, d_, k_ in ((cosq, a_cosq, 'cosq'), (sinq, a_sinq, 'sinq'), (cosk, a_cosk, 'cosk'), (sink_, a_sink, 'sink')):
                    dma('sp', t_[:], d_[:, :], writes=[k_])
                dma('sp', rot_f[:], rotm[:, :], writes=['rot_f'])
                dma('sp', maskA[:], a_mask[:, :], writes=['maskA'])
                S.op('dve', lambda e: e.tensor_copy(out=rot_b[:], in_=rot_f[:]), reads=['rot_f'], writes=['rot_b'])
                S.op('dve', lambda e: e.memset(Vaug[:, :, 128:132], 1.0), writes=['Vaug_ones'])

                adaln_in(0, list(range(8)), hT, 'hT')

                def proj_fm_rope(w, wk, slot, dst, ct, st_, ck_, sk_):
                    for hf in range(2):
                        b = next_bank()
                        for kc in range(16):
                            S.op('pe', lambda e, kc=kc, b=b, hf=hf: e.matmul(
                                banks[b][:], lhsT=w[:, kc, slot, :], rhs=hT[:, kc, hf * 512:(hf + 1) * 512],
                                start=(kc == 0), stop=(kc == 15)), reads=[wk, ('hT', kc)], writes=[bk(b)])
                        S.op('act', lambda e, b=b, hf=hf: e.copy(out=raw[:, 0, :], in_=banks[b][:]),
                             reads=[bk(b)], writes=[('raw', 0)])
                        b2 = next_bank()
                        S.op('pe', lambda e, b2=b2, hf=hf: e.matmul(banks[b2][:], lhsT=rot_b[:], rhs=raw[:, 0, :],
                                                                    start=True, stop=True),
                             reads=['rot_b', ('raw', 0)], writes=[bk(b2)])
                        S.op('dve', lambda e, b=b, hf=hf: e.tensor_tensor(out=t1[:, 0, :], in0=banks[b][:],
                                                                           in1=ct[:, hf * 512:(hf + 1) * 512], op=ALU.mult),
                             reads=[bk(b), ck_], writes=[('t1', 0)])
                        S.op('dve', lambda e, b2=b2, hf=hf: e.tensor_tensor(out=t2[:, 0, :], in0=banks[b2][:],
                                                                             in1=st_[:, hf * 512:(hf + 1) * 512], op=ALU.mult),
                             reads=[bk(b2), sk_], writes=[('t2', 0)])
                        S.op('dve', lambda e, hf=hf: e.tensor_tensor(out=dst[:, hf * 512:(hf + 1) * 512], in0=t1[:, 0, :],
                                                                      in1=t2[:, 0, :], op=ALU.add),
                             reads=[('t1', 0), ('t2', 0)], writes=[('dstrope', id(dst))])

                for h in range(16):
                    w = wh[0]
                    wk = ('wh', 0)
                    for sl in range(3):
                        src = a_w_in[:, sl * D + h * 128:sl * D + (h + 1) * 128].rearrange("(c p) n -> p c n", p=128)
                        dma('pool', w[:, :, sl, :], src, writes=[wk])
                    proj_fm_rope(w, wk, 0, QT, cosq, sinq, tabk[0], tabk[1])
                    proj_fm_rope(w, wk, 1, KT, cosk, sink_, tabk[2], tabk[3])
                    for tt in range(8):
                        b = next_bank()
                        for kc in range(16):
                            S.op('pe', lambda e, kc=kc, b=b, tt=tt: e.matmul(
                                banks[b][:, 0:256], lhsT=hT[:, kc, tt * 128:(tt + 1) * 128],
                                rhs=w[:, kc, 1:3, :].rearrange("p a b -> p (a b)"),
                                start=(kc == 0), stop=(kc == 15)), reads=[wk, ('hT', kc)], writes=[bk(b)])
                        S.op('act', lambda e, b=b, tt=tt: e.copy(out=kvo[:, :, tt % 2, :],
                                                                in_=banks[b][:, 0:256].rearrange("p (a b) -> p a b", a=2)),
                             reads=[bk(b)], writes=[('kvo', tt % 2)])
                        S.op('dve', lambda e, b=b, tt=tt: e.tensor_copy(out=Vaug[:, tt, 0:128], in_=banks[b][:, 128:256]),
                             reads=[bk(b)], writes=[('Vaug', tt)])
                        dma('sp', a_ko[tt * 128:(tt + 1) * 128, h * 128:(h + 1) * 128], kvo[:, 0, tt % 2, :],
                            reads=[('kvo', tt % 2)], pool='st')
                        dma('sp', a_vo[tt * 128:(tt + 1) * 128, h * 128:(h + 1) * 128], kvo[:, 1, tt % 2, :],
                            reads=[('kvo', tt % 2)], pool='st')
                    dma('pool', cks[:], a_ck[:, h * 128:(h + 1) * 128].rearrange("(t p) d -> p t d", p=128), writes=['cks'])
                    dma('pool', Vaug[:, 8:12, 0:128], a_cv[:, h * 128:(h + 1) * 128].rearrange("(t p) d -> p t d", p=128),
                        writes=[('Vaug', 8 + i) for i in range(4)])
                    b = next_bank()
                    pb = banks[b][:].bitcast(BF16)
                    for i in range(4):
                        S.op('pe', lambda e, i=i, pb=pb: e.transpose(out=pb[:, i * 128:(i + 1) * 128], in_=cks[:, i, :],
                                                                      identity=ident_b[:]),
                             reads=['cks', 'ident_b'], writes=[bk(b)])
                    S.op('act', lambda e, pb=pb: e.copy(out=KT[:, NT:NT + 512], in_=pb[:, 0:512]),
                         reads=[bk(b)], writes=['KTctx'])
                    for qp in range(4):
                        ob = [[next_bank() for _ in range(2)] for _ in range(2)]
                        for j in range(2):
                            pr = slice(j * 64, (j + 1) * 64)
                            for kc in range(12):
                                b = next_bank()
                                while any(b in r for r in ob):
                                    b = next_bank()
                                es_ = kc % 2
                                S.op('pe', lambda e, b=b, kc=kc, pr=pr, qp=qp: e.matmul(
                                    banks[b][:, 0:256], lhsT=KT[pr, kc * 128:(kc + 1) * 128], rhs=QT[pr, qp * 256:(qp + 1) * 256],
                                    start=True, stop=True),
                                    reads=[('dstrope', id(KT)), ('dstrope', id(QT)), 'KTctx'], writes=[bk(b)])
                                S.op('act', lambda e, b=b, kc=kc, qp=qp, es_=es_: e.activation(
                                    out=Et[:, es_, :], in_=banks[b][:, 0:256], func=AF.Exp,
                                    bias=maskA[:, kc * 4 + qp:kc * 4 + qp + 1], scale=1.0),
                                    reads=[bk(b), 'maskA'], writes=[('Et', es_)])
                                for qq in range(2):
                                    S.op('pe', lambda e, kc=kc, qq=qq, j=j, es_=es_, ob=ob: e.matmul(
                                        banks[ob[j][qq]][:, 0:129], lhsT=Et[:, es_, qq * 128:(qq + 1) * 128], rhs=Vaug[:, kc, 0:129],
                                        start=(kc == 0), stop=(kc == 11)),
                                        reads=[('Et', es_), ('Vaug', kc), 'Vaug_ones'], writes=[bk(ob[j][qq])])
                        for qq in range(2):
                            tt = qp * 2 + qq
                            o1, o2 = banks[ob[0][qq]], banks[ob[1][qq]]
                            S.op('dve', lambda e, o1=o1: e.reciprocal(out=sst[:, 0:1], in_=o1[:, 128:129]),
                                 reads=[bk(ob[0][qq])], writes=[('sst', 0)])
                            S.op('dve', lambda e, o2=o2: e.reciprocal(out=sst[:, 1:2], in_=o2[:, 128:129]),
                                 reads=[bk(ob[1][qq])], writes=[('sst', 1)])
                            S.op('dve', lambda e, h=h: e.tensor_tensor(out=sst[:, 2:3], in0=sst[:, 1:2], in1=lam_s[:, 3, h:h + 1],
                                                                        op=ALU.mult),
                                 reads=[('sst', 1), ('lam_s', 3)], writes=[('sst', 2)])
                            S.op('dve', lambda e, o1=o1, qq=qq: e.tensor_scalar(out=osb[:, qq, :], in0=o1[:, 0:128], scalar1=sst[:, 0:1],
                                                                                 scalar2=None, op0=ALU.mult),
                                 reads=[bk(ob[0][qq]), ('sst', 0)], writes=[('osb', qq)])
                            S.op('dve', lambda e, o2=o2, qq=qq: e.scalar_tensor_tensor(
                                out=osb[:, qq, :], in0=o2[:, 0:128], scalar=sst[:, 2:3], in1=osb[:, qq, :], op0=ALU.mult, op1=ALU.add),
                                reads=[bk(ob[1][qq]), ('sst', 2), ('osb', qq)], writes=[('osb', qq)])
                            S.op('act', lambda e, qq=qq: e.activation(out=sq_junk[:, 0:128], in_=osb[:, qq, :], func=AF.Square,
                                                                       accum_out=sst[:, 3:4]),
                                 reads=[('osb', qq)], writes=['sq_junk', ('sst', 3)])
                            S.op('dve', lambda e: e.tensor_scalar(out=sst[:, 4:5], in0=sst[:, 3:4], scalar1=1.0 / 128, scalar2=EPS,
                                                                   op0=ALU.mult, op1=ALU.add), reads=[('sst', 3)], writes=[('sst', 4)])
                            S.op('act', lambda e: e.activation(out=sst[:, 6:7], in_=sst[:, 4:5], func=AF.Ln),
                                 reads=[('sst', 4)], writes=[('sst', 6)])
                            S.op('act', lambda e: e.activation(out=sst[:, 5:6], in_=sst[:, 6:7], func=AF.Exp, scale=-0.5),
                                 reads=[('sst', 6)], writes=[('sst', 5)])
                            S.op('dve', lambda e, qq=qq: e.scalar_tensor_tensor(
                                out=onb[:, qq, :], in0=osb[:, qq, :], scalar=sst[:, 5:6], in1=gsub[:], op0=ALU.mult, op1=ALU.mult),
                                reads=[('osb', qq), ('sst', 5), 'gsub'], writes=[('onb', qq)])
                            b = next_bank()
                            pb = banks[b][:].bitcast(BF16)
                            S.op('pe', lambda e, qq=qq, pb=pb: e.transpose(out=pb[:, 0:128], in_=onb[:, qq, :], identity=ident_b[:]),
                                 reads=[('onb', qq), 'ident_b'], writes=[bk(b)])
                            S.op('act', lambda e, pb=pb, h=h, tt=tt: e.copy(out=oT[:, h, tt * 128:(tt + 1) * 128], in_=pb[:, 0:128]),
                                 reads=[bk(b)], writes=[('oT', h)])
            S.barrier()
            out_proj(a_w_out, oT, st)
        S.barrier()

    NA_L = [[0, 1, 2, 3], [0, 1, 2, 3], [0, 1, 2, 3, 4], [1, 2, 3, 4, 5], [2, 3, 4, 5, 6], [3, 4, 5, 6, 7], [4, 5, 6, 7], [4, 5, 6, 7]]

    def mixer_cd(l, kind):
        isd = (kind == 'd')
        w_in_d, w_out_d = (d_w_in, d_w_out) if isd else (c_w_in, c_w_out)
        ck_d, cv_d = (d_ck, d_cv) if isd else (c_ck, c_cv)
        k_out, v_out = (d_ko, d_vo) if isd else (c_ko, c_vo)
        dv = 64 if isd else 128
        nslot = 4 if isd else 3
        with ExitStack() as st:
            oT = sb("oT", [128, 16, NT], BF16, st)
            with ExitStack() as st2:
                hT = sb("hT", [128, 16, NT], BF16, st2)
                w = sb("wcd", [128, 16, nslot, 128], BF16, st2)
                wk = 'wcd'
                QT = sb("QT", [128, 2 if isd else 1, NT], BF16, st2)
                KT = sb("KT", [128, NT + 512], BF16, st2)
                Vaug = sb("Vaug", [128, 12, dv + 4], BF16, st2)
                kvo = sb("kvo", [128, 2, 2, dv], F32, st2)
                cks = sb("cks", [128, 4, 128], BF16, st2)
                Et = sb("Et", [128, 2, 128], BF16, st2)
                sbs = sb("sbs", [128, 2, 128], F32, st2)
                onb = sb("onb", [128, 2, 256 if isd else 128], BF16, st2)
                sst = sb("sst", [128, 16], F32, st2)
                ctxb = sb("ctxb_s", [128, 1], F32, st2)
                dma('sp', ctxb[:], ctxb_d[:, :], writes=['ctxb'])
                if isd:
                    cosk = sb("cosk", [128, NT], F32, st2)
                    sink_ = sb("sink", [128, NT], F32, st2)
                    rot_f = sb("rot_f", [128, 128], F32, st2)
                    rot_b = sb("rot_b", [128, 128], BF16, st2)
                    raw = sb("raw", [128, 512], BF16, st2)
                    maskD = sb("maskD", [128, 2, 8, 128], F32, st2)
                    snk = sb("snk", [128, 32], F32, st2)
                    dma('sp', cosk[:], a_cosk[:, :], writes=['cosk'])
                    dma('sp', sink_[:], a_sink[:, :], writes=['sink'])
                    dma('sp', rot_f[:], rotm[:, :], writes=['rot_f'])
                    dma('sp', maskD[:], d_mask[:, :].rearrange("p (a b c) -> p a b c", a=2, b=8), writes=['maskD'])
                    dma('sp', snk[:], d_snk[:, :], writes=['snk0'])
                    S.op('dve', lambda e: e.tensor_copy(out=rot_b[:], in_=rot_f[:]), reads=['rot_f'], writes=['rot_b'])
                    S.op('act', lambda e: e.activation(out=snk[:], in_=snk[:], func=AF.Exp), reads=['snk0'], writes=['snk'])
                else:
                    biasC = sb("biasC", [128, 2, 5, 128], F32, st2)
                S.op('dve', lambda e: e.memset(Vaug[:, :, dv:dv + 4], 1.0), writes=['Vaug_ones'])
                adaln_in(0, list(range(8)), hT, 'hT')

                def proj_fm(slot, dst, dkey, rope, scale):
                    for hf in range(2):
                        b = next_bank()
                        for kc in range(16):
                            S.op('pe', lambda e, kc=kc, b=b, hf=hf: e.matmul(
                                banks[b][:], lhsT=w[:, kc, slot, :], rhs=hT[:, kc, hf * 512:(hf + 1) * 512],
                                start=(kc == 0), stop=(kc == 15)), reads=[wk, ('hT', kc)], writes=[bk(b)])
                        dsl = dst[:, hf * 512:(hf + 1) * 512]
                        if not rope:
                            S.op('act', lambda e, b=b: e.mul(out=dsl, in_=banks[b][:], mul=scale), reads=[bk(b)], writes=[dkey])
                            continue
                        S.op('act', lambda e, b=b: e.copy(out=raw[:], in_=banks[b][:]), reads=[bk(b)], writes=['raw'])
                        b2 = next_bank()
                        S.op('pe', lambda e, b2=b2: e.matmul(banks[b2][:], lhsT=rot_b[:], rhs=raw[:], start=True, stop=True),
                             reads=['rot_b', 'raw'], writes=[bk(b2)])
                        S.op('dve', lambda e, b=b, hf=hf: e.tensor_tensor(out=tmp_f[:, 0, :], in0=banks[b][:],
                                                                           in1=cosk[:, hf * 512:(hf + 1) * 512], op=ALU.mult),
                             reads=[bk(b), 'cosk'], writes=[('tmp_f', 0)])
                        S.op('dve', lambda e, b2=b2, hf=hf: e.tensor_tensor(out=tmp_f[:, 1, :], in0=banks[b2][:],
                                                                             in1=sink_[:, hf * 512:(hf + 1) * 512], op=ALU.mult),
                             reads=[bk(b2), 'sink'], writes=[('tmp_f', 1)])
                        S.op('dve', lambda e: e.tensor_tensor(out=dsl, in0=tmp_f[:, 0, :], in1=tmp_f[:, 1, :], op=ALU.add),
                             reads=[('tmp_f', 0), ('tmp_f', 1)], writes=[dkey])

                ngrp = 8 if isd else 16
                for g in range(ngrp):
                    rs = lambda a: a.rearrange("(c p) n -> p c n", p=128)
                    if isd:
                        for ci in range(2):
                            dma('pool', w[:, :, ci, :], rs(w_in_d[:, g * 256 + ci * 128:g * 256 + (ci + 1) * 128]), writes=[wk])
                        kcols = rs(w_in_d[:, 2048 + g * 64:2048 + (g + 1) * 64])
                        vcols = rs(w_in_d[:, 2560 + g * 64:2560 + (g + 1) * 64])
                        dma('pool', w[:, :, 2, 0:64], kcols, writes=[wk])
                        dma('pool', w[:, :, 2, 64:128], vcols, writes=[wk])
                        dma('pool', w[:, :, 3, 0:64], kcols, writes=[wk])
                        dma('pool', w[:, :, 3, 64:128], kcols, writes=[wk])
                        proj_fm(0, QT[:, 0, :], 'QT', True, 1.0)
                        proj_fm(1, QT[:, 1, :], 'QT', True, 1.0)
                        proj_fm(3, KT[:, 0:NT], 'KT', True, 1.0)
                        kvslot = w[:, :, 2, :]
                        nkv = 128
                    else:
                        for sl in range(3):
                            dma('pool', w[:, :, sl, :], rs(w_in_d[:, sl * D + g * 128:sl * D + (g + 1) * 128]), writes=[wk])
                        proj_fm(0, QT[:, 0, :], 'QT', False, 128 ** -0.5)
                        proj_fm(1, KT[:, 0:NT], 'KT', False, 1.0)
                        kvslot = w[:, :, 1:3, :].rearrange("p c a b -> p c (a b)")
                        nkv = 256
                    for tt in range(8):
                        b = next_bank()
                        for kc in range(16):
                            S.op('pe', lambda e, kc=kc, b=b, tt=tt: e.matmul(
                                banks[b][:, 0:nkv], lhsT=hT[:, kc, tt * 128:(tt + 1) * 128], rhs=kvslot[:, kc, :],
                                start=(kc == 0), stop=(kc == 15)), reads=[wk, ('hT', kc)], writes=[bk(b)])
                        S.op('act', lambda e, b=b, tt=tt: e.copy(out=kvo[:, :, tt % 2, :],
                                                                in_=banks[b][:, 0:nkv].rearrange("p (a b) -> p a b", a=2)),
                             reads=[bk(b)], writes=[('kvo', tt % 2)])
                        S.op('dve', lambda e, b=b, tt=tt: e.tensor_copy(out=Vaug[:, tt, 0:dv], in_=banks[b][:, dv:2 * dv]),
                             reads=[bk(b)], writes=[('Vaug', tt)])
                        dma('sp', k_out[tt * 128:(tt + 1) * 128, g * dv:(g + 1) * dv], kvo[:, 0, tt % 2, :],
                            reads=[('kvo', tt % 2)], pool='st')
                        dma('sp', v_out[tt * 128:(tt + 1) * 128, g * dv:(g + 1) * dv], kvo[:, 1, tt % 2, :],
                            reads=[('kvo', tt % 2)], pool='st')
                    csrc = ck_d[:, g * dv:(g + 1) * dv].rearrange("(t p) d -> p t d", p=128)
                    if isd:
                        dma('pool', cks[:, :, 0:64], csrc, writes=['cks'])
                        dma('pool', cks[:, :, 64:128], csrc, writes=['cks'])
                    else:
                        dma('pool', cks[:], csrc, writes=['cks'])
                    dma('pool', Vaug[:, 8:12, 0:dv], cv_d[:, g * dv:(g + 1) * dv].rearrange("(t p) d -> p t d", p=128),
                        writes=[('Vaug', 8 + i) for i in range(4)])
                    b = next_bank()
                    pb = banks[b][:].bitcast(BF16)
                    for i in range(4):
                        S.op('pe', lambda e, i=i, pb=pb: e.transpose(out=pb[:, i * 128:(i + 1) * 128], in_=cks[:, i, :],
                                                                      identity=ident_b[:]),
                             reads=['cks', 'ident_b'], writes=[bk(b)])
                    S.op('act', lambda e, pb=pb: e.copy(out=KT[:, NT:NT + 512], in_=pb[:, 0:512]), reads=[bk(b)], writes=['KTctx'])
                    for qt in range(8):
                        if isd:
                            chunks = [(qt + r, ri) for r, ri in ((-1, 0), (0, None), (1, 1)) if 0 <= qt + r < 8]
                        else:
                            bslot = (g * 8 + qt) % 2
                            dma('sp', biasC[:, bslot, :, :], c_bias[(g * 8 + qt) * 128:(g * 8 + qt + 1) * 128, :].rearrange(
                                "p (a b) -> p a b", a=5), writes=[('biasC', bslot)])
                            chunks = [(kc, si) for si, kc in enumerate(NA_L[qt])]
                        chunks += [(8 + i, 'ctx') for i in range(4)]
                        for gi in range(4 if isd else 1):
                            pr = slice((gi % 2) * 64, (gi % 2 + 1) * 64) if isd else slice(0, 128)
                            qsrc = QT[pr, gi // 2, qt * 128:(qt + 1) * 128]
                            ob = next_bank()
                            for n, (kc, mk) in enumerate(chunks):
                                b = next_bank()
                                if b == ob:
                                    b = next_bank()
                                es_ = n % 2
                                S.op('pe', lambda e, b=b, kc=kc: e.matmul(
                                    banks[b][:, 0:128], lhsT=KT[pr, kc * 128:(kc + 1) * 128], rhs=qsrc, start=True, stop=True),
                                    reads=['KT', 'QT', 'KTctx'], writes=[bk(b)])
                                esc = 0.125 if isd else 1.0
                                if mk == 'ctx':
                                    S.op('act', lambda e, b=b, es_=es_: e.activation(out=Et[:, es_, :], in_=banks[b][:, 0:128], func=AF.Exp,
                                                                                      bias=ctxb[:, 0:1], scale=esc),
                                         reads=[bk(b), 'ctxb'], writes=[('Et', es_)])
                                elif mk is None:
                                    S.op('act', lambda e, b=b, es_=es_: e.activation(out=Et[:, es_, :], in_=banks[b][:, 0:128], func=AF.Exp,
                                                                                      scale=esc),
                                         reads=[bk(b)], writes=[('Et', es_)])
                                else:
                                    msrc = maskD[:, mk, qt, :] if isd else biasC[:, bslot, mk, :]
                                    mkey = 'maskD' if isd else ('biasC', bslot)
                                    S.op('dve', lambda e, b=b, es_=es_: e.tensor_tensor(out=sbs[:, es_, :], in0=banks[b][:, 0:128], in1=msrc,
                                                                                         op=ALU.add),
                                         reads=[bk(b), mkey], writes=[('sbs', es_)])
                                    S.op('act', lambda e, es_=es_: e.activation(out=Et[:, es_, :], in_=sbs[:, es_, :], func=AF.Exp, scale=esc),
                                         reads=[('sbs', es_)], writes=[('Et', es_)])
                                S.op('pe', lambda e, kc=kc, es_=es_, n=n: e.matmul(
                                    banks[ob][:, 0:dv + 1], lhsT=Et[:, es_, :], rhs=Vaug[:, kc, 0:dv + 1],
                                    start=(n == 0), stop=(n == len(chunks) - 1)),
                                    reads=[('Et', es_), ('Vaug', kc), 'Vaug_ones'], writes=[bk(ob)])
                            if isd:
                                hq = g * 4 + gi
                                S.op('dve', lambda e: e.tensor_tensor(out=sst[:, 0:1], in0=banks[ob][:, dv:dv + 1], in1=snk[:, hq:hq + 1],
                                                                      op=ALU.add), reads=[bk(ob), 'snk'], writes=[('sst', 0)])
                                S.op('dve', lambda e: e.reciprocal(out=sst[:, 1:2], in_=sst[:, 0:1]), reads=[('sst', 0)], writes=[('sst', 1)])
                            else:
                                S.op('dve', lambda e: e.reciprocal(out=sst[:, 1:2], in_=banks[ob][:, dv:dv + 1]),
                                     reads=[bk(ob)], writes=[('sst', 1)])
                            os_ = qt % 2
                            S.op('dve', lambda e: e.tensor_scalar(out=onb[:, os_, gi * dv:(gi + 1) * dv], in0=banks[ob][:, 0:dv],
                                                                   scalar1=sst[:, 1:2], scalar2=None, op0=ALU.mult),
                                 reads=[bk(ob), ('sst', 1)], writes=[('onb', os_)])
                        for ci in range(2 if isd else 1):
                            b = next_bank()
                            pb = banks[b][:].bitcast(BF16)
                            S.op('pe', lambda e, pb=pb, ci=ci: e.transpose(out=pb[:, 0:128], in_=onb[:, os_, ci * 128:(ci + 1) * 128],
                                                                            identity=ident_b[:]),
                                 reads=[('onb', os_), 'ident_b'], writes=[bk(b)])
                            och = (2 * g + ci) if isd else g
                            S.op('act', lambda e, pb=pb: e.copy(out=oT[:, och, qt * 128:(qt + 1) * 128], in_=pb[:, 0:128]),
                                 reads=[bk(b)], writes=[('oT', och)])
            S.barrier()
            out_proj(w_out_d, oT, st)
        S.barrier()

    def mixer_b(l):
        with ExitStack() as st:
            oT = sb("oT", [128, 16, NT], BF16, st)
            with ExitStack() as st2:
                hT = sb("hT", [128, 16, NT], BF16, st2)
                wt = [sb("wb%d" % i, [128, 16, 256], BF16, st2) for i in range(1)]
                wg = sb("wg", [128, 16, 32], BF16, st2)
                gb = sb("gb", [128, 32], F32, st2)
                nrm = sb("nrm", [128, 256], F32, st2)
                keep = sb("keep", [128, 1], F32, st2)
                m0e = sb("m0e", [128, 16], F32, st2)
                Um = sb("Um", [128, 2, 128], F32, st2)
                Mk = sb("Mk", [128, 2, 128], F32, st2)
                G = sb("G", [128, 256], F32, st2)
                LF = sb("LF", [128, 256], F32, st2)
                BB = sb("BB", [128, 256], F32, st2)
                AB = sb("AB", [128, 128], F32, st2)
                EE = sb("EE", [128, 128], F32, st2)
                WL = sb("WL", [128, 128], F32, st2)
                DT = sb("DT", [128, 128], F32, st2)
                BT8 = sb("BT8", [8, 16], F32, st2)
                EM8 = sb("EM8", [8, 16], F32, st2)
                M8 = sb("M8", [8, 8], F32, st2)
                T8 = sb("T8", [8, 2], F32, st2)
                SC8 = sb("SC8", [8, 8], F32, st2)
                d8 = sb("d8", [8, 8, 8], F32, st2)
                SCB = sb("SCB", [128, 64], F32, st2)
                QT = sb("QT", [128, NT], BF16, st2)
                KT = sb("KT", [128, NT], BF16, st2)
                Ktm = sb("Ktm", [128, 8, 128], BF16, st2)
                Vaug = sb("Vaug", [128, 8, 260], BF16, st2)
                SIGO = sb("SIGO", [128, 2, 256], F32, st2)
                Hs = sb("Hs", [128, 8, 256], F32, st2)
                S32 = sb("S32", [128, 2, 257], F32, st2)
                Sbf = sb("Sbf", [128, 2, 260], BF16, st2)
                outS = sb("outS", [128, 1, 257], F32, st2)
                lfbc = sb("lfbc", [128, 1, 128], F32, st2)
                tD = sb("tD", [128, 1, 128], F32, st2)
                Dt = sb("Dt", [128, 1, 128], F32, st2)
                EB = sb("EB", [128, 1, 128], F32, st2)
                Qs = sb("Qs", [128, 2, 128], BF16, st2)
                PT = sb("PT", [128, 2, 128], BF16, st2)
                Kw = sb("Kw", [128, 2, 128], BF16, st2)
                rr = sb("rr", [128, 2, 4], F32, st2)
                hn = tmp_f[:, 0, 0:256]
                onb = sb("onb", [128, 2, 256], BF16, st2)
                sst = sb("sst", [128, 8], F32, st2)

                dma('sp', gb[:], b_gb[:, :], writes=['gb'])
                dma('sp', keep[:], b_keep[:, :], writes=['keep'])
                dma('sp', m0e[:], b_m0[:, :], writes=['m0raw'])
                dma('sp', Um[:], b_um[:, :].rearrange("p (a b) -> p a b", a=2), writes=['Um'])
                dma('sp', Mk[:], b_mk[:, :].rearrange("p (a b) -> p a b", a=2), writes=['Mk'])
                dma('pool', wg[:], b_w_in[:, 6144:6176].rearrange("(c p) n -> p c n", p=128), writes=['wg'])
                S.op('act', lambda e: e.activation(out=m0e[:], in_=m0e[:], func=AF.Exp), reads=['m0raw'], writes=['m0e'])
                S.op('dve', lambda e: e.memset(Vaug[:, :, 256:260], 1.0), writes=['Vaug_ones'])
                adaln_in(0, list(range(8)), hT, 'hT')

                bg = next_bank()
                for tt in range(8):
                    for kc in range(16):
                        S.op('pe', lambda e, tt=tt, kc=kc: e.matmul(
                            banks[bg][:, tt * 32:(tt + 1) * 32], lhsT=hT[:, kc, tt * 128:(tt + 1) * 128], rhs=wg[:, kc, :],
                            start=(kc == 0), stop=(kc == 15)), reads=['wg', ('hT', kc)], writes=[bk(bg)])
                for tt in range(8):
                    S.op('dve', lambda e, tt=tt: e.tensor_tensor(out=G[:, tt * 32:(tt + 1) * 32], in0=banks[bg][:, tt * 32:(tt + 1) * 32],
                                                                  in1=gb[:], op=ALU.add), reads=[bk(bg), 'gb'], writes=['G'])
                S.op('act', lambda e: e.activation(out=LF[:], in_=G[:], func=AF.Exp, scale=-1.0), reads=['G'], writes=['LF'])
                S.op('dve', lambda e: e.tensor_scalar(out=LF[:], in0=LF[:], scalar1=1.0, scalar2=None, op0=ALU.add),
                     reads=['LF'], writes=['LF'])
                S.op('act', lambda e: e.activation(out=LF[:], in_=LF[:], func=AF.Ln), reads=['LF'], writes=['LF'])
                S.op('dve', lambda e: e.tensor_scalar(out=LF[:], in0=LF[:], scalar1=-1.0, scalar2=None, op0=ALU.mult),
                     reads=['LF'], writes=['LF'])
                bb_ = next_bank()
                for c in range(8):
                    for d in range(2):
                        cd = c * 2 + d
                        lf_cd = LF[:, c * 32 + (2 * d + 1) * 8:c * 32 + (2 * d + 1) * 8 + 8]
                        S.op('pe', lambda e, cd=cd, d=d, lf_cd=lf_cd: e.matmul(
                            banks[bb_][:, cd * 16:cd * 16 + 8], lhsT=Um[:, d, :], rhs=lf_cd, start=True, stop=True),
                            reads=['Um', 'LF'], writes=[bk(bb_)])
                        S.op('pe', lambda e, cd=cd, lf_cd=lf_cd: e.matmul(
                            banks[bb_][:, cd * 16 + 8:cd * 16 + 16], lhsT=ones_f[:], rhs=lf_cd, start=True, stop=True),
                            reads=['ones_f', 'LF'], writes=[bk(bb_)])
                S.op('act', lambda e: e.copy(out=BB[:], in_=banks[bb_][:, 0:256]), reads=[bk(bb_)], writes=['BB'])
                for c in range(8):
                    for d in range(2):
                        cd = c * 2 + d
                        ig = G[:, c * 32 + 2 * d * 8:c * 32 + 2 * d * 8 + 8]
                        S.op('dve', lambda e, cd=cd, ig=ig: e.tensor_tensor(out=AB[:, cd * 8:cd * 8 + 8], in0=ig,
                                                                             in1=BB[:, cd * 16:cd * 16 + 8], op=ALU.subtract),
                             reads=['G', 'BB'], writes=['AB'])
                        S.op('dve', lambda e, cd=cd: e.tensor_tensor(out=EE[:, cd * 8:cd * 8 + 8], in0=AB[:, cd * 8:cd * 8 + 8],
                                                                      in1=BB[:, cd * 16 + 8:cd * 16 + 16], op=ALU.add),
                             reads=['AB', 'BB'], writes=['EE'])
                        S.op('act', lambda e, cd=cd: e.activation(out=DT[:, cd * 8:cd * 8 + 8], in_=BB[:, cd * 16 + 8:cd * 16 + 16],
                                                                   func=AF.Exp), reads=['BB'], writes=['DT'])
                S.op('act', lambda e: e.activation(out=WL[:], in_=EE[:], func=AF.Exp), reads=['EE'], writes=['WL'])
                bt_ = next_bank()
                for cd in range(16):
                    c, d = divmod(cd, 2)
                    lf_cd = LF[:, c * 32 + (2 * d + 1) * 8:c * 32 + (2 * d + 1) * 8 + 8]
                    S.op('pe', lambda e, cd=cd, lf_cd=lf_cd: e.matmul(banks[bt_][0:8, cd:cd + 1], lhsT=lf_cd, rhs=ones_f[:, 0:1],
                                                                      start=True, stop=True),
                         reads=['LF', 'ones_f'], writes=[bk(bt_)])
                S.op('act', lambda e: e.copy(out=BT8[:], in_=banks[bt_][0:8, 0:16]), reads=[bk(bt_)], writes=['BT8'])
                for g4 in range(4):
                    be = next_bank()
                    for i in range(4):
                        cd = g4 * 4 + i
                        S.op('pe', lambda e, cd=cd, i=i, be=be: e.matmul(banks[be][0:8, i * 128:(i + 1) * 128],
                                                                        lhsT=EE[:, cd * 8:cd * 8 + 8], rhs=ident_f[:],
                                                                        start=True, stop=True),
                             reads=['EE', 'ident_f'], writes=[bk(be)])
                    S.op('dve', lambda e, g4=g4, be=be: e.tensor_reduce(
                        out=EM8[:, g4 * 4:(g4 + 1) * 4], in_=banks[be][0:8, :].rearrange("p (a b) -> p a b", a=4),
                        axis=mybir.AxisListType.X, op=ALU.max), reads=[bk(be)], writes=['EM8'])
                for j in range(4):
                    for d in range(2):
                        ca, cb = (2 * j, 2 * j + 1) if d == 0 else (2 * j + 1, 2 * j)
                        a_, b_ = ca * 2 + d, cb * 2 + d
                        jd = j * 2 + d
                        S.op('dve', lambda e, a_=a_: e.tensor_tensor(out=T8[:, 0:1], in0=BT8[:, a_:a_ + 1], in1=EM8[:, a_:a_ + 1],
                                                                      op=ALU.max), reads=['BT8', 'EM8'], writes=[('T8', 0)])
                        S.op('dve', lambda e, b_=b_: e.tensor_tensor(out=T8[:, 1:2], in0=T8[:, 0:1], in1=BT8[:, b_:b_ + 1],
                                                                      op=ALU.add), reads=['BT8', ('T8', 0)], writes=[('T8', 1)])
                        S.op('dve', lambda e, b_=b_, jd=jd: e.tensor_tensor(out=M8[:, jd:jd + 1], in0=T8[:, 1:2], in1=EM8[:, b_:b_ + 1],
                                                                             op=ALU.max), reads=['EM8', ('T8', 1)], writes=['M8'])
                S.op('act', lambda e: e.activation(out=SC8[:], in_=M8[:], func=AF.Exp, scale=-1.0), reads=['M8'], writes=['SC8'])
                dma('sp', b_mo[:, :], M8[:], reads=['M8'], pool='st')
                bs_ = next_bank()
                for jd in range(8):
                    S.op('dve', lambda e, jd=jd: e.tensor_scalar(out=d8[:, jd, :], in0=ident_f[0:8, 0:8], scalar1=SC8[:, jd:jd + 1],
                                                                  scalar2=None, op0=ALU.mult),
                         reads=['ident_f', 'SC8'], writes=[('d8', jd)])
                    S.op('pe', lambda e, jd=jd: e.matmul(banks[bs_][:, jd * 8:(jd + 1) * 8], lhsT=ones_f[0:8, :], rhs=d8[:, jd, :],
                                                         start=True, stop=True),
                         reads=['ones_f', ('d8', jd)], writes=[bk(bs_)])
                S.op('act', lambda e: e.copy(out=SCB[:], in_=banks[bs_][:, 0:64]), reads=[bk(bs_)], writes=['SCB'])

                wi = [0]

                def loadw(col_ranges):
                    w = wt[0]
                    wk = ('wb', 0)
                    wi[0] += 1
                    o_ = 0
                    for c0, n_ in col_ranges:
                        dma('pool', w[:, :, o_:o_ + n_], b_w_in[:, c0:c0 + n_].rearrange("(c p) n -> p c n", p=128), writes=[wk])
                        o_ += n_
                    return w, wk

                def chunk(c, d, h, s_):
                    col = (c * 2 + d) * 8 + h
                    lfi = c * 32 + (2 * d + 1) * 8 + h
                    S.op('dve', lambda e: e.tensor_scalar(out=lfbc[:, 0, :], in0=ones_f[:], scalar1=LF[:, lfi:lfi + 1], scalar2=None,
                                                           op0=ALU.mult), reads=['LF', 'ones_f'], writes=[('lfbc', 0)])
                    br = next_bank()
                    S.op('pe', lambda e: e.matmul(banks[br][:, 0:128], lhsT=lfbc[:, 0, :], rhs=Um[:, d, :], start=True, stop=True),
                         reads=[('lfbc', 0), 'Um'], writes=[bk(br)])
                    S.op('dve', lambda e: e.tensor_tensor(out=tD[:, 0, :], in0=banks[br][:, 0:128], in1=Mk[:, d, :], op=ALU.add),
                         reads=[bk(br), 'Mk'], writes=[('tD', 0)])
                    S.op('act', lambda e: e.activation(out=Dt[:, 0, :], in_=tD[:, 0, :], func=AF.Exp, bias=AB[:, col:col + 1], scale=1.0),
                         reads=[('tD', 0), 'AB'], writes=[('Dt', 0)])
                    S.op('act', lambda e: e.activation(out=EB[:, 0, :], in_=banks[br][:, 0:128], func=AF.Exp),
                         reads=[bk(br)], writes=[('EB', 0)])
                    S.op('dve', lambda e: e.tensor_tensor(out=Qs[:, s_, :], in0=QT[:, c * 128:(c + 1) * 128], in1=EB[:, 0, :], op=ALU.mult),
                         reads=['QT', ('EB', 0)], writes=[('Qs', s_)])
                    bs2 = next_bank()
                    S.op('pe', lambda e: e.matmul(banks[bs2][:, 0:128], lhsT=KT[:, c * 128:(c + 1) * 128], rhs=QT[:, c * 128:(c + 1) * 128],
                                                  start=True, stop=True), reads=['KT', 'QT'], writes=[bk(bs2)])
                    S.op('dve', lambda e: e.tensor_tensor(out=PT[:, s_, :], in0=banks[bs2][:, 0:128], in1=Dt[:, 0, :], op=ALU.mult),
                         reads=[bk(bs2), ('Dt', 0)], writes=[('PT', s_)])
                    bn = next_bank()
                    S.op('pe', lambda e: e.matmul(banks[bn][:, 0:257], lhsT=PT[:, s_, :], rhs=Vaug[:, c, 0:257], start=True, stop=False),
                         reads=[('PT', s_), ('Vaug', c), 'Vaug_ones'], writes=[bk(bn)])
                    S.op('pe', lambda e: e.matmul(banks[bn][:, 0:257], lhsT=Qs[:, s_, :], rhs=Sbf[:, d, 0:257], start=False, stop=True),
                         reads=[('Qs', s_), ('Sbf', d)], writes=[bk(bn)])
                    S.op('dve', lambda e: e.tensor_scalar(out=rr[:, s_, 2:3], in0=banks[bn][:, 256:257], scalar1=-1.0, scalar2=None,
                                                           op0=ALU.mult), reads=[bk(bn)], writes=[('rr2', s_)])
                    S.op('dve', lambda e: e.scalar_tensor_tensor(out=rr[:, s_, 0:1], in0=banks[bn][:, 256:257], scalar=1.0,
                                                                  in1=rr[:, s_, 2:3], op0=ALU.max, op1=ALU.max),
                         reads=[bk(bn), ('rr2', s_)], writes=[('rr0', s_)])
                    S.op('dve', lambda e: e.reciprocal(out=rr[:, s_, 1:2], in_=rr[:, s_, 0:1]), reads=[('rr0', s_)], writes=[('rr1', s_)])
                    if d == 0:
                        S.op('dve', lambda e: e.tensor_scalar(out=Hs[:, c, :], in0=banks[bn][:, 0:256], scalar1=rr[:, s_, 1:2], scalar2=None,
                                                               op0=ALU.mult), reads=[bk(bn), ('rr1', s_)], writes=[('Hs', c)])
                    else:
                        S.op('dve', lambda e: e.scalar_tensor_tensor(out=Hs[:, c, :], in0=banks[bn][:, 0:256], scalar=rr[:, s_, 1:2],
                                                                      in1=Hs[:, c, :], op0=ALU.mult, op1=ALU.add),
                             reads=[bk(bn), ('rr1', s_), ('Hs', c)], writes=[('Hs', c)])
                    S.op('dve', lambda e: e.tensor_scalar(out=Kw[:, s_, :], in0=Ktm[:, c, :], scalar1=WL[:, col:col + 1], scalar2=None,
                                                           op0=ALU.mult), reads=[('Ktm', c), 'WL'], writes=[('Kw', s_)])
                    bu = next_bank()
                    S.op('pe', lambda e: e.matmul(banks[bu][:, 0:257], lhsT=Kw[:, s_, :], rhs=Vaug[:, c, 0:257], start=True, stop=True),
                         reads=[('Kw', s_), ('Vaug', c), 'Vaug_ones'], writes=[bk(bu)])
                    S.op('dve', lambda e: e.scalar_tensor_tensor(out=S32[:, d, :], in0=S32[:, d, :], scalar=DT[:, col:col + 1],
                                                                  in1=banks[bu][:, 0:257], op0=ALU.mult, op1=ALU.add),
                         reads=[('S32', d), 'DT', bk(bu)], writes=[('S32', d)])
                    S.op('act', lambda e: e.copy(out=Sbf[:, d, 0:257], in_=S32[:, d, :]), reads=[('S32', d)], writes=[('Sbf', d)])

                for h in range(8):
                    w, wk = loadw([(h * 128, 128), (1024 + h * 128, 128)])
                    for slot, dst, dkey, scl in ((0, QT, 'QT', 128 ** -0.5), (1, KT, 'KT', 1.0)):
                        for hf in range(2):
                            b = next_bank()
                            for kc in range(16):
                                S.op('pe', lambda e, kc=kc, b=b, hf=hf, slot=slot, w=w: e.matmul(
                                    banks[b][:], lhsT=w[:, kc, slot * 128:(slot + 1) * 128], rhs=hT[:, kc, hf * 512:(hf + 1) * 512],
                                    start=(kc == 0), stop=(kc == 15)), reads=[wk, ('hT', kc)], writes=[bk(b)])
                            S.op('act', lambda e, b=b, hf=hf, dst=dst, scl=scl: e.mul(out=dst[:, hf * 512:(hf + 1) * 512], in_=banks[b][:],
                                                                                      mul=scl), reads=[bk(b)], writes=[dkey])
                    for tt in range(8):
                        b = next_bank()
                        for kc in range(16):
                            S.op('pe', lambda e, kc=kc, b=b, tt=tt, w=w: e.matmul(
                                banks[b][:, 0:128], lhsT=hT[:, kc, tt * 128:(tt + 1) * 128], rhs=w[:, kc, 128:256],
                                start=(kc == 0), stop=(kc == 15)), reads=[wk, ('hT', kc)], writes=[bk(b)])
                        S.op('act', lambda e, b=b, tt=tt: e.copy(out=Ktm[:, tt, :], in_=banks[b][:, 0:128]),
                             reads=[bk(b)], writes=[('Ktm', tt)])
                    w, wk = loadw([(2048 + h * 256, 256)])
                    for tt in range(8):
                        b = next_bank()
                        for kc in range(16):
                            S.op('pe', lambda e, kc=kc, b=b, tt=tt, w=w: e.matmul(
                                banks[b][:, 0:256], lhsT=hT[:, kc, tt * 128:(tt + 1) * 128], rhs=w[:, kc, :],
                                start=(kc == 0), stop=(kc == 15)), reads=[wk, ('hT', kc)], writes=[bk(b)])
                        S.op('dve', lambda e, b=b, tt=tt: e.tensor_copy(out=Vaug[:, tt, 0:256], in_=banks[b][:, 0:256]),
                             reads=[bk(b)], writes=[('Vaug', tt)])
                    for d in range(2):
                        r0 = (d * 8 + h) * 128
                        dma('sp', S32[:, d, :], b_S0[r0:r0 + 128, :], writes=[('S32', d)])
                        S.op('dve', lambda e, d=d: e.tensor_scalar(out=S32[:, d, :], in0=S32[:, d, :],
                                                                    scalar1=m0e[:, d * 8 + h:d * 8 + h + 1], scalar2=None, op0=ALU.mult),
                             reads=[('S32', d), 'm0e'], writes=[('S32', d)])
                        S.op('act', lambda e, d=d: e.copy(out=Sbf[:, d, 0:257], in_=S32[:, d, :]), reads=[('S32', d)], writes=[('Sbf', d)])
                    nchunk = 0
                    for d in range(2):
                        order = list(range(8)) if d == 0 else list(range(7, -1, -1))
                        for n_, c in enumerate(order):
                            if n_ > 0 and n_ % 2 == 0:
                                S.op('dve', lambda e, d=d: e.tensor_scalar(out=S32[:, d, :], in0=S32[:, d, :], scalar1=keep[:, 0:1],
                                                                            scalar2=None, op0=ALU.mult),
                                     reads=[('S32', d), 'keep'], writes=[('S32', d)])
                                S.op('act', lambda e, d=d: e.copy(out=Sbf[:, d, 0:257], in_=S32[:, d, :]),
                                     reads=[('S32', d)], writes=[('Sbf', d)])
                            chunk(c, d, h, nchunk % 2)
                            nchunk += 1
                            if n_ % 2 == 1:
                                j = c // 2
                                jd = j * 2 + d
                                so = 0
                                S.op('dve', lambda e, d=d, jd=jd, so=so: e.tensor_scalar(
                                    out=outS[:, so, :], in0=S32[:, d, :], scalar1=SCB[:, jd * 8 + h:jd * 8 + h + 1], scalar2=None,
                                    op0=ALU.mult), reads=[('S32', d), 'SCB'], writes=[('outS', so)])
                                r0 = (j * 16 + d * 8 + h) * 128
                                dma('sp', b_So[r0:r0 + 128, :], outS[:, so, :], reads=[('outS', so)], pool='st')
                    w, wk = loadw([(4096 + h * 256, 256)])
                    dma('sp', nrm[:], b_nrm[:, h * 256:(h + 1) * 256], writes=['nrm'])
                    for tt in range(8):
                        os_ = tt % 2
                        b = next_bank()
                        for kc in range(16):
                            S.op('pe', lambda e, kc=kc, b=b, tt=tt, w=w: e.matmul(
                                banks[b][:, 0:256], lhsT=hT[:, kc, tt * 128:(tt + 1) * 128], rhs=w[:, kc, :],
                                start=(kc == 0), stop=(kc == 15)), reads=[wk, ('hT', kc)], writes=[bk(b)])
                        S.op('act', lambda e, b=b, os_=os_: e.activation(out=SIGO[:, os_, :], in_=banks[b][:, 0:256], func=AF.Sigmoid),
                             reads=[bk(b)], writes=[('SIGO', os_)])
                        S.op('act', lambda e, tt=tt: e.activation(out=sq_junk[:, 0:256], in_=Hs[:, tt, :], func=AF.Square,
                                                                   accum_out=sst[:, 0:1]),
                             reads=[('Hs', tt)], writes=['sq_junk', ('sst', 0)])
                        S.op('dve', lambda e: e.tensor_scalar(out=sst[:, 1:2], in0=sst[:, 0:1], scalar1=1.0 / 256, scalar2=EPS,
                                                               op0=ALU.mult, op1=ALU.add), reads=[('sst', 0)], writes=[('sst', 1)])
                        S.op('act', lambda e: e.activation(out=sst[:, 2:3], in_=sst[:, 1:2], func=AF.Ln),
                             reads=[('sst', 1)], writes=[('sst', 2)])
                        S.op('act', lambda e: e.activation(out=sst[:, 3:4], in_=sst[:, 2:3], func=AF.Exp, scale=-0.5),
                             reads=[('sst', 2)], writes=[('sst', 3)])
                        S.op('dve', lambda e, tt=tt: e.scalar_tensor_tensor(
                            out=hn, in0=Hs[:, tt, :], scalar=sst[:, 3:4], in1=nrm[:],
                            op0=ALU.mult, op1=ALU.mult), reads=[('Hs', tt), ('sst', 3), 'nrm'], writes=[('tmp_f', 0)])
                        S.op('dve', lambda e, tt=tt, os_=os_: e.tensor_tensor(out=onb[:, os_, :], in0=hn, in1=SIGO[:, os_, :], op=ALU.mult),
                             reads=[('tmp_f', 0), ('SIGO', os_)], writes=[('onb', os_)])
                        for ci in range(2):
                            b = next_bank()
                            pb = banks[b][:].bitcast(BF16)
                            S.op('pe', lambda e, pb=pb, ci=ci, os_=os_: e.transpose(out=pb[:, 0:128], in_=onb[:, os_, ci * 128:(ci + 1) * 128],
                                                                                    identity=ident_b[:]),
                                 reads=[('onb', os_), 'ident_b'], writes=[bk(b)])
                            och = 2 * h + ci
                            S.op('act', lambda e, pb=pb, och=och, tt=tt: e.copy(out=oT[:, och, tt * 128:(tt + 1) * 128], in_=pb[:, 0:128]),
                                 reads=[bk(b)], writes=[('oT', och)])
            S.barrier()
            out_proj(b_w_out, oT, st)
        S.barrier()

    def out_proj(w_out_d, oT, st):
        with ExitStack() as st3:
            wo = sb("wo", [128, 16, D], BF16, st3)
            for c4 in range(4):
                dma('pool', wo[:, :, c4 * 512:(c4 + 1) * 512],
                    w_out_d[:, c4 * 512:(c4 + 1) * 512].rearrange("(c p) n -> p c n", p=128), writes=[('wo', c4)])
            for tt in range(8):
                yb = [next_bank() for _ in range(4)]
                for q in range(4):
                    for kc in range(16):
                        S.op('pe', lambda e, q=q, kc=kc, tt=tt, yb=yb: e.matmul(
                            banks[yb[q]][:], lhsT=oT[:, kc, tt * 128:(tt + 1) * 128], rhs=wo[:, kc, q * 512:(q + 1) * 512],
                            start=(kc == 0), stop=(kc == 15)), reads=[('wo', q), ('oT', kc)], writes=[bk(yb[q])])
                post_norm(0, tt, yb)

    for l in layers:
        with ExitStack() as wst:
            modulation(l, wst)
        S.barrier()
        if l % 4 == 0:
            mixer_a(l)
        elif l % 4 == 1:
            mixer_b(l)
        elif l % 4 == 2:
            mixer_cd(l, 'c')
        elif l % 4 == 3:
            mixer_cd(l, 'd')
        if dbg == ('mid', l):
            for tt in range(8):
                dma('sp', dbg_out[tt * 128:(tt + 1) * 128, :], xres[:, tt, :], reads=[('x', tt)], pool='st')
        mlp(l)

    for tt in range(8):
        dma('sp', y_out[tt * 128:(tt + 1) * 128, :], xres[:, tt, :], reads=[('x', tt)], pool='st')
    if max_ops is not None:
        S.truncate(max_ops)
    info = S.emit()
    es.close()
    return nc, info


def fm(v):
    v = np.asarray(v, np.float32)
    return np.ascontiguousarray(np.moveaxis(v.reshape(v.shape[:-1] + (-1, 128)), -1, 0))


def make_in_maps(inp, nlw=4):
    maps = []
    ident = np.eye(128, dtype=np.float32)
    rot = rot_matrix()
    shared = dict(
        w_mod=inp['w_mod'][:nlw].reshape(nlw * D, 6 * D), w_ff1=inp['w_ff1'][:nlw].reshape(nlw * D, DFF),
        w_ff2=inp['w_ff2'][:nlw].reshape(nlw * DFF, D), ident=ident, rotm=rot,
        bmod=fm(inp['b_mod']).reshape(128, 4 * 96),
        gfm=fm(inp['g_norm']).reshape(128, 4 * 4 * 16),
        a_w_in=inp['a_w_in'][0], a_w_out=inp['a_w_out'][0],
        a_lam=np.ascontiguousarray(np.broadcast_to(inp['a_lambda'][0].reshape(1, 4096), (128, 4096))),
        a_sub=np.ascontiguousarray(np.broadcast_to(inp['a_subln'][0].reshape(1, 128), (128, 128))),
    )
    NA_L = [[0, 1, 2, 3], [0, 1, 2, 3], [0, 1, 2, 3, 4], [1, 2, 3, 4, 5], [2, 3, 4, 5, 6], [3, 4, 5, 6, 7], [4, 5, 6, 7], [4, 5, 6, 7]]
    rpb = np.asarray(inp['c_rpb'][0], np.float32).reshape(16, 465)
    ext = np.concatenate([rpb, np.full((16, 1), NEG, np.float32), np.zeros((16, 1), np.float32)], 1)
    cbias = {}
    dmask = {}
    for sample in (True, False):
        idx = np.full((8, 128, 5, 128), 465, np.int64)
        for qt in range(8):
            for si, kc in enumerate(NA_L[qt]):
                k = kc * 128 + np.arange(128)[:, None]
                q = qt * 128 + np.arange(128)[None, :]
                if sample:
                    rk, ck_, rq, cq_ = k // 64, k % 64, q // 64, q % 64
                    r0 = np.clip(rq - 4, 0, 8)
                    cs = np.clip(cq_ - 8, 0, 48)
                    valid = (rk >= r0) & (rk < r0 + 8) & (ck_ >= cs) & (ck_ < cs + 16)
                    ii = np.where(valid, (rk - rq + 7) * 31 + np.clip(ck_ - cq_, -15, 15) + 15, 465)
                else:
                    ii = np.full((128, 128), 466 if kc // 2 == qt // 2 else 465)
                idx[qt, :, si, :] = ii
        cbias[sample] = np.ascontiguousarray(ext[:, idx].reshape(16 * 8 * 128, 5 * 128))
        dm = np.full((128, 2, 8, 128), NEG, np.float32)
        kp = np.arange(128)[:, None]
        qq = np.arange(128)[None, :]
        for qt in range(8):
            if sample:
                dm[:, 0, qt, :] = np.where(kp >= qq, 0.0, NEG)
                dm[:, 1, qt, :] = np.where(kp <= qq, 0.0, NEG)
            else:
                dm[:, 0, qt, :] = 0.0 if qt % 2 == 1 else NEG
                dm[:, 1, qt, :] = 0.0 if qt % 2 == 0 else NEG
        dmask[sample] = dm.reshape(128, 2 * 8 * 128)
    shared.update(
        c_w_in=inp['c_w_in'][0], c_w_out=inp['c_w_out'][0], d_w_in=inp['d_w_in'][0], d_w_out=inp['d_w_out'][0],
        d_snk=np.ascontiguousarray(np.broadcast_to(inp['d_sink'][0].reshape(1, 32), (128, 32))))
    pi = np.arange(128)[:, None]
    ti = np.arange(128)[None, :]
    um = np.concatenate([(pi <= ti), (pi >= ti)], 1).astype(np.float32)
    mkb = np.concatenate([np.where(pi <= ti, 0.0, NEG), np.where(pi >= ti, 0.0, NEG)], 1).astype(np.float32)
    shared.update(
        b_w_in=inp['b_w_in'][0], b_w_out=inp['b_w_out'][0], b_um=um, b_mk=mkb,
        b_gb=np.ascontiguousarray(np.broadcast_to(inp['b_gate_bias'][0].reshape(1, 32), (128, 32))),
        b_nrm=np.ascontiguousarray(np.broadcast_to(inp['b_norm'][0].reshape(1, D), (128, D))))
    for core in range(8):
        sample = core < 4
        m = dict(shared)
        cb = core if sample else 0
        if sample:
            ct = np.swapaxes(inp['state_b_C'][core, 0], -1, -2)
            m['b_S0'] = np.concatenate([ct, inp['state_b_n'][core, 0][..., None]], -1).reshape(16 * 128, 257)
            m['b_m0'] = np.ascontiguousarray(np.broadcast_to(inp['state_b_m'][core, 0].reshape(1, 16), (128, 16)))
            m['b_keep'] = np.ones((128, 1), np.float32)
        else:
            m['b_S0'] = np.zeros((16 * 128, 257), np.float32)
            m['b_m0'] = np.zeros((128, 16), np.float32)
            m['b_keep'] = np.zeros((128, 1), np.float32)
        m['c_ck'] = inp['cache_c_k'][cb, 0].reshape(512, D)
        m['c_cv'] = inp['cache_c_v'][cb, 0].reshape(512, D)
        m['d_ck'] = inp['cache_d_k'][cb, 0].reshape(512, 512)
        m['d_cv'] = inp['cache_d_v'][cb, 0].reshape(512, 512)
        m['c_bias'] = cbias[sample]
        m['d_mask'] = dmask[sample]
        m['ctxb'] = np.full((128, 1), 0.0 if sample else NEG, np.float32)
        if sample:
            m['x'] = inp['x_sample'][core]
            m['cvec'] = fm(inp['c'][core])
            m['a_ck'] = inp['cache_a_k'][core, 0].reshape(512, D)
            m['a_cv'] = inp['cache_a_v'][core, 0].reshape(512, D)
            mask = np.zeros((12, 4), np.float32)
        else:
            p0 = (core - 4) * 4
            m['x'] = inp['x_prompt'][p0:p0 + 4].reshape(NT, D)
            m['cvec'] = fm(inp['c_ctx'])
            m['a_ck'] = inp['cache_a_k'][0, 0].reshape(512, D)
            m['a_cv'] = inp['cache_a_v'][0, 0].reshape(512, D)
            mask = np.full((12, 4), NEG, np.float32)
            for qp in range(4):
                mask[2 * qp:2 * qp + 2, qp] = 0.0
        m['a_mask'] = np.ascontiguousarray(np.broadcast_to(mask.reshape(1, 48), (128, 48)))
        cq, sq = rope_tables(sample, 0.125)
        ck, sk = rope_tables(sample, 1.0)
        m['a_cosq'], m['a_sinq'], m['a_cosk'], m['a_sink'] = cq, sq, ck, sk
        maps.append({k: (v if (v.dtype == np.float32 and v.flags['C_CONTIGUOUS']) else np.ascontiguousarray(v, dtype=np.float32))
                     for k, v in m.items()})
    return maps


def kernel(**inputs):
    inp = {k: np.asarray(v) for k, v in inputs.items()}
    nc, info = build()
    maps = make_in_maps(inp)
    res = run_bass_kernel_spmd(nc, maps, core_ids=list(range(8)))
    r = res.results
    y_sample = np.stack([r[c]['y'] for c in range(4)], 0)
    y_prompt = np.concatenate([r[c]['y'].reshape(4, 256, D) for c in range(4, 8)], 0)

    def ctx(name, hh, dd):
        return np.concatenate([r[c][name].reshape(4, 1, 256, hh, dd) for c in range(4, 8)], 0)
    so = np.concatenate([r[c]['b_So'].reshape(4, 1, 2, 8, 128, 257) for c in range(4, 8)], 0)
    b_C = np.ascontiguousarray(np.swapaxes(so[..., 0:256], -1, -2))
    b_n = np.ascontiguousarray(so[..., 256])
    b_m = np.concatenate([np.transpose(r[c]['b_mo'].reshape(8, 4, 1, 2), (1, 2, 3, 0)) for c in range(4, 8)], 0)
    b_m = np.ascontiguousarray(b_m)
    return (y_prompt, y_sample, ctx('a_k', 16, 128), ctx('a_v', 16, 128), b_C, b_n, b_m,
            ctx('c_k', 16, 128), ctx('c_v', 16, 128), ctx('d_k', 8, 64), ctx('d_v', 8, 64))
```

```python
import math
import numpy as np
import concourse.bass as bass
import concourse.mybir as mybir
from concourse.bass_utils import run_bass_kernel_spmd
from contextlib import ExitStack

F32 = mybir.dt.float32
BF16 = mybir.dt.bfloat16
AF = mybir.ActivationFunctionType
ALU = mybir.AluOpType

D = 2048
NT = 1024
DFF = 8192
EPS = 1e-6
NEG = -30000.0


class _Rec:
    def __init__(self):
        self.call = None

    def __getattr__(self, name):
        def f(*a, **k):
            self.call = (name, a, k)
            return None
        return f


class Sched:
    COMPUTE = ('pe', 'act', 'dve', 'pool')

    def __init__(self, nc, es):
        self.nc = nc
        self.es = es
        self.ops = []
        self.last_w = {}
        self.readers = {}
        self.engs = ('pe', 'act', 'dve', 'pool', 'sp')
        self.psem = {e: es.enter_context(nc.semaphore("P_" + e)) for e in self.COMPUTE}
        self.dma_last = {}
        self.dsems = {}
        self.last_on = {}
        self.bar = None
        self.bar_done = set()

    def barrier(self):
        self.bar = set(self.last_on.values()) | set(self.dma_last.values())
        self.bar_done = set()

    def op(self, eng, fn, reads=(), writes=(), dma=None):
        deps = set()
        for r in reads:
            if r in self.last_w:
                deps.add(self.last_w[r])
            if isinstance(r, tuple) and r[0] == 'bank':
                for k_, v_ in (self.readers.get(r) or {}).items():
                    if k_ != eng:
                        deps.add(v_)
        for w in writes:
            if w in self.last_w:
                deps.add(self.last_w[w])
            rd = self.readers.get(w)
            if rd:
                deps.update(rd.values())
        if dma is not None:
            if dma not in self.dsems:
                self.dsems[dma] = self.es.enter_context(self.nc.semaphore("D_" + dma))
            if dma in self.dma_last:
                deps.add(self.dma_last[dma])
        if self.bar is not None and eng not in self.bar_done:
            deps.update(self.bar)
            self.bar_done.add(eng)
        oid = len(self.ops)
        rec = _Rec()
        fn(rec)
        name_, a_, k_ = rec.call
        fn = (lambda E, name_=name_, a_=a_, k_=k_: getattr(E, name_)(*a_, **k_))
        self.ops.append([eng, fn, deps, dma, False, None])
        if dma is not None:
            self.dma_last[dma] = oid
        else:
            self.last_on[eng] = oid
        for w in writes:
            self.last_w[w] = oid
            self.readers[w] = {}
        for r in reads:
            d = self.readers.setdefault(r, {})
            d[eng if dma is None else ('dma', oid)] = oid
        return oid

    def truncate(self, n):
        self.ops = self.ops[:n]

    def emit(self, final_wait_eng='sp'):
        ops = self.ops
        for o in ops:
            for d in o[2]:
                do = ops[d]
                if do[3] is None and not (do[0] == 'pe' and o[0] == 'pe' and o[3] is None):
                    do[4] = True
        cnt = {e: 0 for e in self.COMPUTE}
        dcnt = {}
        for o in ops:
            if o[3] is not None:
                dcnt[o[3]] = dcnt.get(o[3], 0) + 16
                o[5] = (o[3], dcnt[o[3]])
            elif o[4]:
                cnt[o[0]] += 1
                o[5] = (o[0], cnt[o[0]])
        waited = {e: {} for e in self.engs}
        streams = {e: [] for e in self.engs}
        for o in ops:
            eng, fn, deps, dma, sig, tok = o
            need = {}
            for d in deps:
                do = ops[d]
                if do[3] is None and do[0] == 'pe' and eng == 'pe' and dma is None:
                    continue
                s, v = do[5]
                if v > need.get(s, 0):
                    need[s] = v
            for s, v in need.items():
                if waited[eng].get(s, 0) >= v:
                    continue
                sem = self.psem[s] if s in self.psem else self.dsems[s]
                streams[eng].append(('w', sem, v))
                waited[eng][s] = v
            if dma is not None:
                streams[eng].append(('i', fn, self.dsems[dma], 16))
            elif sig:
                streams[eng].append(('i', fn, self.psem[eng], 1))
            else:
                streams[eng].append(('i', fn, None, 0))
        for s, v in dcnt.items():
            streams[final_wait_eng].append(('w', self.dsems[s], v))

        def runner(eng):
            def f(E):
                for it in streams[eng]:
                    if it[0] == 'w':
                        E.wait_ge(it[1], it[2])
                    else:
                        ins = it[1](E)
                        if it[2] is not None:
                            ins.then_inc(it[2], it[3])
            return f
        with self.nc.Block() as block:
            block.sync(runner('sp'))
            block.scalar(runner('act'))
            block.vector(runner('dve'))
            block.gpsimd(runner('pool'))
            block.tensor(runner('pe'))
        return dict(n_ops=len(ops), counts=cnt)


def rope_tables(sample, qscale):
    t = np.arange(NT)
    cos = np.ones((64, NT), np.float64)
    sin = np.zeros((64, NT), np.float64)
    if sample:
        inv = 10000.0 ** (-np.arange(16, dtype=np.float32) / 16)
        for grp, pos in ((0, t // 64), (1, t % 64)):
            ang = pos.astype(np.float32)[None, :] * inv[:, None].astype(np.float32)
            c, s = np.cos(ang), np.sin(ang)
            cos[grp * 32:grp * 32 + 16] = c
            cos[grp * 32 + 16:grp * 32 + 32] = c
            sin[grp * 32:grp * 32 + 16] = -s
            sin[grp * 32 + 16:grp * 32 + 32] = s
    cos = np.concatenate([cos, cos], 0) * qscale
    sin = np.concatenate([sin, sin], 0) * qscale
    return cos.astype(np.float32), sin.astype(np.float32)


def rot_matrix():
    P = np.zeros((128, 128), np.float32)
    for m in range(128):
        g, i = divmod(m, 32)
        partner = g * 32 + (i + 16) % 32
        P[partner, m] = 1.0
    return P


def build(layers=(0, 1, 2, 3), dbg=None, max_ops=None):
    NLW = max(layers) + 1
    nc = bass.Bass("TRN2", target_bir_lowering=False)
    es = ExitStack()
    S = Sched(nc, es)

    def din(name, shape):
        return nc.dram_tensor(name, list(shape), F32, kind="ExternalInput").ap()

    def dout(name, shape):
        return nc.dram_tensor(name, list(shape), F32, kind="ExternalOutput").ap()

    x_in = din("x", [NT, D])
    cvec = din("cvec", [128, 16])
    w_mod = din("w_mod", [NLW * D, 6 * D])
    bmod = din("bmod", [128, 4 * 96])
    gfm = din("gfm", [128, 4 * 4 * 16])
    w_ff1 = din("w_ff1", [NLW * D, DFF])
    w_ff2 = din("w_ff2", [NLW * DFF, D])
    ident_d = din("ident", [128, 128])
    a_w_in = din("a_w_in", [D, 3 * D])
    a_w_out = din("a_w_out", [D, D])
    a_ck = din("a_ck", [512, D])
    a_cv = din("a_cv", [512, D])
    a_lam = din("a_lam", [128, 4096])
    a_sub = din("a_sub", [128, 128])
    a_cosq = din("a_cosq", [128, NT])
    a_sinq = din("a_sinq", [128, NT])
    a_cosk = din("a_cosk", [128, NT])
    a_sink = din("a_sink", [128, NT])
    a_mask = din("a_mask", [128, 48])
    rotm = din("rotm", [128, 128])

    c_w_in = din("c_w_in", [D, 3 * D])
    c_w_out = din("c_w_out", [D, D])
    c_ck = din("c_ck", [512, D])
    c_cv = din("c_cv", [512, D])
    c_bias = din("c_bias", [16 * 8 * 128, 5 * 128])
    d_w_in = din("d_w_in", [D, 3072])
    d_w_out = din("d_w_out", [D, D])
    d_ck = din("d_ck", [512, 512])
    d_cv = din("d_cv", [512, 512])
    d_mask = din("d_mask", [128, 2 * 8 * 128])
    d_snk = din("d_snk", [128, 32])
    ctxb_d = din("ctxb", [128, 1])
    b_w_in = din("b_w_in", [D, 6176])
    b_w_out = din("b_w_out", [D, D])
    b_gb = din("b_gb", [128, 32])
    b_nrm = din("b_nrm", [128, D])
    b_keep = din("b_keep", [128, 1])
    b_m0 = din("b_m0", [128, 16])
    b_S0 = din("b_S0", [16 * 128, 257])
    b_um = din("b_um", [128, 256])
    b_mk = din("b_mk", [128, 256])
    b_So = dout("b_So", [4 * 16 * 128, 257])
    b_mo = dout("b_mo", [8, 8])
    c_ko = dout("c_k", [NT, D])
    c_vo = dout("c_v", [NT, D])
    d_ko = dout("d_k", [NT, 512])
    d_vo = dout("d_v", [NT, 512])
    y_out = dout("y", [NT, D])
    a_ko = dout("a_k", [NT, D])
    a_vo = dout("a_v", [NT, D])
    dbg_out = dout("dbg", [NT, D]) if dbg else None

    sb_ctr = [0]

    def sb(name, shape, dt, stack=es):
        sb_ctr[0] += 1
        return stack.enter_context(nc.sbuf_tensor("%s_%d" % (name, sb_ctr[0]), list(shape), dt))

    banks = [es.enter_context(nc.psum_tensor("bank%d" % i, [128, 512], F32)) for i in range(8)]
    bank_rr = [0]

    def next_bank():
        b = bank_rr[0]
        bank_rr[0] = (b + 1) % 8
        return b

    def bk(b):
        return ('bank', b)

    xres = sb("xres", [128, 8, D], F32)
    ident_f = sb("ident_f", [128, 128], F32)
    ident_b = sb("ident_b", [128, 128], BF16)
    ones_f = sb("ones_f", [128, 128], F32)
    cv_f = sb("cv_f", [128, 16], F32)
    cv_b = sb("cv_b", [128, 16], BF16)
    bmod_s = sb("bmod_s", [128, 4 * 96], F32)
    gfm_s = sb("gfm_s", [128, 4 * 4 * 16], F32)
    modv = sb("modv", [128, 96], F32)
    Acoef = sb("Acoef", [128, 2, 16], F32)
    Gfm = sb("Gfm", [128, 2, 16], F32)
    Gbc = sb("Gbc", [128, 2, D], F32)
    diag = sb("diag", [128, 2, 128], F32)
    stat = sb("stat", [128, 64], F32)
    sq_junk = sb("sq_junk", [128, 512], BF16)
    xn_b = sb("xn_b", [128, 1, D], BF16)
    tmp_f = sb("tmp_f", [128, 2, 512], F32)

    n_dma = [0]

    def dma(eng, out, in_, reads=(), writes=(), pool=None):
        if pool is None:
            pool = 'ld' if eng == 'sp' else 'wq'
        k = n_dma[0]
        n_dma[0] += 1
        name = "%s%d" % (pool, k % 6)
        S.op(eng, lambda e: e.dma_start(out=out, in_=in_), reads=reads, writes=writes, dma=name)

    for tt in range(8):
        dma('sp', xres[:, tt, :], x_in[tt * 128:(tt + 1) * 128, :], writes=[('x', tt)])
    dma('sp', ident_f[:], ident_d[:, :], writes=['ident_f'])
    dma('sp', cv_f[:], cvec[:, :], writes=['cv_f'])
    dma('sp', bmod_s[:], bmod[:, :], writes=['bmod_s'])
    dma('sp', gfm_s[:], gfm[:, :], writes=['gfm_s'])
    S.op('dve', lambda e: e.tensor_copy(out=ident_b[:], in_=ident_f[:]), reads=['ident_f'], writes=['ident_b'])
    S.op('dve', lambda e: e.memset(ones_f[:], 1.0), writes=['ones_f'])
    S.op('act', lambda e: e.activation(out=cv_b[:], in_=cv_f[:], func=AF.Silu), reads=['cv_f'], writes=['cv_b'])

    wctr = [0]

    def modulation(l, wst):
        wblk = [sb("wm%d" % i, [128, 16, 512], BF16, wst) for i in range(3)]
        b = next_bank()
        for nb in range(24):
            w = wblk[nb % 3]
            wk = ('wm', nb % 3)
            src = w_mod[l * D:(l + 1) * D, nb * 512:(nb + 1) * 512].rearrange("(c p) n -> p c n", p=128)
            dma('pool', w[:], src, writes=[wk])
            for j in range(4):
                col = nb * 4 + j
                for kc in range(16):
                    S.op('pe', lambda e, w=w, j=j, kc=kc, col=col, b=b: e.matmul(
                        banks[b][:, col:col + 1], lhsT=w[:, kc, j * 128:(j + 1) * 128], rhs=cv_b[:, kc:kc + 1],
                        start=(kc == 0), stop=(kc == 15)), reads=[wk, 'cv_b'], writes=[bk(b)])
        S.op('dve', lambda e, b=b: e.tensor_tensor(out=modv[:], in0=banks[b][:, 0:96], in1=bmod_s[:, l * 96:(l + 1) * 96],
                                                    op=ALU.add), reads=[bk(b), 'bmod_s'], writes=['modv'])
        g = lambda i: gfm_s[:, (l * 4 + i) * 16:(l * 4 + i + 1) * 16]
        for half in range(2):
            sc = modv[:, (3 * half + 1) * 16:(3 * half + 2) * 16]
            gt = modv[:, (3 * half + 2) * 16:(3 * half + 3) * 16]
            S.op('dve', lambda e, half=half, sc=sc: e.scalar_tensor_tensor(
                out=Acoef[:, half, :], in0=sc, scalar=1.0, in1=g(2 * half), op0=ALU.add, op1=ALU.mult),
                reads=['modv', 'gfm_s'], writes=[('Acoef', half)])
            S.op('dve', lambda e, half=half, gt=gt: e.tensor_tensor(
                out=Gfm[:, half, :], in0=gt, in1=g(2 * half + 1), op=ALU.mult),
                reads=['modv', 'gfm_s'], writes=[('Gfm', half)])
            for c4 in range(4):
                b2 = next_bank()
                for cc in range(4):
                    c = c4 * 4 + cc
                    S.op('dve', lambda e, c=c, half=half: e.tensor_scalar(
                        out=diag[:, c % 2, :], in0=ident_f[:], scalar1=Gfm[:, half, c:c + 1], scalar2=None, op0=ALU.mult),
                        reads=['ident_f', ('Gfm', half)], writes=[('diag', c % 2)])
                    S.op('pe', lambda e, c=c, cc=cc, b2=b2: e.matmul(
                        banks[b2][:, cc * 128:(cc + 1) * 128], lhsT=ones_f[:], rhs=diag[:, c % 2, :], start=True, stop=True),
                        reads=['ones_f', ('diag', c % 2)], writes=[bk(b2)])
                S.op('act', lambda e, c4=c4, half=half, b2=b2: e.copy(
                    out=Gbc[:, half, c4 * 512:(c4 + 1) * 512], in_=banks[b2][:]),
                    reads=[bk(b2)], writes=[('Gbc', half)])

    def adaln_in(half, tiles, hT, hkey, col0=0):
        shift = modv[:, (3 * half) * 16:(3 * half + 1) * 16]
        for i, tt in enumerate(tiles):
            s = tt % 2
            S.op('act', lambda e, tt=tt, s=s: e.activation(out=xn_b[:, 0, :], in_=xres[:, tt, :], func=AF.Square,
                                                            accum_out=stat[:, s:s + 1]),
                 reads=[('x', tt)], writes=[('xn_b', 0), ('stat', s)])
            S.op('dve', lambda e, s=s: e.tensor_scalar(out=stat[:, 2 + s:3 + s], in0=stat[:, s:s + 1], scalar1=1.0 / D,
                                                        scalar2=EPS, op0=ALU.mult, op1=ALU.add),
                 reads=[('stat', s)], writes=[('stat', 2 + s)])
            S.op('act', lambda e, s=s: e.activation(out=stat[:, 6 + s:7 + s], in_=stat[:, 2 + s:3 + s], func=AF.Ln),
                 reads=[('stat', 2 + s)], writes=[('stat', 6 + s)])
            S.op('act', lambda e, s=s: e.activation(out=stat[:, 4 + s:5 + s], in_=stat[:, 6 + s:7 + s], func=AF.Exp, scale=-0.5),
                 reads=[('stat', 6 + s)], writes=[('stat', 4 + s)])
            S.op('dve', lambda e, tt=tt, s=s: e.tensor_scalar(out=xn_b[:, 0, :], in0=xres[:, tt, :],
                                                               scalar1=stat[:, 4 + s:5 + s], scalar2=None, op0=ALU.mult),
                 reads=[('x', tt), ('stat', 4 + s)], writes=[('xn_b', 0)])
            for c8 in range(2):
                b = next_bank()
                pb = banks[b][:].bitcast(BF16)
                for cc in range(8):
                    c = c8 * 8 + cc
                    S.op('pe', lambda e, c=c, cc=cc, s=s, pb=pb: e.transpose(
                        out=pb[:, cc * 128:(cc + 1) * 128], in_=xn_b[:, 0, c * 128:(c + 1) * 128], identity=ident_b[:]),
                        reads=[('xn_b', 0), 'ident_b'], writes=[bk(b)])
                for cc in range(8):
                    c = c8 * 8 + cc
                    dst = hT[:, c, col0 + i * 128:col0 + (i + 1) * 128]
                    if True:
                        S.op('act', lambda e, c=c, cc=cc, pb=pb, dst=dst, half=half: e.activation(
                            out=dst, in_=pb[:, cc * 128:(cc + 1) * 128], func=AF.Identity,
                            bias=shift[:, c:c + 1], scale=Acoef[:, half, c:c + 1]),
                            reads=[bk(b), ('Acoef', half), 'modv'], writes=[(hkey, c)])
                    else:
                        S.op('dve', lambda e, c=c, cc=cc, pb=pb, dst=dst, half=half: e.tensor_scalar(
                            out=dst, in0=pb[:, cc * 128:(cc + 1) * 128], scalar1=Acoef[:, half, c:c + 1],
                            scalar2=shift[:, c:c + 1], op0=ALU.mult, op1=ALU.add),
                            reads=[bk(b), ('Acoef', half), 'modv'], writes=[(hkey, c)])

    def post_norm(half, tt, yb):
        for q in range(4):
            S.op('act', lambda e, q=q: e.activation(out=sq_junk[:, :], in_=banks[yb[q]][:],
                                                     func=AF.Square, accum_out=stat[:, 8 + q:9 + q]),
                 reads=[bk(yb[q])], writes=['sq_junk', ('stat', 8 + q)])
        S.op('dve', lambda e: e.tensor_reduce(out=stat[:, 12:13], in_=stat[:, 8:12], axis=mybir.AxisListType.X, op=ALU.add),
             reads=[('stat', 8 + q) for q in range(4)], writes=[('stat', 12)])
        S.op('dve', lambda e: e.tensor_scalar(out=stat[:, 13:14], in0=stat[:, 12:13], scalar1=1.0 / D, scalar2=EPS,
                                               op0=ALU.mult, op1=ALU.add), reads=[('stat', 12)], writes=[('stat', 13)])
        S.op('act', lambda e: e.activation(out=stat[:, 15:16], in_=stat[:, 13:14], func=AF.Ln),
             reads=[('stat', 13)], writes=[('stat', 15)])
        S.op('act', lambda e: e.activation(out=stat[:, 14:15], in_=stat[:, 15:16], func=AF.Exp, scale=-0.5),
             reads=[('stat', 15)], writes=[('stat', 14)])
        for q in range(4):
            s = q % 2
            S.op('dve', lambda e, q=q, s=s: e.scalar_tensor_tensor(
                out=tmp_f[:, s, :], in0=banks[yb[q]][:], scalar=stat[:, 14:15], in1=Gbc[:, half, q * 512:(q + 1) * 512],
                op0=ALU.mult, op1=ALU.mult), reads=[bk(yb[q]), ('stat', 14), ('Gbc', half)], writes=[('tmp_f', s)])
            S.op('dve', lambda e, q=q, s=s, tt=tt: e.tensor_tensor(
                out=xres[:, tt, q * 512:(q + 1) * 512], in0=xres[:, tt, q * 512:(q + 1) * 512], in1=tmp_f[:, s, :],
                op=ALU.add), reads=[('x', tt), ('tmp_f', s)], writes=[('x', tt)])

    def mlp(l):
        with ExitStack() as st:
            hTq = sb("hTq", [128, 16, 256], BF16, st)
            uTq = sb("uTq", [128, 64, 256], BF16, st)
            w1 = [sb("w1_%d" % i, [128, 16, 512], BF16, st) for i in range(3)]
            w2 = [sb("w2_%d" % i, [128, D], BF16, st) for i in range(5)]
            rl = sb("rl", [128, 2, 256], F32, st)
            for q in range(4):
                adaln_in(1, [2 * q, 2 * q + 1], hTq, 'hTq')
                for nb in range(16):
                    w = w1[nb % 3]
                    wk = ('w1', nb % 3)
                    src = w_ff1[l * D:(l + 1) * D, nb * 512:(nb + 1) * 512].rearrange("(c p) n -> p c n", p=128)
                    dma('pool', w[:], src, writes=[wk])
                    for j in range(4):
                        b = next_bank()
                        fc = nb * 4 + j
                        for kc in range(16):
                            S.op('pe', lambda e, w=w, j=j, kc=kc, b=b: e.matmul(
                                banks[b][:, 0:256], lhsT=w[:, kc, j * 128:(j + 1) * 128], rhs=hTq[:, kc, :],
                                start=(kc == 0), stop=(kc == 15)), reads=[wk, ('hTq', kc)], writes=[bk(b)])
                        s = fc % 2
                        S.op('act', lambda e, b=b, s=s: e.activation(out=rl[:, s, :], in_=banks[b][:, 0:256], func=AF.Relu),
                             reads=[bk(b)], writes=[('rl', s)])
                        S.op('dve', lambda e, s=s, fc=fc: e.tensor_tensor(out=uTq[:, fc, :], in0=rl[:, s, :], in1=rl[:, s, :],
                                                                           op=ALU.mult),
                             reads=[('rl', s)], writes=[('uTq', fc)])
                yb = [[next_bank() for _ in range(4)] for _ in range(2)]
                for kc in range(64):
                    w = w2[kc % 5]
                    wk = ("w2", kc % 5)
                    dma('pool', w[:], w_ff2[l * DFF + kc * 128:l * DFF + (kc + 1) * 128, :], writes=[wk])
                    for t in range(2):
                        for nq in range(4):
                            S.op('pe', lambda e, w=w, kc=kc, t=t, nq=nq: e.matmul(
                                banks[yb[t][nq]][:], lhsT=uTq[:, kc, t * 128:(t + 1) * 128], rhs=w[:, nq * 512:(nq + 1) * 512],
                                start=(kc == 0), stop=(kc == 63)), reads=[wk, ('uTq', kc)], writes=[bk(yb[t][nq])])
                for t in range(2):
                    post_norm(1, 2 * q + t, yb[t])
        S.barrier()

    def mixer_a(l):
        lam_init = 0.8 - 0.6 * math.exp(-0.3 * l)
        with ExitStack() as st:
            oT = sb("oT", [128, 16, NT], BF16, st)
            with ExitStack() as st2:
                lam_s = sb("lam_s", [128, 4, 16], F32, st2)
                gsub = sb("gsub", [128, 128], F32, st2)
                st_lam = ExitStack()
                lam_t = sb("lam_t", [128, 4096], F32, st_lam)
                lam_p = sb("lam_p", [128, 2, 1024], F32, st_lam)
                dma('sp', lam_t[:], a_lam[:, :], writes=['lam_t'])
                dma('sp', gsub[:], a_sub[:, :], writes=['gsub0'])
                for i in range(2):
                    S.op('dve', lambda e, i=i: e.tensor_tensor(out=lam_p[:, i, :], in0=lam_t[:, (2 * i) * 1024:(2 * i + 1) * 1024],
                                                                in1=lam_t[:, (2 * i + 1) * 1024:(2 * i + 2) * 1024], op=ALU.mult),
                         reads=['lam_t'], writes=[('lam_p', i)])
                    S.op('dve', lambda e, i=i: e.tensor_reduce(out=lam_s[:, i, :], in_=lam_p[:, i, :].rearrange("p (h d) -> p h d", d=64),
                                                                axis=mybir.AxisListType.X, op=ALU.add),
                         reads=[('lam_p', i)], writes=[('lam_s', i)])
                    S.op('act', lambda e, i=i: e.activation(out=lam_s[:, i, :], in_=lam_s[:, i, :], func=AF.Exp),
                         reads=[('lam_s', i)], writes=[('lam_s', i)])
                S.op('dve', lambda e: e.tensor_tensor(out=lam_s[:, 2, :], in0=lam_s[:, 0, :], in1=lam_s[:, 1, :], op=ALU.subtract),
                     reads=[('lam_s', 0), ('lam_s', 1)], writes=[('lam_s', 2)])
                S.op('dve', lambda e: e.tensor_scalar(out=lam_s[:, 3, :], in0=lam_s[:, 2, :], scalar1=lam_init, scalar2=-1.0,
                                                       op0=ALU.add, op1=ALU.mult), reads=[('lam_s', 2)], writes=[('lam_s', 3)])
                S.op('dve', lambda e: e.tensor_scalar(out=gsub[:], in0=gsub[:], scalar1=1.0 - lam_init, scalar2=None, op0=ALU.mult),
                     reads=['gsub0'], writes=['gsub'])
                S.barrier()
                st_lam.close()
                hT = sb("hT", [128, 16, NT], BF16, st2)
                wh = [sb("wh%d" % i, [128, 16, 3, 128], BF16, st2) for i in range(1)]
                cosq = sb("cosq", [128, NT], F32, st2)
                sinq = sb("sinq", [128, NT], F32, st2)
                cosk = sb("cosk", [128, NT], F32, st2)
                sink_ = sb("sink", [128, NT], F32, st2)
                rot_f = sb("rot_f", [128, 128], F32, st2)
                rot_b = sb("rot_b", [128, 128], BF16, st2)
                maskA = sb("maskA", [128, 48], F32, st2)
                QT = sb("QT", [128, NT], BF16, st2)
                KT = sb("KT", [128, NT + 512], BF16, st2)
                raw = sb("raw", [128, 1, 512], BF16, st2)
                t1 = sb("t1", [128, 1, 512], F32, st2)
                t2 = sb("t2", [128, 1, 512], F32, st2)
                Vaug = sb("Vaug", [128, 12, 132], BF16, st2)
                kvo = sb("kvo", [128, 2, 2, 128], F32, st2)
                cks = sb("cks", [128, 4, 128], BF16, st2)
                Et = sb("Et", [128, 3, 256], BF16, st2)
                osb = sb("osb", [128, 2, 128], F32, st2)
                onb = sb("onb", [128, 2, 128], BF16, st2)
                sst = sb("sst", [128, 16], F32, st2)

                tabk = ['cosq', 'sinq', 'cosk', 'sink']
                for t_, d_, k_ in ((cosq, a_cosq, 'cosq'), (sinq, a_sinq, 'sinq'), (cosk, a_cosk, 'cosk'), (sink_, a_sink, 'sink')):
                    dma('sp', t_[:], d_[:, :], writes=[k_])
                dma('sp', rot_f[:], rotm[:, :], writes=['rot_f'])
                dma('sp', maskA[:], a_mask[:, :], writes=['maskA'])
                S.op('dve', lambda e: e.tensor_copy(out=rot_b[:], in_=rot_f[:]), reads=['rot_f'], writes=['rot_b'])
                S.op('dve', lambda e: e.memset(Vaug[:, :, 128:132], 1.0), writes=['Vaug_ones'])

                adaln_in(0, list(range(8)), hT, 'hT')

                def proj_fm_rope(w, wk, slot, dst, ct, st_, ck_, sk_):
                    for hf in range(2):
                        b = next_bank()
                        for kc in range(16):
                            S.op('pe', lambda e, kc=kc, b=b, hf=hf: e.matmul(
                                banks[b][:], lhsT=w[:, kc, slot, :], rhs=hT[:, kc, hf * 512:(hf + 1) * 512],
                                start=(kc == 0), stop=(kc == 15)), reads=[wk, ('hT', kc)], writes=[bk(b)])
                        S.op('act', lambda e, b=b, hf=hf: e.copy(out=raw[:, 0, :], in_=banks[b][:]),
                             reads=[bk(b)], writes=[('raw', 0)])
                        b2 = next_bank()
                        S.op('pe', lambda e, b2=b2, hf=hf: e.matmul(banks[b2][:], lhsT=rot_b[:], rhs=raw[:, 0, :],
                                                                    start=True, stop=True),
                             reads=['rot_b', ('raw', 0)], writes=[bk(b2)])
                        S.op('dve', lambda e, b=b, hf=hf: e.tensor_tensor(out=t1[:, 0, :], in0=banks[b][:],
                                                                           in1=ct[:, hf * 512:(hf + 1) * 512], op=ALU.mult),
                             reads=[bk(b), ck_], writes=[('t1', 0)])
                        S.op('dve', lambda e, b2=b2, hf=hf: e.tensor_tensor(out=t2[:, 0, :], in0=banks[b2][:],
                                                                             in1=st_[:, hf * 512:(hf + 1) * 512], op=ALU.mult),
                             reads=[bk(b2), sk_], writes=[('t2', 0)])
                        S.op('dve', lambda e, hf=hf: e.tensor_tensor(out=dst[:, hf * 512:(hf + 1) * 512], in0=t1[:, 0, :],
                                                                      in1=t2[:, 0, :], op=ALU.add),
                             reads=[('t1', 0), ('t2', 0)], writes=[('dstrope', id(dst))])

                for h in range(16):
                    w = wh[0]
                    wk = ('wh', 0)
                    for sl in range(3):
                        src = a_w_in[:, sl * D + h * 128:sl * D + (h + 1) * 128].rearrange("(c p) n -> p c n", p=128)
                        dma('pool', w[:, :, sl, :], src, writes=[wk])
                    proj_fm_rope(w, wk, 0, QT, cosq, sinq, tabk[0], tabk[1])
                    proj_fm_rope(w, wk, 1, KT, cosk, sink_, tabk[2], tabk[3])
                    for tt in range(8):
                        b = next_bank()
                        for kc in range(16):
                            S.op('pe', lambda e, kc=kc, b=b, tt=tt: e.matmul(
                                banks[b][:, 0:256], lhsT=hT[:, kc, tt * 128:(tt + 1) * 128],
                                rhs=w[:, kc, 1:3, :].rearrange("p a b -> p (a b)"),
                                start=(kc == 0), stop=(kc == 15)), reads=[wk, ('hT', kc)], writes=[bk(b)])
                        S.op('act', lambda e, b=b, tt=tt: e.copy(out=kvo[:, :, tt % 2, :],
                                                                in_=banks[b][:, 0:256].rearrange("p (a b) -> p a b", a=2)),
                             reads=[bk(b)], writes=[('kvo', tt % 2)])
                        S.op('dve', lambda e, b=b, tt=tt: e.tensor_copy(out=Vaug[:, tt, 0:128], in_=banks[b][:, 128:256]),
                             reads=[bk(b)], writes=[('Vaug', tt)])
                        dma('sp', a_ko[tt * 128:(tt + 1) * 128, h * 128:(h + 1) * 128], kvo[:, 0, tt % 2, :],
                            reads=[('kvo', tt % 2)], pool='st')
                        dma('sp', a_vo[tt * 128:(tt + 1) * 128, h * 128:(h + 1) * 128], kvo[:, 1, tt % 2, :],
                            reads=[('kvo', tt % 2)], pool='st')
                    dma('pool', cks[:], a_ck[:, h * 128:(h + 1) * 128].rearrange("(t p) d -> p t d", p=128), writes=['cks'])
                    dma('pool', Vaug[:, 8:12, 0:128], a_cv[:, h * 128:(h + 1) * 128].rearrange("(t p) d -> p t d", p=128),
                        writes=[('Vaug', 8 + i) for i in range(4)])
                    b = next_bank()
                    pb = banks[b][:].bitcast(BF16)
                    for i in range(4):
                        S.op('pe', lambda e, i=i, pb=pb: e.transpose(out=pb[:, i * 128:(i + 1) * 128], in_=cks[:, i, :],
                                                                      identity=ident_b[:]),
                             reads=['cks', 'ident_b'], writes=[bk(b)])
                    S.op('act', lambda e, pb=pb: e.copy(out=KT[:, NT:NT + 512], in_=pb[:, 0:512]),
                         reads=[bk(b)], writes=['KTctx'])
                    for qp in range(4):
                        ob = [[next_bank() for _ in range(2)] for _ in range(2)]
                        its = [(j, kc) for j in range(2) for kc in range(12)]
                        sbk = {}

                        def st_score(n, qp=qp, ob=ob, its=its, sbk=sbk):
                            j, kc = its[n]
                            pr = slice(j * 64, (j + 1) * 64)
                            b = next_bank()
                            while any(b in r for r in ob):
                                b = next_bank()
                            sbk[n] = b
                            S.op('pe', lambda e: e.matmul(
                                banks[b][:, 0:256], lhsT=KT[pr, kc * 128:(kc + 1) * 128], rhs=QT[pr, qp * 256:(qp + 1) * 256],
                                start=True, stop=True),
                                reads=[('dstrope', id(KT)), ('dstrope', id(QT)), 'KTctx'], writes=[bk(b)])

                        def st_exp(n, qp=qp, its=its, sbk=sbk):
                            j, kc = its[n]
                            b = sbk[n]
                            es_ = n % 3
                            S.op('act', lambda e: e.activation(
                                out=Et[:, es_, :], in_=banks[b][:, 0:256], func=AF.Exp,
                                bias=maskA[:, kc * 4 + qp:kc * 4 + qp + 1], scale=1.0),
                                reads=[bk(b), 'maskA'], writes=[('Et', es_)])

                        def st_pv(n, ob=ob, its=its):
                            j, kc = its[n]
                            es_ = n % 3
                            for qq in range(2):
                                S.op('pe', lambda e: e.matmul(
                                    banks[ob[j][qq]][:, 0:129], lhsT=Et[:, es_, qq * 128:(qq + 1) * 128], rhs=Vaug[:, kc, 0:129],
                                    start=(kc == 0), stop=(kc == 11)),
                                    reads=[('Et', es_), ('Vaug', kc), 'Vaug_ones'], writes=[bk(ob[j][qq])])

                        LA = 2
                        for t in range(len(its) + LA):
                            if t < len(its):
                                st_score(t)
                            if 1 <= t <= len(its):
                                st_exp(t - 1)
                            if t >= LA:
                                st_pv(t - LA)
                        for qq in range(2):
                            tt = qp * 2 + qq
                            o1, o2 = banks[ob[0][qq]], banks[ob[1][qq]]
                            S.op('dve', lambda e, o1=o1: e.reciprocal(out=sst[:, 0:1], in_=o1[:, 128:129]),
                                 reads=[bk(ob[0][qq])], writes=[('sst', 0)])
                            S.op('dve', lambda e, o2=o2: e.reciprocal(out=sst[:, 1:2], in_=o2[:, 128:129]),
                                 reads=[bk(ob[1][qq])], writes=[('sst', 1)])
                            S.op('dve', lambda e, h=h: e.tensor_tensor(out=sst[:, 2:3], in0=sst[:, 1:2], in1=lam_s[:, 3, h:h + 1],
                                                                        op=ALU.mult),
                                 reads=[('sst', 1), ('lam_s', 3)], writes=[('sst', 2)])
                            S.op('dve', lambda e, o1=o1, qq=qq: e.tensor_scalar(out=osb[:, qq, :], in0=o1[:, 0:128], scalar1=sst[:, 0:1],
                                                                                 scalar2=None, op0=ALU.mult),
                                 reads=[bk(ob[0][qq]), ('sst', 0)], writes=[('osb', qq)])
                            S.op('dve', lambda e, o2=o2, qq=qq: e.scalar_tensor_tensor(
                                out=osb[:, qq, :], in0=o2[:, 0:128], scalar=sst[:, 2:3], in1=osb[:, qq, :], op0=ALU.mult, op1=ALU.add),
                                reads=[bk(ob[1][qq]), ('sst', 2), ('osb', qq)], writes=[('osb', qq)])
                            S.op('act', lambda e, qq=qq: e.activation(out=sq_junk[:, 0:128], in_=osb[:, qq, :], func=AF.Square,
                                                                       accum_out=sst[:, 3:4]),
                                 reads=[('osb', qq)], writes=['sq_junk', ('sst', 3)])
                            S.op('dve', lambda e: e.tensor_scalar(out=sst[:, 4:5], in0=sst[:, 3:4], scalar1=1.0 / 128, scalar2=EPS,
                                                                   op0=ALU.mult, op1=ALU.add), reads=[('sst', 3)], writes=[('sst', 4)])
                            S.op('act', lambda e: e.activation(out=sst[:, 6:7], in_=sst[:, 4:5], func=AF.Ln),
                                 reads=[('sst', 4)], writes=[('sst', 6)])
                            S.op('act', lambda e: e.activation(out=sst[:, 5:6], in_=sst[:, 6:7], func=AF.Exp, scale=-0.5),
                                 reads=[('sst', 6)], writes=[('sst', 5)])
                            S.op('dve', lambda e, qq=qq: e.scalar_tensor_tensor(
                                out=onb[:, qq, :], in0=osb[:, qq, :], scalar=sst[:, 5:6], in1=gsub[:], op0=ALU.mult, op1=ALU.mult),
                                reads=[('osb', qq), ('sst', 5), 'gsub'], writes=[('onb', qq)])
                            b = next_bank()
                            pb = banks[b][:].bitcast(BF16)
                            S.op('pe', lambda e, qq=qq, pb=pb: e.transpose(out=pb[:, 0:128], in_=onb[:, qq, :], identity=ident_b[:]),
                                 reads=[('onb', qq), 'ident_b'], writes=[bk(b)])
                            S.op('act', lambda e, pb=pb, h=h, tt=tt: e.copy(out=oT[:, h, tt * 128:(tt + 1) * 128], in_=pb[:, 0:128]),
                                 reads=[bk(b)], writes=[('oT', h)])
            S.barrier()
            out_proj(a_w_out, oT, st)
        S.barrier()

    NA_L = [[0, 1, 2, 3], [0, 1, 2, 3], [0, 1, 2, 3, 4], [1, 2, 3, 4, 5], [2, 3, 4, 5, 6], [3, 4, 5, 6, 7], [4, 5, 6, 7], [4, 5, 6, 7]]

    def mixer_cd(l, kind):
        isd = (kind == 'd')
        w_in_d, w_out_d = (d_w_in, d_w_out) if isd else (c_w_in, c_w_out)
        ck_d, cv_d = (d_ck, d_cv) if isd else (c_ck, c_cv)
        k_out, v_out = (d_ko, d_vo) if isd else (c_ko, c_vo)
        dv = 64 if isd else 128
        nslot = 4 if isd else 3
        with ExitStack() as st:
            oT = sb("oT", [128, 16, NT], BF16, st)
            with ExitStack() as st2:
                hT = sb("hT", [128, 16, NT], BF16, st2)
                w = sb("wcd", [128, 16, nslot, 128], BF16, st2)
                wk = 'wcd'
                QT = sb("QT", [128, 2 if isd else 1, NT], BF16, st2)
                KT = sb("KT", [128, NT + 512], BF16, st2)
                Vaug = sb("Vaug", [128, 12, dv + 4], BF16, st2)
                kvo = sb("kvo", [128, 2, 2, dv], F32, st2)
                cks = sb("cks", [128, 4, 128], BF16, st2)
                Et = sb("Et", [128, 4, 128], BF16, st2)
                sbs = sb("sbs", [128, 4, 128], F32, st2)
                onb = sb("onb", [128, 2, 256 if isd else 128], BF16, st2)
                sst = sb("sst", [128, 16], F32, st2)
                ctxb = sb("ctxb_s", [128, 1], F32, st2)
                dma('sp', ctxb[:], ctxb_d[:, :], writes=['ctxb'])
                if isd:
                    cosk = sb("cosk", [128, NT], F32, st2)
                    sink_ = sb("sink", [128, NT], F32, st2)
                    rot_f = sb("rot_f", [128, 128], F32, st2)
                    rot_b = sb("rot_b", [128, 128], BF16, st2)
                    raw = sb("raw", [128, 512], BF16, st2)
                    maskD = sb("maskD", [128, 2, 8, 128], F32, st2)
                    snk = sb("snk", [128, 32], F32, st2)
                    dma('sp', cosk[:], a_cosk[:, :], writes=['cosk'])
                    dma('sp', sink_[:], a_sink[:, :], writes=['sink'])
                    dma('sp', rot_f[:], rotm[:, :], writes=['rot_f'])
                    dma('sp', maskD[:], d_mask[:, :].rearrange("p (a b c) -> p a b c", a=2, b=8), writes=['maskD'])
                    dma('sp', snk[:], d_snk[:, :], writes=['snk0'])
                    S.op('dve', lambda e: e.tensor_copy(out=rot_b[:], in_=rot_f[:]), reads=['rot_f'], writes=['rot_b'])
                    S.op('act', lambda e: e.activation(out=snk[:], in_=snk[:], func=AF.Exp), reads=['snk0'], writes=['snk'])
                else:
                    biasC = sb("biasC", [128, 2, 5, 128], F32, st2)
                S.op('dve', lambda e: e.memset(Vaug[:, :, dv:dv + 4], 1.0), writes=['Vaug_ones'])
                adaln_in(0, list(range(8)), hT, 'hT')

                def proj_fm(slot, dst, dkey, rope, scale):
                    for hf in range(2):
                        b = next_bank()
                        for kc in range(16):
                            S.op('pe', lambda e, kc=kc, b=b, hf=hf: e.matmul(
                                banks[b][:], lhsT=w[:, kc, slot, :], rhs=hT[:, kc, hf * 512:(hf + 1) * 512],
                                start=(kc == 0), stop=(kc == 15)), reads=[wk, ('hT', kc)], writes=[bk(b)])
                        dsl = dst[:, hf * 512:(hf + 1) * 512]
                        if not rope:
                            S.op('act', lambda e, b=b: e.mul(out=dsl, in_=banks[b][:], mul=scale), reads=[bk(b)], writes=[dkey])
                            continue
                        S.op('act', lambda e, b=b: e.copy(out=raw[:], in_=banks[b][:]), reads=[bk(b)], writes=['raw'])
                        b2 = next_bank()
                        S.op('pe', lambda e, b2=b2: e.matmul(banks[b2][:], lhsT=rot_b[:], rhs=raw[:], start=True, stop=True),
                             reads=['rot_b', 'raw'], writes=[bk(b2)])
                        S.op('dve', lambda e, b=b, hf=hf: e.tensor_tensor(out=tmp_f[:, 0, :], in0=banks[b][:],
                                                                           in1=cosk[:, hf * 512:(hf + 1) * 512], op=ALU.mult),
                             reads=[bk(b), 'cosk'], writes=[('tmp_f', 0)])
                        S.op('dve', lambda e, b2=b2, hf=hf: e.tensor_tensor(out=tmp_f[:, 1, :], in0=banks[b2][:],
                                                                             in1=sink_[:, hf * 512:(hf + 1) * 512], op=ALU.mult),
                             reads=[bk(b2), 'sink'], writes=[('tmp_f', 1)])
                        S.op('dve', lambda e: e.tensor_tensor(out=dsl, in0=tmp_f[:, 0, :], in1=tmp_f[:, 1, :], op=ALU.add),
                             reads=[('tmp_f', 0), ('tmp_f', 1)], writes=[dkey])

                ngrp = 8 if isd else 16
                for g in range(ngrp):
                    rs = lambda a: a.rearrange("(c p) n -> p c n", p=128)
                    if isd:
                        for ci in range(2):
                            dma('pool', w[:, :, ci, :], rs(w_in_d[:, g * 256 + ci * 128:g * 256 + (ci + 1) * 128]), writes=[wk])
                        kcols = rs(w_in_d[:, 2048 + g * 64:2048 + (g + 1) * 64])
                        vcols = rs(w_in_d[:, 2560 + g * 64:2560 + (g + 1) * 64])
                        dma('pool', w[:, :, 2, 0:64], kcols, writes=[wk])
                        dma('pool', w[:, :, 2, 64:128], vcols, writes=[wk])
                        dma('pool', w[:, :, 3, 0:64], kcols, writes=[wk])
                        dma('pool', w[:, :, 3, 64:128], kcols, writes=[wk])
                        proj_fm(0, QT[:, 0, :], 'QT', True, 1.0)
                        proj_fm(1, QT[:, 1, :], 'QT', True, 1.0)
                        proj_fm(3, KT[:, 0:NT], 'KT', True, 1.0)
                        kvslot = w[:, :, 2, :]
                        nkv = 128
                    else:
                        for sl in range(3):
                            dma('pool', w[:, :, sl, :], rs(w_in_d[:, sl * D + g * 128:sl * D + (g + 1) * 128]), writes=[wk])
                        proj_fm(0, QT[:, 0, :], 'QT', False, 128 ** -0.5)
                        proj_fm(1, KT[:, 0:NT], 'KT', False, 1.0)
                        kvslot = w[:, :, 1:3, :].rearrange("p c a b -> p c (a b)")
                        nkv = 256
                    for tt in range(8):
                        b = next_bank()
                        for kc in range(16):
                            S.op('pe', lambda e, kc=kc, b=b, tt=tt: e.matmul(
                                banks[b][:, 0:nkv], lhsT=hT[:, kc, tt * 128:(tt + 1) * 128], rhs=kvslot[:, kc, :],
                                start=(kc == 0), stop=(kc == 15)), reads=[wk, ('hT', kc)], writes=[bk(b)])
                        S.op('act', lambda e, b=b, tt=tt: e.copy(out=kvo[:, :, tt % 2, :],
                                                                in_=banks[b][:, 0:nkv].rearrange("p (a b) -> p a b", a=2)),
                             reads=[bk(b)], writes=[('kvo', tt % 2)])
                        S.op('dve', lambda e, b=b, tt=tt: e.tensor_copy(out=Vaug[:, tt, 0:dv], in_=banks[b][:, dv:2 * dv]),
                             reads=[bk(b)], writes=[('Vaug', tt)])
                        dma('sp', k_out[tt * 128:(tt + 1) * 128, g * dv:(g + 1) * dv], kvo[:, 0, tt % 2, :],
                            reads=[('kvo', tt % 2)], pool='st')
                        dma('sp', v_out[tt * 128:(tt + 1) * 128, g * dv:(g + 1) * dv], kvo[:, 1, tt % 2, :],
                            reads=[('kvo', tt % 2)], pool='st')
                    csrc = ck_d[:, g * dv:(g + 1) * dv].rearrange("(t p) d -> p t d", p=128)
                    if isd:
                        dma('pool', cks[:, :, 0:64], csrc, writes=['cks'])
                        dma('pool', cks[:, :, 64:128], csrc, writes=['cks'])
                    else:
                        dma('pool', cks[:], csrc, writes=['cks'])
                    dma('pool', Vaug[:, 8:12, 0:dv], cv_d[:, g * dv:(g + 1) * dv].rearrange("(t p) d -> p t d", p=128),
                        writes=[('Vaug', 8 + i) for i in range(4)])
                    b = next_bank()
                    pb = banks[b][:].bitcast(BF16)
                    for i in range(4):
                        S.op('pe', lambda e, i=i, pb=pb: e.transpose(out=pb[:, i * 128:(i + 1) * 128], in_=cks[:, i, :],
                                                                      identity=ident_b[:]),
                             reads=['cks', 'ident_b'], writes=[bk(b)])
                    S.op('act', lambda e, pb=pb: e.copy(out=KT[:, NT:NT + 512], in_=pb[:, 0:512]), reads=[bk(b)], writes=['KTctx'])
                    items = []
                    for qt in range(8):
                        if isd:
                            chunks = [(qt + r, ri) for r, ri in ((-1, 0), (0, None), (1, 1)) if 0 <= qt + r < 8]
                        else:
                            chunks = [(kc, si) for si, kc in enumerate(NA_L[qt])]
                        chunks += [(8 + i, 'ctx') for i in range(4)]
                        for gi in range(4 if isd else 1):
                            for n, (kc, mk) in enumerate(chunks):
                                items.append((qt, gi, n, kc, mk, len(chunks)))
                    sbk = {}
                    obk = {}
                    live_ob = set()
                    esc = 0.125 if isd else 1.0
                    ngi = 4 if isd else 1

                    def prq(gi):
                        return slice((gi % 2) * 64, (gi % 2 + 1) * 64) if isd else slice(0, 128)

                    def st_score(t):
                        qt, gi, n, kc, mk, nch = items[t]
                        if n == 0:
                            if gi == 0 and not isd:
                                bslot = (g * 8 + qt) % 2
                                dma('sp', biasC[:, bslot, :, :], c_bias[(g * 8 + qt) * 128:(g * 8 + qt + 1) * 128, :].rearrange(
                                    "p (a b) -> p a b", a=5), writes=[('biasC', bslot)])
                            ob = next_bank()
                            while ob in live_ob:
                                ob = next_bank()
                            obk[(qt, gi)] = ob
                            live_ob.add(ob)
                        b = next_bank()
                        while b in live_ob:
                            b = next_bank()
                        sbk[t] = b
                        pr = prq(gi)
                        qsrc = QT[pr, gi // 2, qt * 128:(qt + 1) * 128]
                        S.op('pe', lambda e: e.matmul(
                            banks[b][:, 0:128], lhsT=KT[pr, kc * 128:(kc + 1) * 128], rhs=qsrc, start=True, stop=True),
                            reads=['KT', 'QT', 'KTctx'], writes=[bk(b)])

                    def st_exp(t):
                        qt, gi, n, kc, mk, nch = items[t]
                        b = sbk[t]
                        es_ = t % 4
                        if mk == 'ctx':
                            S.op('act', lambda e: e.activation(out=Et[:, es_, :], in_=banks[b][:, 0:128], func=AF.Exp,
                                                               bias=ctxb[:, 0:1], scale=esc),
                                 reads=[bk(b), 'ctxb'], writes=[('Et', es_)])
                        elif mk is None:
                            S.op('act', lambda e: e.activation(out=Et[:, es_, :], in_=banks[b][:, 0:128], func=AF.Exp, scale=esc),
                                 reads=[bk(b)], writes=[('Et', es_)])
                        else:
                            bslot = (g * 8 + qt) % 2
                            msrc = maskD[:, mk, qt, :] if isd else biasC[:, bslot, mk, :]
                            mkey = 'maskD' if isd else ('biasC', bslot)
                            S.op('dve', lambda e: e.tensor_tensor(out=sbs[:, es_, :], in0=banks[b][:, 0:128], in1=msrc, op=ALU.add),
                                 reads=[bk(b), mkey], writes=[('sbs', es_)])
                            S.op('act', lambda e: e.activation(out=Et[:, es_, :], in_=sbs[:, es_, :], func=AF.Exp, scale=esc),
                                 reads=[('sbs', es_)], writes=[('Et', es_)])

                    def st_pv(t):
                        qt, gi, n, kc, mk, nch = items[t]
                        es_ = t % 4
                        ob = obk[(qt, gi)]
                        S.op('pe', lambda e: e.matmul(
                            banks[ob][:, 0:dv + 1], lhsT=Et[:, es_, :], rhs=Vaug[:, kc, 0:dv + 1],
                            start=(n == 0), stop=(n == nch - 1)),
                            reads=[('Et', es_), ('Vaug', kc), 'Vaug_ones'], writes=[bk(ob)])
                        if n != nch - 1:
                            return
                        os_ = qt % 2
                        if isd:
                            hq = g * 4 + gi
                            S.op('dve', lambda e: e.tensor_tensor(out=sst[:, 0:1], in0=banks[ob][:, dv:dv + 1], in1=snk[:, hq:hq + 1],
                                                                  op=ALU.add), reads=[bk(ob), 'snk'], writes=[('sst', 0)])
                            S.op('dve', lambda e: e.reciprocal(out=sst[:, 1:2], in_=sst[:, 0:1]), reads=[('sst', 0)], writes=[('sst', 1)])
                        else:
                            S.op('dve', lambda e: e.reciprocal(out=sst[:, 1:2], in_=banks[ob][:, dv:dv + 1]),
                                 reads=[bk(ob)], writes=[('sst', 1)])
                        S.op('dve', lambda e: e.tensor_scalar(out=onb[:, os_, gi * dv:(gi + 1) * dv], in0=banks[ob][:, 0:dv],
                                                               scalar1=sst[:, 1:2], scalar2=None, op0=ALU.mult),
                             reads=[bk(ob), ('sst', 1)], writes=[('onb', os_)])
                        live_ob.discard(ob)
                        if gi != ngi - 1:
                            return
                        for ci in range(2 if isd else 1):
                            b = next_bank()
                            pb = banks[b][:].bitcast(BF16)
                            S.op('pe', lambda e: e.transpose(out=pb[:, 0:128], in_=onb[:, os_, ci * 128:(ci + 1) * 128],
                                                             identity=ident_b[:]),
                                 reads=[('onb', os_), 'ident_b'], writes=[bk(b)])
                            och = (2 * g + ci) if isd else g
                            S.op('act', lambda e: e.copy(out=oT[:, och, qt * 128:(qt + 1) * 128], in_=pb[:, 0:128]),
                                 reads=[bk(b)], writes=[('oT', och)])

                    LA = 2
                    for t in range(len(items) + LA):
                        if t < len(items):
                            st_score(t)
                        if 1 <= t <= len(items):
                            st_exp(t - 1)
                        if t >= LA:
                            st_pv(t - LA)
            S.barrier()
            out_proj(w_out_d, oT, st)
        S.barrier()

    def mixer_b(l):
        with ExitStack() as st:
            oT = sb("oT", [128, 16, NT], BF16, st)
            with ExitStack() as st2:
                hT = sb("hT", [128, 16, NT], BF16, st2)
                wt = [sb("wb%d" % i, [128, 16, 256], BF16, st2) for i in range(1)]
                wg = sb("wg", [128, 16, 32], BF16, st2)
                gb = sb("gb", [128, 32], F32, st2)
                nrm = sb("nrm", [128, 256], F32, st2)
                keep = sb("keep", [128, 1], F32, st2)
                m0e = sb("m0e", [128, 16], F32, st2)
                Um = sb("Um", [128, 2, 128], F32, st2)
                Mk = sb("Mk", [128, 2, 128], F32, st2)
                G = sb("G", [128, 256], F32, st2)
                LF = sb("LF", [128, 256], F32, st2)
                BB = tmp_f[:, 1, 0:256]
                AB = sb("AB", [128, 128], F32, st2)
                EE = sb("EE", [128, 128], F32, st2)
                WL = sb("WL", [128, 128], F32, st2)
                DT = sb("DT", [128, 128], F32, st2)
                BT8 = sb("BT8", [8, 16], F32, st2)
                EM8 = sb("EM8", [8, 16], F32, st2)
                M8 = sb("M8", [8, 8], F32, st2)
                T8 = sb("T8", [8, 2], F32, st2)
                SC8 = sb("SC8", [8, 8], F32, st2)
                d8 = sb("d8", [8, 8, 8], F32, st2)
                SCB = sb("SCB", [128, 64], F32, st2)
                QT = sb("QT", [128, NT], BF16, st2)
                KT = sb("KT", [128, NT], BF16, st2)
                Ktm = sb("Ktm", [128, 8, 128], BF16, st2)
                Vaug = sb("Vaug", [128, 8, 260], BF16, st2)
                SIGO = sb("SIGO", [128, 2, 256], F32, st2)
                Hs = sb("Hs", [128, 8, 256], F32, st2)
                S32 = sb("S32", [128, 2, 257], F32, st2)
                Sbf = sb("Sbf", [128, 2, 260], BF16, st2)
                outS = sb("outS", [128, 2, 257], F32, st2)
                lfbc = sb("lfbc", [128, 2, 128], F32, st2)
                tD = sb("tD", [128, 2, 128], F32, st2)
                Dt = sb("Dt", [128, 2, 128], F32, st2)
                EB = sb("EB", [128, 2, 128], F32, st2)
                Qs = sb("Qs", [128, 2, 128], BF16, st2)
                PT = sb("PT", [128, 2, 128], BF16, st2)
                Kw = sb("Kw", [128, 2, 128], BF16, st2)
                rr = sb("rr", [128, 2, 4], F32, st2)
                hn = tmp_f[:, 0, 0:256]
                onb = sb("onb", [128, 2, 256], BF16, st2)
                sst = sb("sst", [128, 8], F32, st2)

                dma('sp', gb[:], b_gb[:, :], writes=['gb'])
                dma('sp', keep[:], b_keep[:, :], writes=['keep'])
                dma('sp', m0e[:], b_m0[:, :], writes=['m0raw'])
                dma('sp', Um[:], b_um[:, :].rearrange("p (a b) -> p a b", a=2), writes=['Um'])
                dma('sp', Mk[:], b_mk[:, :].rearrange("p (a b) -> p a b", a=2), writes=['Mk'])
                dma('pool', wg[:], b_w_in[:, 6144:6176].rearrange("(c p) n -> p c n", p=128), writes=['wg'])
                S.op('act', lambda e: e.activation(out=m0e[:], in_=m0e[:], func=AF.Exp), reads=['m0raw'], writes=['m0e'])
                S.op('dve', lambda e: e.memset(Vaug[:, :, 256:260], 1.0), writes=['Vaug_ones'])
                adaln_in(0, list(range(8)), hT, 'hT')

                bg = next_bank()
                for tt in range(8):
                    for kc in range(16):
                        S.op('pe', lambda e, tt=tt, kc=kc: e.matmul(
                            banks[bg][:, tt * 32:(tt + 1) * 32], lhsT=hT[:, kc, tt * 128:(tt + 1) * 128], rhs=wg[:, kc, :],
                            start=(kc == 0), stop=(kc == 15)), reads=['wg', ('hT', kc)], writes=[bk(bg)])
                for tt in range(8):
                    S.op('dve', lambda e, tt=tt: e.tensor_tensor(out=G[:, tt * 32:(tt + 1) * 32], in0=banks[bg][:, tt * 32:(tt + 1) * 32],
                                                                  in1=gb[:], op=ALU.add), reads=[bk(bg), 'gb'], writes=['G'])
                S.op('act', lambda e: e.activation(out=LF[:], in_=G[:], func=AF.Exp, scale=-1.0), reads=['G'], writes=['LF'])
                S.op('dve', lambda e: e.tensor_scalar(out=LF[:], in0=LF[:], scalar1=1.0, scalar2=None, op0=ALU.add),
                     reads=['LF'], writes=['LF'])
                S.op('act', lambda e: e.activation(out=LF[:], in_=LF[:], func=AF.Ln), reads=['LF'], writes=['LF'])
                S.op('dve', lambda e: e.tensor_scalar(out=LF[:], in0=LF[:], scalar1=-1.0, scalar2=None, op0=ALU.mult),
                     reads=['LF'], writes=['LF'])
                bb_ = next_bank()
                for c in range(8):
                    for d in range(2):
                        cd = c * 2 + d
                        lf_cd = LF[:, c * 32 + (2 * d + 1) * 8:c * 32 + (2 * d + 1) * 8 + 8]
                        S.op('pe', lambda e, cd=cd, d=d, lf_cd=lf_cd: e.matmul(
                            banks[bb_][:, cd * 16:cd * 16 + 8], lhsT=Um[:, d, :], rhs=lf_cd, start=True, stop=True),
                            reads=['Um', 'LF'], writes=[bk(bb_)])
                        S.op('pe', lambda e, cd=cd, lf_cd=lf_cd: e.matmul(
                            banks[bb_][:, cd * 16 + 8:cd * 16 + 16], lhsT=ones_f[:], rhs=lf_cd, start=True, stop=True),
                            reads=['ones_f', 'LF'], writes=[bk(bb_)])
                S.op('act', lambda e: e.copy(out=BB, in_=banks[bb_][:, 0:256]), reads=[bk(bb_)], writes=['BB'])
                for c in range(8):
                    for d in range(2):
                        cd = c * 2 + d
                        ig = G[:, c * 32 + 2 * d * 8:c * 32 + 2 * d * 8 + 8]
                        S.op('dve', lambda e, cd=cd, ig=ig: e.tensor_tensor(out=AB[:, cd * 8:cd * 8 + 8], in0=ig,
                                                                             in1=BB[:, cd * 16:cd * 16 + 8], op=ALU.subtract),
                             reads=['G', 'BB'], writes=['AB'])
                        S.op('dve', lambda e, cd=cd: e.tensor_tensor(out=EE[:, cd * 8:cd * 8 + 8], in0=AB[:, cd * 8:cd * 8 + 8],
                                                                      in1=BB[:, cd * 16 + 8:cd * 16 + 16], op=ALU.add),
                             reads=['AB', 'BB'], writes=['EE'])
                        S.op('act', lambda e, cd=cd: e.activation(out=DT[:, cd * 8:cd * 8 + 8], in_=BB[:, cd * 16 + 8:cd * 16 + 16],
                                                                   func=AF.Exp), reads=['BB'], writes=['DT'])
                S.op('act', lambda e: e.activation(out=WL[:], in_=EE[:], func=AF.Exp), reads=['EE'], writes=['WL'])
                bt_ = next_bank()
                for cd in range(16):
                    c, d = divmod(cd, 2)
                    lf_cd = LF[:, c * 32 + (2 * d + 1) * 8:c * 32 + (2 * d + 1) * 8 + 8]
                    S.op('pe', lambda e, cd=cd, lf_cd=lf_cd: e.matmul(banks[bt_][0:8, cd:cd + 1], lhsT=lf_cd, rhs=ones_f[:, 0:1],
                                                                      start=True, stop=True),
                         reads=['LF', 'ones_f'], writes=[bk(bt_)])
                S.op('act', lambda e: e.copy(out=BT8[:], in_=banks[bt_][0:8, 0:16]), reads=[bk(bt_)], writes=['BT8'])
                for g4 in range(4):
                    be = next_bank()
                    for i in range(4):
                        cd = g4 * 4 + i
                        S.op('pe', lambda e, cd=cd, i=i, be=be: e.matmul(banks[be][0:8, i * 128:(i + 1) * 128],
                                                                        lhsT=EE[:, cd * 8:cd * 8 + 8], rhs=ident_f[:],
                                                                        start=True, stop=True),
                             reads=['EE', 'ident_f'], writes=[bk(be)])
                    S.op('dve', lambda e, g4=g4, be=be: e.tensor_reduce(
                        out=EM8[:, g4 * 4:(g4 + 1) * 4], in_=banks[be][0:8, :].rearrange("p (a b) -> p a b", a=4),
                        axis=mybir.AxisListType.X, op=ALU.max), reads=[bk(be)], writes=['EM8'])
                for j in range(4):
                    for d in range(2):
                        ca, cb = (2 * j, 2 * j + 1) if d == 0 else (2 * j + 1, 2 * j)
                        a_, b_ = ca * 2 + d, cb * 2 + d
                        jd = j * 2 + d
                        S.op('dve', lambda e, a_=a_: e.tensor_tensor(out=T8[:, 0:1], in0=BT8[:, a_:a_ + 1], in1=EM8[:, a_:a_ + 1],
                                                                      op=ALU.max), reads=['BT8', 'EM8'], writes=[('T8', 0)])
                        S.op('dve', lambda e, b_=b_: e.tensor_tensor(out=T8[:, 1:2], in0=T8[:, 0:1], in1=BT8[:, b_:b_ + 1],
                                                                      op=ALU.add), reads=['BT8', ('T8', 0)], writes=[('T8', 1)])
                        S.op('dve', lambda e, b_=b_, jd=jd: e.tensor_tensor(out=M8[:, jd:jd + 1], in0=T8[:, 1:2], in1=EM8[:, b_:b_ + 1],
                                                                             op=ALU.max), reads=['EM8', ('T8', 1)], writes=['M8'])
                S.op('act', lambda e: e.activation(out=SC8[:], in_=M8[:], func=AF.Exp, scale=-1.0), reads=['M8'], writes=['SC8'])
                dma('sp', b_mo[:, :], M8[:], reads=['M8'], pool='st')
                bs_ = next_bank()
                for jd in range(8):
                    S.op('dve', lambda e, jd=jd: e.tensor_scalar(out=d8[:, jd, :], in0=ident_f[0:8, 0:8], scalar1=SC8[:, jd:jd + 1],
                                                                  scalar2=None, op0=ALU.mult),
                         reads=['ident_f', 'SC8'], writes=[('d8', jd)])
                    S.op('pe', lambda e, jd=jd: e.matmul(banks[bs_][:, jd * 8:(jd + 1) * 8], lhsT=ones_f[0:8, :], rhs=d8[:, jd, :],
                                                         start=True, stop=True),
                         reads=['ones_f', ('d8', jd)], writes=[bk(bs_)])
                S.op('act', lambda e: e.copy(out=SCB[:], in_=banks[bs_][:, 0:64]), reads=[bk(bs_)], writes=['SCB'])

                wi = [0]
                hs_done = set()

                def record(fn):
                    saved = []
                    S.op = lambda *a, **k: saved.append((a, k))
                    try:
                        fn()
                    finally:
                        del S.op
                    return saved

                def loadw(col_ranges):
                    w = wt[0]
                    wk = ('wb', 0)
                    wi[0] += 1
                    o_ = 0
                    for c0, n_ in col_ranges:
                        dma('pool', w[:, :, o_:o_ + n_], b_w_in[:, c0:c0 + n_].rearrange("(c p) n -> p c n", p=128), writes=[wk])
                        o_ += n_
                    return w, wk

                def chunk(c, d, h, s_):
                    col = (c * 2 + d) * 8 + h
                    lfi = c * 32 + (2 * d + 1) * 8 + h
                    S.op('dve', lambda e: e.tensor_scalar(out=lfbc[:, s_, :], in0=ones_f[:], scalar1=LF[:, lfi:lfi + 1], scalar2=None,
                                                           op0=ALU.mult), reads=['LF', 'ones_f'], writes=[('lfbc', s_)])
                    br = next_bank()
                    S.op('pe', lambda e: e.matmul(banks[br][:, 0:128], lhsT=lfbc[:, s_, :], rhs=Um[:, d, :], start=True, stop=True),
                         reads=[('lfbc', s_), 'Um'], writes=[bk(br)])
                    S.op('dve', lambda e: e.tensor_tensor(out=tD[:, s_, :], in0=banks[br][:, 0:128], in1=Mk[:, d, :], op=ALU.add),
                         reads=[bk(br), 'Mk'], writes=[('tD', s_)])
                    S.op('act', lambda e: e.activation(out=Dt[:, s_, :], in_=tD[:, s_, :], func=AF.Exp, bias=AB[:, col:col + 1], scale=1.0),
                         reads=[('tD', s_), 'AB'], writes=[('Dt', s_)])
                    S.op('act', lambda e: e.activation(out=EB[:, s_, :], in_=banks[br][:, 0:128], func=AF.Exp),
                         reads=[bk(br)], writes=[('EB', s_)])
                    S.op('dve', lambda e: e.tensor_tensor(out=Qs[:, s_, :], in0=QT[:, c * 128:(c + 1) * 128], in1=EB[:, s_, :], op=ALU.mult),
                         reads=['QT', ('EB', s_)], writes=[('Qs', s_)])
                    bs2 = next_bank()
                    S.op('pe', lambda e: e.matmul(banks[bs2][:, 0:128], lhsT=KT[:, c * 128:(c + 1) * 128], rhs=QT[:, c * 128:(c + 1) * 128],
                                                  start=True, stop=True), reads=['KT', 'QT'], writes=[bk(bs2)])
                    S.op('dve', lambda e: e.tensor_tensor(out=PT[:, s_, :], in0=banks[bs2][:, 0:128], in1=Dt[:, s_, :], op=ALU.mult),
                         reads=[bk(bs2), ('Dt', s_)], writes=[('PT', s_)])
                    bn = next_bank()
                    S.op('pe', lambda e: e.matmul(banks[bn][:, 0:257], lhsT=PT[:, s_, :], rhs=Vaug[:, c, 0:257], start=True, stop=False),
                         reads=[('PT', s_), ('Vaug', c), 'Vaug_ones'], writes=[bk(bn)])
                    S.op('pe', lambda e: e.matmul(banks[bn][:, 0:257], lhsT=Qs[:, s_, :], rhs=Sbf[:, d, 0:257], start=False, stop=True),
                         reads=[('Qs', s_), ('Sbf', d)], writes=[bk(bn)])
                    S.op('dve', lambda e: e.tensor_scalar(out=rr[:, s_, 2:3], in0=banks[bn][:, 256:257], scalar1=-1.0, scalar2=None,
                                                           op0=ALU.mult), reads=[bk(bn)], writes=[('rr2', s_)])
                    S.op('dve', lambda e: e.scalar_tensor_tensor(out=rr[:, s_, 0:1], in0=banks[bn][:, 256:257], scalar=1.0,
                                                                  in1=rr[:, s_, 2:3], op0=ALU.max, op1=ALU.max),
                         reads=[bk(bn), ('rr2', s_)], writes=[('rr0', s_)])
                    S.op('dve', lambda e: e.reciprocal(out=rr[:, s_, 1:2], in_=rr[:, s_, 0:1]), reads=[('rr0', s_)], writes=[('rr1', s_)])
                    if c not in hs_done:
                        hs_done.add(c)
                        S.op('dve', lambda e: e.tensor_scalar(out=Hs[:, c, :], in0=banks[bn][:, 0:256], scalar1=rr[:, s_, 1:2], scalar2=None,
                                                               op0=ALU.mult), reads=[bk(bn), ('rr1', s_)], writes=[('Hs', c)])
                    else:
                        S.op('dve', lambda e: e.scalar_tensor_tensor(out=Hs[:, c, :], in0=banks[bn][:, 0:256], scalar=rr[:, s_, 1:2],
                                                                      in1=Hs[:, c, :], op0=ALU.mult, op1=ALU.add),
                             reads=[bk(bn), ('rr1', s_), ('Hs', c)], writes=[('Hs', c)])
                    S.op('dve', lambda e: e.tensor_scalar(out=Kw[:, s_, :], in0=Ktm[:, c, :], scalar1=WL[:, col:col + 1], scalar2=None,
                                                           op0=ALU.mult), reads=[('Ktm', c), 'WL'], writes=[('Kw', s_)])
                    bu = next_bank()
                    S.op('pe', lambda e: e.matmul(banks[bu][:, 0:257], lhsT=Kw[:, s_, :], rhs=Vaug[:, c, 0:257], start=True, stop=True),
                         reads=[('Kw', s_), ('Vaug', c), 'Vaug_ones'], writes=[bk(bu)])
                    S.op('dve', lambda e: e.scalar_tensor_tensor(out=S32[:, d, :], in0=S32[:, d, :], scalar=DT[:, col:col + 1],
                                                                  in1=banks[bu][:, 0:257], op0=ALU.mult, op1=ALU.add),
                         reads=[('S32', d), 'DT', bk(bu)], writes=[('S32', d)])
                    S.op('act', lambda e: e.copy(out=Sbf[:, d, 0:257], in_=S32[:, d, :]), reads=[('S32', d)], writes=[('Sbf', d)])

                for h in range(8):
                    w, wk = loadw([(h * 128, 128), (1024 + h * 128, 128)])
                    for slot, dst, dkey, scl in ((0, QT, 'QT', 128 ** -0.5), (1, KT, 'KT', 1.0)):
                        for hf in range(2):
                            b = next_bank()
                            for kc in range(16):
                                S.op('pe', lambda e, kc=kc, b=b, hf=hf, slot=slot, w=w: e.matmul(
                                    banks[b][:], lhsT=w[:, kc, slot * 128:(slot + 1) * 128], rhs=hT[:, kc, hf * 512:(hf + 1) * 512],
                                    start=(kc == 0), stop=(kc == 15)), reads=[wk, ('hT', kc)], writes=[bk(b)])
                            S.op('act', lambda e, b=b, hf=hf, dst=dst, scl=scl: e.mul(out=dst[:, hf * 512:(hf + 1) * 512], in_=banks[b][:],
                                                                                      mul=scl), reads=[bk(b)], writes=[dkey])
                    for tt in range(8):
                        b = next_bank()
                        for kc in range(16):
                            S.op('pe', lambda e, kc=kc, b=b, tt=tt, w=w: e.matmul(
                                banks[b][:, 0:128], lhsT=hT[:, kc, tt * 128:(tt + 1) * 128], rhs=w[:, kc, 128:256],
                                start=(kc == 0), stop=(kc == 15)), reads=[wk, ('hT', kc)], writes=[bk(b)])
                        S.op('act', lambda e, b=b, tt=tt: e.copy(out=Ktm[:, tt, :], in_=banks[b][:, 0:128]),
                             reads=[bk(b)], writes=[('Ktm', tt)])
                    w, wk = loadw([(2048 + h * 256, 256)])
                    for tt in range(8):
                        b = next_bank()
                        for kc in range(16):
                            S.op('pe', lambda e, kc=kc, b=b, tt=tt, w=w: e.matmul(
                                banks[b][:, 0:256], lhsT=hT[:, kc, tt * 128:(tt + 1) * 128], rhs=w[:, kc, :],
                                start=(kc == 0), stop=(kc == 15)), reads=[wk, ('hT', kc)], writes=[bk(b)])
                        S.op('dve', lambda e, b=b, tt=tt: e.tensor_copy(out=Vaug[:, tt, 0:256], in_=banks[b][:, 0:256]),
                             reads=[bk(b)], writes=[('Vaug', tt)])
                    for d in range(2):
                        r0 = (d * 8 + h) * 128
                        dma('sp', S32[:, d, :], b_S0[r0:r0 + 128, :], writes=[('S32', d)])
                        S.op('dve', lambda e, d=d: e.tensor_scalar(out=S32[:, d, :], in0=S32[:, d, :],
                                                                    scalar1=m0e[:, d * 8 + h:d * 8 + h + 1], scalar2=None, op0=ALU.mult),
                             reads=[('S32', d), 'm0e'], writes=[('S32', d)])
                        S.op('act', lambda e, d=d: e.copy(out=Sbf[:, d, 0:257], in_=S32[:, d, :]), reads=[('S32', d)], writes=[('Sbf', d)])
                    hs_done.clear()
                    for n_ in range(8):
                        recs = []
                        for d in range(2):
                            c = n_ if d == 0 else 7 - n_

                            def step(d=d, c=c, n_=n_):
                                if n_ > 0 and n_ % 2 == 0:
                                    S.op('dve', lambda e: e.tensor_scalar(out=S32[:, d, :], in0=S32[:, d, :], scalar1=keep[:, 0:1],
                                                                           scalar2=None, op0=ALU.mult),
                                         reads=[('S32', d), 'keep'], writes=[('S32', d)])
                                    S.op('act', lambda e: e.copy(out=Sbf[:, d, 0:257], in_=S32[:, d, :]),
                                         reads=[('S32', d)], writes=[('Sbf', d)])
                                chunk(c, d, h, d)
                                if n_ % 2 == 1:
                                    j = c // 2
                                    jd = j * 2 + d
                                    S.op('dve', lambda e: e.tensor_scalar(
                                        out=outS[:, d, :], in0=S32[:, d, :], scalar1=SCB[:, jd * 8 + h:jd * 8 + h + 1], scalar2=None,
                                        op0=ALU.mult), reads=[('S32', d), 'SCB'], writes=[('outS', d)])
                                    r0 = (j * 16 + d * 8 + h) * 128
                                    dma('sp', b_So[r0:r0 + 128, :], outS[:, d, :], reads=[('outS', d)], pool='st')
                            recs.append(record(step))
                        for i in range(max(len(r_) for r_ in recs)):
                            for r_ in recs:
                                if i < len(r_):
                                    S.op(*r_[i][0], **r_[i][1])
                    w, wk = loadw([(4096 + h * 256, 256)])
                    dma('sp', nrm[:], b_nrm[:, h * 256:(h + 1) * 256], writes=['nrm'])
                    for tt in range(8):
                        os_ = tt % 2
                        b = next_bank()
                        for kc in range(16):
                            S.op('pe', lambda e, kc=kc, b=b, tt=tt, w=w: e.matmul(
                                banks[b][:, 0:256], lhsT=hT[:, kc, tt * 128:(tt + 1) * 128], rhs=w[:, kc, :],
                                start=(kc == 0), stop=(kc == 15)), reads=[wk, ('hT', kc)], writes=[bk(b)])
                        S.op('act', lambda e, b=b, os_=os_: e.activation(out=SIGO[:, os_, :], in_=banks[b][:, 0:256], func=AF.Sigmoid),
                             reads=[bk(b)], writes=[('SIGO', os_)])
                        S.op('act', lambda e, tt=tt: e.activation(out=sq_junk[:, 0:256], in_=Hs[:, tt, :], func=AF.Square,
                                                                   accum_out=sst[:, 0:1]),
                             reads=[('Hs', tt)], writes=['sq_junk', ('sst', 0)])
                        S.op('dve', lambda e: e.tensor_scalar(out=sst[:, 1:2], in0=sst[:, 0:1], scalar1=1.0 / 256, scalar2=EPS,
                                                               op0=ALU.mult, op1=ALU.add), reads=[('sst', 0)], writes=[('sst', 1)])
                        S.op('act', lambda e: e.activation(out=sst[:, 2:3], in_=sst[:, 1:2], func=AF.Ln),
                             reads=[('sst', 1)], writes=[('sst', 2)])
                        S.op('act', lambda e: e.activation(out=sst[:, 3:4], in_=sst[:, 2:3], func=AF.Exp, scale=-0.5),
                             reads=[('sst', 2)], writes=[('sst', 3)])
                        S.op('dve', lambda e, tt=tt: e.scalar_tensor_tensor(
                            out=hn, in0=Hs[:, tt, :], scalar=sst[:, 3:4], in1=nrm[:],
                            op0=ALU.mult, op1=ALU.mult), reads=[('Hs', tt), ('sst', 3), 'nrm'], writes=[('tmp_f', 0)])
                        S.op('dve', lambda e, tt=tt, os_=os_: e.tensor_tensor(out=onb[:, os_, :], in0=hn, in1=SIGO[:, os_, :], op=ALU.mult),
                             reads=[('tmp_f', 0), ('SIGO', os_)], writes=[('onb', os_)])
                        for ci in range(2):
                            b = next_bank()
                            pb = banks[b][:].bitcast(BF16)
                            S.op('pe', lambda e, pb=pb, ci=ci, os_=os_: e.transpose(out=pb[:, 0:128], in_=onb[:, os_, ci * 128:(ci + 1) * 128],
                                                                                    identity=ident_b[:]),
                                 reads=[('onb', os_), 'ident_b'], writes=[bk(b)])
                            och = 2 * h + ci
                            S.op('act', lambda e, pb=pb, och=och, tt=tt: e.copy(out=oT[:, och, tt * 128:(tt + 1) * 128], in_=pb[:, 0:128]),
                                 reads=[bk(b)], writes=[('oT', och)])
            S.barrier()
            out_proj(b_w_out, oT, st)
        S.barrier()

    def out_proj(w_out_d, oT, st):
        with ExitStack() as st3:
            wo = sb("wo", [128, 16, D], BF16, st3)
            for c4 in range(4):
                dma('pool', wo[:, :, c4 * 512:(c4 + 1) * 512],
                    w_out_d[:, c4 * 512:(c4 + 1) * 512].rearrange("(c p) n -> p c n", p=128), writes=[('wo', c4)])
            for tt in range(8):
                yb = [next_bank() for _ in range(4)]
                for q in range(4):
                    for kc in range(16):
                        S.op('pe', lambda e, q=q, kc=kc, tt=tt, yb=yb: e.matmul(
                            banks[yb[q]][:], lhsT=oT[:, kc, tt * 128:(tt + 1) * 128], rhs=wo[:, kc, q * 512:(q + 1) * 512],
                            start=(kc == 0), stop=(kc == 15)), reads=[('wo', q), ('oT', kc)], writes=[bk(yb[q])])
                post_norm(0, tt, yb)

    for l in layers:
        with ExitStack() as wst:
            modulation(l, wst)
        S.barrier()
        if l % 4 == 0:
            mixer_a(l)
        elif l % 4 == 1:
            mixer_b(l)
        elif l % 4 == 2:
            mixer_cd(l, 'c')
        elif l % 4 == 3:
            mixer_cd(l, 'd')
        if dbg == ('mid', l):
            for tt in range(8):
                dma('sp', dbg_out[tt * 128:(tt + 1) * 128, :], xres[:, tt, :], reads=[('x', tt)], pool='st')
        mlp(l)

    for tt in range(8):
        dma('sp', y_out[tt * 128:(tt + 1) * 128, :], xres[:, tt, :], reads=[('x', tt)], pool='st')
    if max_ops is not None:
        S.truncate(max_ops)
    info = S.emit()
    es.close()
    return nc, info


def fm(v):
    v = np.asarray(v, np.float32)
    return np.ascontiguousarray(np.moveaxis(v.reshape(v.shape[:-1] + (-1, 128)), -1, 0))


def make_in_maps(inp, nlw=4):
    maps = []
    ident = np.eye(128, dtype=np.float32)
    rot = rot_matrix()
    shared = dict(
        w_mod=inp['w_mod'][:nlw].reshape(nlw * D, 6 * D), w_ff1=inp['w_ff1'][:nlw].reshape(nlw * D, DFF),
        w_ff2=inp['w_ff2'][:nlw].reshape(nlw * DFF, D), ident=ident, rotm=rot,
        bmod=fm(inp['b_mod']).reshape(128, 4 * 96),
        gfm=fm(inp['g_norm']).reshape(128, 4 * 4 * 16),
        a_w_in=inp['a_w_in'][0], a_w_out=inp['a_w_out'][0],
        a_lam=np.ascontiguousarray(np.broadcast_to(inp['a_lambda'][0].reshape(1, 4096), (128, 4096))),
        a_sub=np.ascontiguousarray(np.broadcast_to(inp['a_subln'][0].reshape(1, 128), (128, 128))),
    )
    NA_L = [[0, 1, 2, 3], [0, 1, 2, 3], [0, 1, 2, 3, 4], [1, 2, 3, 4, 5], [2, 3, 4, 5, 6], [3, 4, 5, 6, 7], [4, 5, 6, 7], [4, 5, 6, 7]]
    rpb = np.asarray(inp['c_rpb'][0], np.float32).reshape(16, 465)
    ext = np.concatenate([rpb, np.full((16, 1), NEG, np.float32), np.zeros((16, 1), np.float32)], 1)
    cbias = {}
    dmask = {}
    for sample in (True, False):
        idx = np.full((8, 128, 5, 128), 465, np.int64)
        for qt in range(8):
            for si, kc in enumerate(NA_L[qt]):
                k = kc * 128 + np.arange(128)[:, None]
                q = qt * 128 + np.arange(128)[None, :]
                if sample:
                    rk, ck_, rq, cq_ = k // 64, k % 64, q // 64, q % 64
                    r0 = np.clip(rq - 4, 0, 8)
                    cs = np.clip(cq_ - 8, 0, 48)
                    valid = (rk >= r0) & (rk < r0 + 8) & (ck_ >= cs) & (ck_ < cs + 16)
                    ii = np.where(valid, (rk - rq + 7) * 31 + np.clip(ck_ - cq_, -15, 15) + 15, 465)
                else:
                    ii = np.full((128, 128), 466 if kc // 2 == qt // 2 else 465)
                idx[qt, :, si, :] = ii
        cbias[sample] = np.ascontiguousarray(ext[:, idx].reshape(16 * 8 * 128, 5 * 128))
        dm = np.full((128, 2, 8, 128), NEG, np.float32)
        kp = np.arange(128)[:, None]
        qq = np.arange(128)[None, :]
        for qt in range(8):
            if sample:
                dm[:, 0, qt, :] = np.where(kp >= qq, 0.0, NEG)
                dm[:, 1, qt, :] = np.where(kp <= qq, 0.0, NEG)
            else:
                dm[:, 0, qt, :] = 0.0 if qt % 2 == 1 else NEG
                dm[:, 1, qt, :] = 0.0 if qt % 2 == 0 else NEG
        dmask[sample] = dm.reshape(128, 2 * 8 * 128)
    shared.update(
        c_w_in=inp['c_w_in'][0], c_w_out=inp['c_w_out'][0], d_w_in=inp['d_w_in'][0], d_w_out=inp['d_w_out'][0],
        d_snk=np.ascontiguousarray(np.broadcast_to(inp['d_sink'][0].reshape(1, 32), (128, 32))))
    pi = np.arange(128)[:, None]
    ti = np.arange(128)[None, :]
    um = np.concatenate([(pi <= ti), (pi >= ti)], 1).astype(np.float32)
    mkb = np.concatenate([np.where(pi <= ti, 0.0, NEG), np.where(pi >= ti, 0.0, NEG)], 1).astype(np.float32)
    shared.update(
        b_w_in=inp['b_w_in'][0], b_w_out=inp['b_w_out'][0], b_um=um, b_mk=mkb,
        b_gb=np.ascontiguousarray(np.broadcast_to(inp['b_gate_bias'][0].reshape(1, 32), (128, 32))),
        b_nrm=np.ascontiguousarray(np.broadcast_to(inp['b_norm'][0].reshape(1, D), (128, D))))
    for core in range(8):
        sample = core < 4
        m = dict(shared)
        cb = core if sample else 0
        if sample:
            ct = np.swapaxes(inp['state_b_C'][core, 0], -1, -2)
            m['b_S0'] = np.concatenate([ct, inp['state_b_n'][core, 0][..., None]], -1).reshape(16 * 128, 257)
            m['b_m0'] = np.ascontiguousarray(np.broadcast_to(inp['state_b_m'][core, 0].reshape(1, 16), (128, 16)))
            m['b_keep'] = np.ones((128, 1), np.float32)
        else:
            m['b_S0'] = np.zeros((16 * 128, 257), np.float32)
            m['b_m0'] = np.zeros((128, 16), np.float32)
            m['b_keep'] = np.zeros((128, 1), np.float32)
        m['c_ck'] = inp['cache_c_k'][cb, 0].reshape(512, D)
        m['c_cv'] = inp['cache_c_v'][cb, 0].reshape(512, D)
        m['d_ck'] = inp['cache_d_k'][cb, 0].reshape(512, 512)
        m['d_cv'] = inp['cache_d_v'][cb, 0].reshape(512, 512)
        m['c_bias'] = cbias[sample]
        m['d_mask'] = dmask[sample]
        m['ctxb'] = np.full((128, 1), 0.0 if sample else NEG, np.float32)
        if sample:
            m['x'] = inp['x_sample'][core]
            m['cvec'] = fm(inp['c'][core])
            m['a_ck'] = inp['cache_a_k'][core, 0].reshape(512, D)
            m['a_cv'] = inp['cache_a_v'][core, 0].reshape(512, D)
            mask = np.zeros((12, 4), np.float32)
        else:
            p0 = (core - 4) * 4
            m['x'] = inp['x_prompt'][p0:p0 + 4].reshape(NT, D)
            m['cvec'] = fm(inp['c_ctx'])
            m['a_ck'] = inp['cache_a_k'][0, 0].reshape(512, D)
            m['a_cv'] = inp['cache_a_v'][0, 0].reshape(512, D)
            mask = np.full((12, 4), NEG, np.float32)
            for qp in range(4):
                mask[2 * qp:2 * qp + 2, qp] = 0.0
        m['a_mask'] = np.ascontiguousarray(np.broadcast_to(mask.reshape(1, 48), (128, 48)))
        cq, sq = rope_tables(sample, 0.125)
        ck, sk = rope_tables(sample, 1.0)
        m['a_cosq'], m['a_sinq'], m['a_cosk'], m['a_sink'] = cq, sq, ck, sk
        maps.append({k: (v if (v.dtype == np.float32 and v.flags['C_CONTIGUOUS']) else np.ascontiguousarray(v, dtype=np.float32))
                     for k, v in m.items()})
    return maps


def kernel(**inputs):
    inp = {k: np.asarray(v) for k, v in inputs.items()}
    nc, info = build()
    maps = make_in_maps(inp)
    res = run_bass_kernel_spmd(nc, maps, core_ids=list(range(8)))
    r = res.results
    y_sample = np.stack([r[c]['y'] for c in range(4)], 0)
    y_prompt = np.concatenate([r[c]['y'].reshape(4, 256, D) for c in range(4, 8)], 0)

    def ctx(name, hh, dd):
        return np.concatenate([r[c][name].reshape(4, 1, 256, hh, dd) for c in range(4, 8)], 0)
    so = np.concatenate([r[c]['b_So'].reshape(4, 1, 2, 8, 128, 257) for c in range(4, 8)], 0)
    b_C = np.ascontiguousarray(np.swapaxes(so[..., 0:256], -1, -2))
    b_n = np.ascontiguousarray(so[..., 256])
    b_m = np.concatenate([np.transpose(r[c]['b_mo'].reshape(8, 4, 1, 2), (1, 2, 3, 0)) for c in range(4, 8)], 0)
    b_m = np.ascontiguousarray(b_m)
    return (y_prompt, y_sample, ctx('a_k', 16, 128), ctx('a_v', 16, 128), b_C, b_n, b_m,
            ctx('c_k', 16, 128), ctx('c_v', 16, 128), ctx('d_k', 8, 64), ctx('d_v', 8, 64))
```

```python
import math
import numpy as np
import concourse.bass as bass
import concourse.mybir as mybir
from concourse.bass_utils import run_bass_kernel_spmd
from contextlib import ExitStack

F32 = mybir.dt.float32
BF16 = mybir.dt.bfloat16
AF = mybir.ActivationFunctionType
ALU = mybir.AluOpType

D = 2048
NT = 1024
DFF = 8192
EPS = 1e-6
NEG = -30000.0


class _Rec:
    def __init__(self):
        self.call = None

    def __getattr__(self, name):
        def f(*a, **k):
            self.call = (name, a, k)
            return None
        return f


class Sched:
    COMPUTE = ('pe', 'act', 'dve', 'pool')

    def __init__(self, nc, es):
        self.nc = nc
        self.es = es
        self.ops = []
        self.last_w = {}
        self.readers = {}
        self.engs = ('pe', 'act', 'dve', 'pool', 'sp')
        self.psem = {e: es.enter_context(nc.semaphore("P_" + e)) for e in self.COMPUTE}
        self.dma_last = {}
        self.dsems = {}
        self.last_on = {}
        self.bar = None
        self.bar_done = set()

    def barrier(self):
        self.bar = set(self.last_on.values()) | set(self.dma_last.values())
        self.bar_done = set()

    def op(self, eng, fn, reads=(), writes=(), dma=None):
        deps = set()
        for r in reads:
            if r in self.last_w:
                deps.add(self.last_w[r])
            if isinstance(r, tuple) and r[0] == 'bank':
                for k_, v_ in (self.readers.get(r) or {}).items():
                    if k_ != eng:
                        deps.add(v_)
        for w in writes:
            if w in self.last_w:
                deps.add(self.last_w[w])
            rd = self.readers.get(w)
            if rd:
                deps.update(rd.values())
        if dma is not None:
            if dma not in self.dsems:
                self.dsems[dma] = self.es.enter_context(self.nc.semaphore("D_" + dma))
            if dma in self.dma_last:
                deps.add(self.dma_last[dma])
        if self.bar is not None and eng not in self.bar_done:
            deps.update(self.bar)
            self.bar_done.add(eng)
        oid = len(self.ops)
        rec = _Rec()
        fn(rec)
        name_, a_, k_ = rec.call
        fn = (lambda E, name_=name_, a_=a_, k_=k_: getattr(E, name_)(*a_, **k_))
        self.ops.append([eng, fn, deps, dma, False, None])
        if dma is not None:
            self.dma_last[dma] = oid
        else:
            self.last_on[eng] = oid
        for w in writes:
            self.last_w[w] = oid
            self.readers[w] = {}
        for r in reads:
            d = self.readers.setdefault(r, {})
            d[eng if dma is None else ('dma', oid)] = oid
        return oid

    def truncate(self, n):
        self.ops = self.ops[:n]

    def emit(self, final_wait_eng='sp'):
        ops = self.ops
        for o in ops:
            for d in o[2]:
                do = ops[d]
                if do[3] is None and not (do[0] == 'pe' and o[0] == 'pe' and o[3] is None):
                    do[4] = True
        cnt = {e: 0 for e in self.COMPUTE}
        dcnt = {}
        for o in ops:
            if o[3] is not None:
                dcnt[o[3]] = dcnt.get(o[3], 0) + 16
                o[5] = (o[3], dcnt[o[3]])
            elif o[4]:
                cnt[o[0]] += 1
                o[5] = (o[0], cnt[o[0]])
        waited = {e: {} for e in self.engs}
        streams = {e: [] for e in self.engs}
        for o in ops:
            eng, fn, deps, dma, sig, tok = o
            need = {}
            for d in deps:
                do = ops[d]
                if do[3] is None and do[0] == 'pe' and eng == 'pe' and dma is None:
                    continue
                s, v = do[5]
                if v > need.get(s, 0):
                    need[s] = v
            for s, v in need.items():
                if waited[eng].get(s, 0) >= v:
                    continue
                sem = self.psem[s] if s in self.psem else self.dsems[s]
                streams[eng].append(('w', sem, v))
                waited[eng][s] = v
            if dma is not None:
                streams[eng].append(('i', fn, self.dsems[dma], 16))
            elif sig:
                streams[eng].append(('i', fn, self.psem[eng], 1))
            else:
                streams[eng].append(('i', fn, None, 0))
        for s, v in dcnt.items():
            streams[final_wait_eng].append(('w', self.dsems[s], v))

        def runner(eng):
            def f(E):
                for it in streams[eng]:
                    if it[0] == 'w':
                        E.wait_ge(it[1], it[2])
                    else:
                        ins = it[1](E)
                        if it[2] is not None:
                            ins.then_inc(it[2], it[3])
            return f
        with self.nc.Block() as block:
            block.sync(runner('sp'))
            block.scalar(runner('act'))
            block.vector(runner('dve'))
            block.gpsimd(runner('pool'))
            block.tensor(runner('pe'))
        return dict(n_ops=len(ops), counts=cnt)


def rope_tables(sample, qscale):
    t = np.arange(NT)
    cos = np.ones((64, NT), np.float64)
    sin = np.zeros((64, NT), np.float64)
    if sample:
        inv = 10000.0 ** (-np.arange(16, dtype=np.float32) / 16)
        for grp, pos in ((0, t // 64), (1, t % 64)):
            ang = pos.astype(np.float32)[None, :] * inv[:, None].astype(np.float32)
            c, s = np.cos(ang), np.sin(ang)
            cos[grp * 32:grp * 32 + 16] = c
            cos[grp * 32 + 16:grp * 32 + 32] = c
            sin[grp * 32:grp * 32 + 16] = -s
            sin[grp * 32 + 16:grp * 32 + 32] = s
    cos = np.concatenate([cos, cos], 0) * qscale
    sin = np.concatenate([sin, sin], 0) * qscale
    return cos.astype(np.float32), sin.astype(np.float32)


def rot_matrix():
    P = np.zeros((128, 128), np.float32)
    for m in range(128):
        g, i = divmod(m, 32)
        partner = g * 32 + (i + 16) % 32
        P[partner, m] = 1.0
    return P


def build(layers=(0, 1, 2, 3), dbg=None, max_ops=None):
    NLW = max(layers) + 1
    nc = bass.Bass("TRN2", target_bir_lowering=False)
    es = ExitStack()
    S = Sched(nc, es)

    def din(name, shape):
        return nc.dram_tensor(name, list(shape), F32, kind="ExternalInput").ap()

    def dout(name, shape):
        return nc.dram_tensor(name, list(shape), F32, kind="ExternalOutput").ap()

    x_in = din("x", [NT, D])
    cvec = din("cvec", [128, 16])
    w_mod = din("w_mod", [NLW * D, 6 * D])
    bmod = din("bmod", [128, 4 * 96])
    gfm = din("gfm", [128, 4 * 4 * 16])
    w_ff1 = din("w_ff1", [NLW * D, DFF])
    w_ff2 = din("w_ff2", [NLW * DFF, D])
    ident_d = din("ident", [128, 128])
    a_w_in = din("a_w_in", [D, 3 * D])
    a_w_out = din("a_w_out", [D, D])
    a_ck = din("a_ck", [512, D])
    a_cv = din("a_cv", [512, D])
    a_lam = din("a_lam", [128, 4096])
    a_sub = din("a_sub", [128, 128])
    a_cosq = din("a_cosq", [128, NT])
    a_sinq = din("a_sinq", [128, NT])
    a_cosk = din("a_cosk", [128, NT])
    a_sink = din("a_sink", [128, NT])
    a_mask = din("a_mask", [128, 48])
    rotm = din("rotm", [128, 128])

    c_w_in = din("c_w_in", [D, 3 * D])
    c_w_out = din("c_w_out", [D, D])
    c_ck = din("c_ck", [512, D])
    c_cv = din("c_cv", [512, D])
    c_bias = din("c_bias", [16 * 8 * 128, 5 * 128])
    d_w_in = din("d_w_in", [D, 3072])
    d_w_out = din("d_w_out", [D, D])
    d_ck = din("d_ck", [512, 512])
    d_cv = din("d_cv", [512, 512])
    d_mask = din("d_mask", [128, 2 * 8 * 128])
    d_snk = din("d_snk", [128, 32])
    ctxb_d = din("ctxb", [128, 1])
    b_w_in = din("b_w_in", [D, 6176])
    b_w_out = din("b_w_out", [D, D])
    b_gb = din("b_gb", [128, 32])
    b_nrm = din("b_nrm", [128, D])
    b_keep = din("b_keep", [128, 1])
    b_m0 = din("b_m0", [128, 16])
    b_S0 = din("b_S0", [16 * 128, 257])
    b_um = din("b_um", [128, 256])
    b_mk = din("b_mk", [128, 256])
    b_So = dout("b_So", [4 * 16 * 128, 257])
    b_mo = dout("b_mo", [8, 8])
    c_ko = dout("c_k", [NT, D])
    c_vo = dout("c_v", [NT, D])
    d_ko = dout("d_k", [NT, 512])
    d_vo = dout("d_v", [NT, 512])
    y_out = dout("y", [NT, D])
    a_ko = dout("a_k", [NT, D])
    a_vo = dout("a_v", [NT, D])
    dbg_out = dout("dbg", [NT, D]) if dbg else None

    sb_ctr = [0]

    def sb(name, shape, dt, stack=es):
        sb_ctr[0] += 1
        return stack.enter_context(nc.sbuf_tensor("%s_%d" % (name, sb_ctr[0]), list(shape), dt))

    banks = [es.enter_context(nc.psum_tensor("bank%d" % i, [128, 512], F32)) for i in range(8)]
    bank_rr = [0]

    def next_bank():
        b = bank_rr[0]
        bank_rr[0] = (b + 1) % 8
        return b

    def bk(b):
        return ('bank', b)

    xres = sb("xres", [128, 8, D], F32)
    ident_f = sb("ident_f", [128, 128], F32)
    ident_b = sb("ident_b", [128, 128], BF16)
    ones_f = sb("ones_f", [128, 128], F32)
    cv_f = sb("cv_f", [128, 16], F32)
    cv_b = sb("cv_b", [128, 16], BF16)
    bmod_s = sb("bmod_s", [128, 4 * 96], F32)
    gfm_s = sb("gfm_s", [128, 4 * 4 * 16], F32)
    modv = sb("modv", [128, 96], F32)
    Acoef = sb("Acoef", [128, 2, 16], F32)
    Gfm = sb("Gfm", [128, 2, 16], F32)
    Gbc = sb("Gbc", [128, 2, D], F32)
    diag = sb("diag", [128, 2, 128], F32)
    stat = sb("stat", [128, 64], F32)
    sq_junk = sb("sq_junk", [128, 512], BF16)
    xn_b = sb("xn_b", [128, 1, D], BF16)
    tmp_f = sb("tmp_f", [128, 2, 512], F32)

    n_dma = [0]

    def dma(eng, out, in_, reads=(), writes=(), pool=None):
        if pool is None:
            pool = 'ld' if eng == 'sp' else 'wq'
        k = n_dma[0]
        n_dma[0] += 1
        name = "%s%d" % (pool, k % 6)
        S.op(eng, lambda e: e.dma_start(out=out, in_=in_), reads=reads, writes=writes, dma=name)

    for tt in range(8):
        dma('sp', xres[:, tt, :], x_in[tt * 128:(tt + 1) * 128, :], writes=[('x', tt)])
    dma('sp', ident_f[:], ident_d[:, :], writes=['ident_f'])
    dma('sp', cv_f[:], cvec[:, :], writes=['cv_f'])
    dma('sp', bmod_s[:], bmod[:, :], writes=['bmod_s'])
    dma('sp', gfm_s[:], gfm[:, :], writes=['gfm_s'])
    S.op('dve', lambda e: e.tensor_copy(out=ident_b[:], in_=ident_f[:]), reads=['ident_f'], writes=['ident_b'])
    S.op('dve', lambda e: e.memset(ones_f[:], 1.0), writes=['ones_f'])
    S.op('act', lambda e: e.activation(out=cv_b[:], in_=cv_f[:], func=AF.Silu), reads=['cv_f'], writes=['cv_b'])

    wctr = [0]

    def modulation(l, wst):
        wblk = [sb("wm%d" % i, [128, 16, 512], BF16, wst) for i in range(3)]
        b = next_bank()
        for nb in range(24):
            w = wblk[nb % 3]
            wk = ('wm', nb % 3)
            src = w_mod[l * D:(l + 1) * D, nb * 512:(nb + 1) * 512].rearrange("(c p) n -> p c n", p=128)
            dma('pool', w[:], src, writes=[wk])
            for j in range(4):
                col = nb * 4 + j
                for kc in range(16):
                    S.op('pe', lambda e, w=w, j=j, kc=kc, col=col, b=b: e.matmul(
                        banks[b][:, col:col + 1], lhsT=w[:, kc, j * 128:(j + 1) * 128], rhs=cv_b[:, kc:kc + 1],
                        start=(kc == 0), stop=(kc == 15)), reads=[wk, 'cv_b'], writes=[bk(b)])
        S.op('dve', lambda e, b=b: e.tensor_tensor(out=modv[:], in0=banks[b][:, 0:96], in1=bmod_s[:, l * 96:(l + 1) * 96],
                                                    op=ALU.add), reads=[bk(b), 'bmod_s'], writes=['modv'])
        g = lambda i: gfm_s[:, (l * 4 + i) * 16:(l * 4 + i + 1) * 16]
        for half in range(2):
            sc = modv[:, (3 * half + 1) * 16:(3 * half + 2) * 16]
            gt = modv[:, (3 * half + 2) * 16:(3 * half + 3) * 16]
            S.op('dve', lambda e, half=half, sc=sc: e.scalar_tensor_tensor(
                out=Acoef[:, half, :], in0=sc, scalar=1.0, in1=g(2 * half), op0=ALU.add, op1=ALU.mult),
                reads=['modv', 'gfm_s'], writes=[('Acoef', half)])
            S.op('dve', lambda e, half=half, gt=gt: e.tensor_tensor(
                out=Gfm[:, half, :], in0=gt, in1=g(2 * half + 1), op=ALU.mult),
                reads=['modv', 'gfm_s'], writes=[('Gfm', half)])
            for c4 in range(4):
                b2 = next_bank()
                for cc in range(4):
                    c = c4 * 4 + cc
                    S.op('dve', lambda e, c=c, half=half: e.tensor_scalar(
                        out=diag[:, c % 2, :], in0=ident_f[:], scalar1=Gfm[:, half, c:c + 1], scalar2=None, op0=ALU.mult),
                        reads=['ident_f', ('Gfm', half)], writes=[('diag', c % 2)])
                    S.op('pe', lambda e, c=c, cc=cc, b2=b2: e.matmul(
                        banks[b2][:, cc * 128:(cc + 1) * 128], lhsT=ones_f[:], rhs=diag[:, c % 2, :], start=True, stop=True),
                        reads=['ones_f', ('diag', c % 2)], writes=[bk(b2)])
                S.op('act', lambda e, c4=c4, half=half, b2=b2: e.copy(
                    out=Gbc[:, half, c4 * 512:(c4 + 1) * 512], in_=banks[b2][:]),
                    reads=[bk(b2)], writes=[('Gbc', half)])

    def adaln_in(half, tiles, hT, hkey, col0=0):
        shift = modv[:, (3 * half) * 16:(3 * half + 1) * 16]
        for i, tt in enumerate(tiles):
            s = tt % 2
            S.op('act', lambda e, tt=tt, s=s: e.activation(out=xn_b[:, 0, :], in_=xres[:, tt, :], func=AF.Square,
                                                            accum_out=stat[:, s:s + 1]),
                 reads=[('x', tt)], writes=[('xn_b', 0), ('stat', s)])
            S.op('dve', lambda e, s=s: e.tensor_scalar(out=stat[:, 2 + s:3 + s], in0=stat[:, s:s + 1], scalar1=1.0 / D,
                                                        scalar2=EPS, op0=ALU.mult, op1=ALU.add),
                 reads=[('stat', s)], writes=[('stat', 2 + s)])
            S.op('act', lambda e, s=s: e.activation(out=stat[:, 6 + s:7 + s], in_=stat[:, 2 + s:3 + s], func=AF.Ln),
                 reads=[('stat', 2 + s)], writes=[('stat', 6 + s)])
            S.op('act', lambda e, s=s: e.activation(out=stat[:, 4 + s:5 + s], in_=stat[:, 6 + s:7 + s], func=AF.Exp, scale=-0.5),
                 reads=[('stat', 6 + s)], writes=[('stat', 4 + s)])
            S.op('dve', lambda e, tt=tt, s=s: e.tensor_scalar(out=xn_b[:, 0, :], in0=xres[:, tt, :],
                                                               scalar1=stat[:, 4 + s:5 + s], scalar2=None, op0=ALU.mult),
                 reads=[('x', tt), ('stat', 4 + s)], writes=[('xn_b', 0)])
            for c8 in range(2):
                b = next_bank()
                pb = banks[b][:].bitcast(BF16)
                for cc in range(8):
                    c = c8 * 8 + cc
                    S.op('pe', lambda e, c=c, cc=cc, s=s, pb=pb: e.transpose(
                        out=pb[:, cc * 128:(cc + 1) * 128], in_=xn_b[:, 0, c * 128:(c + 1) * 128], identity=ident_b[:]),
                        reads=[('xn_b', 0), 'ident_b'], writes=[bk(b)])
                for cc in range(8):
                    c = c8 * 8 + cc
                    dst = hT[:, c, col0 + i * 128:col0 + (i + 1) * 128]
                    if True:
                        S.op('act', lambda e, c=c, cc=cc, pb=pb, dst=dst, half=half: e.activation(
                            out=dst, in_=pb[:, cc * 128:(cc + 1) * 128], func=AF.Identity,
                            bias=shift[:, c:c + 1], scale=Acoef[:, half, c:c + 1]),
                            reads=[bk(b), ('Acoef', half), 'modv'], writes=[(hkey, c)])
                    else:
                        S.op('dve', lambda e, c=c, cc=cc, pb=pb, dst=dst, half=half: e.tensor_scalar(
                            out=dst, in0=pb[:, cc * 128:(cc + 1) * 128], scalar1=Acoef[:, half, c:c + 1],
                            scalar2=shift[:, c:c + 1], op0=ALU.mult, op1=ALU.add),
                            reads=[bk(b), ('Acoef', half), 'modv'], writes=[(hkey, c)])

    def post_norm(half, tt, yb):
        for q in range(4):
            S.op('act', lambda e, q=q: e.activation(out=sq_junk[:, :], in_=banks[yb[q]][:],
                                                     func=AF.Square, accum_out=stat[:, 8 + q:9 + q]),
                 reads=[bk(yb[q])], writes=['sq_junk', ('stat', 8 + q)])
        S.op('dve', lambda e: e.tensor_reduce(out=stat[:, 12:13], in_=stat[:, 8:12], axis=mybir.AxisListType.X, op=ALU.add),
             reads=[('stat', 8 + q) for q in range(4)], writes=[('stat', 12)])
        S.op('dve', lambda e: e.tensor_scalar(out=stat[:, 13:14], in0=stat[:, 12:13], scalar1=1.0 / D, scalar2=EPS,
                                               op0=ALU.mult, op1=ALU.add), reads=[('stat', 12)], writes=[('stat', 13)])
        S.op('act', lambda e: e.activation(out=stat[:, 15:16], in_=stat[:, 13:14], func=AF.Ln),
             reads=[('stat', 13)], writes=[('stat', 15)])
        S.op('act', lambda e: e.activation(out=stat[:, 14:15], in_=stat[:, 15:16], func=AF.Exp, scale=-0.5),
             reads=[('stat', 15)], writes=[('stat', 14)])
        for q in range(4):
            s = q % 2
            S.op('dve', lambda e, q=q, s=s: e.scalar_tensor_tensor(
                out=tmp_f[:, s, :], in0=banks[yb[q]][:], scalar=stat[:, 14:15], in1=Gbc[:, half, q * 512:(q + 1) * 512],
                op0=ALU.mult, op1=ALU.mult), reads=[bk(yb[q]), ('stat', 14), ('Gbc', half)], writes=[('tmp_f', s)])
            S.op('dve', lambda e, q=q, s=s, tt=tt: e.tensor_tensor(
                out=xres[:, tt, q * 512:(q + 1) * 512], in0=xres[:, tt, q * 512:(q + 1) * 512], in1=tmp_f[:, s, :],
                op=ALU.add), reads=[('x', tt), ('tmp_f', s)], writes=[('x', tt)])

    def mlp(l):
        with ExitStack() as st:
            hTq = sb("hTq", [128, 16, 512], BF16, st)
            ystash = hTq[:].rearrange("p c n -> p (c n)").bitcast(F32).rearrange("p (t n) -> p t n", t=4)
            uTq = sb("uTq", [128, 64, 512], BF16, st)
            w1 = [sb("w1_%d" % i, [128, 16, 256], BF16, st) for i in range(2)]
            w2 = [sb("w2_%d" % i, [128, 1024], BF16, st) for i in range(6)]
            rl = sb("rl", [128, 2, 512], F32, st)
            hkeys = [('hTq', kc) for kc in range(16)]
            w2n = 0
            for hq in range(2):
                adaln_in(1, [4 * hq + i for i in range(4)], hTq, 'hTq')
                for nb in range(32):
                    w = w1[nb % 2]
                    wk = ('w1', nb % 2)
                    src = w_ff1[l * D:(l + 1) * D, nb * 256:(nb + 1) * 256].rearrange("(c p) n -> p c n", p=128)
                    dma('pool', w[:], src, writes=[wk])
                    for j in range(2):
                        b = next_bank()
                        fc = nb * 2 + j
                        for kc in range(16):
                            S.op('pe', lambda e, w=w, j=j, kc=kc, b=b: e.matmul(
                                banks[b][:], lhsT=w[:, kc, j * 128:(j + 1) * 128], rhs=hTq[:, kc, :],
                                start=(kc == 0), stop=(kc == 15)), reads=[wk, ('hTq', kc)], writes=[bk(b)])
                        s = fc % 2
                        S.op('act', lambda e, b=b, s=s: e.activation(out=rl[:, s, :], in_=banks[b][:], func=AF.Relu),
                             reads=[bk(b)], writes=[('rl', s)])
                        S.op('dve', lambda e, s=s, fc=fc: e.tensor_tensor(out=uTq[:, fc, :], in0=rl[:, s, :], in1=rl[:, s, :],
                                                                           op=ALU.mult),
                             reads=[('rl', s)], writes=[('uTq', fc)])
                for ch in range(2):
                    yb = [[next_bank() for _ in range(2)] for _ in range(4)]
                    for kc in range(64):
                        w = w2[w2n % 6]
                        wk = ("w2", w2n % 6)
                        w2n += 1
                        dma('pool', w[:], w_ff2[l * DFF + kc * 128:l * DFF + (kc + 1) * 128, ch * 1024:(ch + 1) * 1024], writes=[wk])
                        for t in range(4):
                            for nq in range(2):
                                S.op('pe', lambda e, w=w, kc=kc, t=t, nq=nq, yb=yb: e.matmul(
                                    banks[yb[t][nq]][:], lhsT=uTq[:, kc, t * 128:(t + 1) * 128], rhs=w[:, nq * 512:(nq + 1) * 512],
                                    start=(kc == 0), stop=(kc == 63)), reads=[wk, ('uTq', kc)], writes=[bk(yb[t][nq])])
                    for t in range(4):
                        tt = 4 * hq + t
                        for nq in range(2):
                            sc = 16 + t * 4 + ch * 2 + nq
                            S.op('act', lambda e, t=t, nq=nq, sc=sc, yb=yb: e.activation(
                                out=sq_junk[:, :], in_=banks[yb[t][nq]][:], func=AF.Square, accum_out=stat[:, sc:sc + 1]),
                                reads=[bk(yb[t][nq])], writes=['sq_junk', ('stat', sc)])
                            if ch == 0:
                                S.op('act', lambda e, t=t, nq=nq, yb=yb: e.copy(out=ystash[:, t, nq * 512:(nq + 1) * 512],
                                                                              in_=banks[yb[t][nq]][:]),
                                     reads=[bk(yb[t][nq])], writes=[('ystash', t, nq)] + hkeys)
                        if ch == 0:
                            continue
                        S.op('dve', lambda e, t=t: e.tensor_reduce(out=stat[:, 12:13], in_=stat[:, 16 + t * 4:20 + t * 4],
                                                                    axis=mybir.AxisListType.X, op=ALU.add),
                             reads=[('stat', 16 + t * 4 + i) for i in range(4)], writes=[('stat', 12)])
                        S.op('dve', lambda e: e.tensor_scalar(out=stat[:, 13:14], in0=stat[:, 12:13], scalar1=1.0 / D, scalar2=EPS,
                                                               op0=ALU.mult, op1=ALU.add), reads=[('stat', 12)], writes=[('stat', 13)])
                        S.op('act', lambda e: e.activation(out=stat[:, 15:16], in_=stat[:, 13:14], func=AF.Ln),
                             reads=[('stat', 13)], writes=[('stat', 15)])
                        S.op('act', lambda e: e.activation(out=stat[:, 14:15], in_=stat[:, 15:16], func=AF.Exp, scale=-0.5),
                             reads=[('stat', 15)], writes=[('stat', 14)])
                        for q in range(4):
                            s = q % 2
                            if q < 2:
                                src, rk = ystash[:, t, q * 512:(q + 1) * 512], [('ystash', t, q)] + hkeys
                            else:
                                src, rk = banks[yb[t][q - 2]][:], [bk(yb[t][q - 2])]
                            S.op('dve', lambda e, q=q, s=s, src=src: e.scalar_tensor_tensor(
                                out=tmp_f[:, s, :], in0=src, scalar=stat[:, 14:15], in1=Gbc[:, 1, q * 512:(q + 1) * 512],
                                op0=ALU.mult, op1=ALU.mult), reads=rk + [('stat', 14), ('Gbc', 1)], writes=[('tmp_f', s)])
                            S.op('dve', lambda e, q=q, s=s, tt=tt: e.tensor_tensor(
                                out=xres[:, tt, q * 512:(q + 1) * 512], in0=xres[:, tt, q * 512:(q + 1) * 512], in1=tmp_f[:, s, :],
                                op=ALU.add), reads=[('x', tt), ('tmp_f', s)], writes=[('x', tt)])
        S.barrier()

    def mixer_a(l):
        lam_init = 0.8 - 0.6 * math.exp(-0.3 * l)
        with ExitStack() as st:
            oT = sb("oT", [128, 16, NT], BF16, st)
            with ExitStack() as st2:
                lam_s = sb("lam_s", [128, 4, 16], F32, st2)
                gsub = sb("gsub", [128, 128], F32, st2)
                st_lam = ExitStack()
                lam_t = sb("lam_t", [128, 4096], F32, st_lam)
                lam_p = sb("lam_p", [128, 2, 1024], F32, st_lam)
                dma('sp', lam_t[:], a_lam[:, :], writes=['lam_t'])
                dma('sp', gsub[:], a_sub[:, :], writes=['gsub0'])
                for i in range(2):
                    S.op('dve', lambda e, i=i: e.tensor_tensor(out=lam_p[:, i, :], in0=lam_t[:, (2 * i) * 1024:(2 * i + 1) * 1024],
                                                                in1=lam_t[:, (2 * i + 1) * 1024:(2 * i + 2) * 1024], op=ALU.mult),
                         reads=['lam_t'], writes=[('lam_p', i)])
                    S.op('dve', lambda e, i=i: e.tensor_reduce(out=lam_s[:, i, :], in_=lam_p[:, i, :].rearrange("p (h d) -> p h d", d=64),
                                                                axis=mybir.AxisListType.X, op=ALU.add),
                         reads=[('lam_p', i)], writes=[('lam_s', i)])
                    S.op('act', lambda e, i=i: e.activation(out=lam_s[:, i, :], in_=lam_s[:, i, :], func=AF.Exp),
                         reads=[('lam_s', i)], writes=[('lam_s', i)])
                S.op('dve', lambda e: e.tensor_tensor(out=lam_s[:, 2, :], in0=lam_s[:, 0, :], in1=lam_s[:, 1, :], op=ALU.subtract),
                     reads=[('lam_s', 0), ('lam_s', 1)], writes=[('lam_s', 2)])
                S.op('dve', lambda e: e.tensor_scalar(out=lam_s[:, 3, :], in0=lam_s[:, 2, :], scalar1=lam_init, scalar2=-1.0,
                                                       op0=ALU.add, op1=ALU.mult), reads=[('lam_s', 2)], writes=[('lam_s', 3)])
                S.op('dve', lambda e: e.tensor_scalar(out=gsub[:], in0=gsub[:], scalar1=1.0 - lam_init, scalar2=None, op0=ALU.mult),
                     reads=['gsub0'], writes=['gsub'])
                S.barrier()
                st_lam.close()
                hT = sb("hT", [128, 16, NT], BF16, st2)
                wh = [sb("wh%d" % i, [128, 16, 3, 128], BF16, st2) for i in range(1)]
                cosq = sb("cosq", [128, NT], F32, st2)
                sinq = sb("sinq", [128, NT], F32, st2)
                cosk = sb("cosk", [128, NT], F32, st2)
                sink_ = sb("sink", [128, NT], F32, st2)
                rot_f = sb("rot_f", [128, 128], F32, st2)
                rot_b = sb("rot_b", [128, 128], BF16, st2)
                maskA = sb("maskA", [128, 48], F32, st2)
                QT = sb("QT", [128, NT], BF16, st2)
                KT = sb("KT", [128, NT + 512], BF16, st2)
                raw = sb("raw", [128, 1, 512], BF16, st2)
                t1 = sb("t1", [128, 1, 512], F32, st2)
                t2 = sb("t2", [128, 1, 512], F32, st2)
                Vaug = sb("Vaug", [128, 12, 132], BF16, st2)
                kvo = sb("kvo", [128, 2, 2, 128], F32, st2)
                cks = sb("cks", [128, 4, 128], BF16, st2)
                Et = sb("Et", [128, 3, 256], BF16, st2)
                osb = sb("osb", [128, 2, 128], F32, st2)
                onb = sb("onb", [128, 2, 128], BF16, st2)
                sst = sb("sst", [128, 16], F32, st2)

                tabk = ['cosq', 'sinq', 'cosk', 'sink']
                for t_, d_, k_ in ((cosq, a_cosq, 'cosq'), (sinq, a_sinq, 'sinq'), (cosk, a_cosk, 'cosk'), (sink_, a_sink, 'sink')):
                    dma('sp', t_[:], d_[:, :], writes=[k_])
                dma('sp', rot_f[:], rotm[:, :], writes=['rot_f'])
                dma('sp', maskA[:], a_mask[:, :], writes=['maskA'])
                S.op('dve', lambda e: e.tensor_copy(out=rot_b[:], in_=rot_f[:]), reads=['rot_f'], writes=['rot_b'])
                S.op('dve', lambda e: e.memset(Vaug[:, :, 128:132], 1.0), writes=['Vaug_ones'])

                adaln_in(0, list(range(8)), hT, 'hT')

                def proj_fm_rope(w, wk, slot, dst, ct, st_, ck_, sk_):
                    for hf in range(2):
                        b = next_bank()
                        for kc in range(16):
                            S.op('pe', lambda e, kc=kc, b=b, hf=hf: e.matmul(
                                banks[b][:], lhsT=w[:, kc, slot, :], rhs=hT[:, kc, hf * 512:(hf + 1) * 512],
                                start=(kc == 0), stop=(kc == 15)), reads=[wk, ('hT', kc)], writes=[bk(b)])
                        S.op('act', lambda e, b=b, hf=hf: e.copy(out=raw[:, 0, :], in_=banks[b][:]),
                             reads=[bk(b)], writes=[('raw', 0)])
                        b2 = next_bank()
                        S.op('pe', lambda e, b2=b2, hf=hf: e.matmul(banks[b2][:], lhsT=rot_b[:], rhs=raw[:, 0, :],
                                                                    start=True, stop=True),
                             reads=['rot_b', ('raw', 0)], writes=[bk(b2)])
                        S.op('dve', lambda e, b=b, hf=hf: e.tensor_tensor(out=t1[:, 0, :], in0=banks[b][:],
                                                                           in1=ct[:, hf * 512:(hf + 1) * 512], op=ALU.mult),
                             reads=[bk(b), ck_], writes=[('t1', 0)])
                        S.op('dve', lambda e, b2=b2, hf=hf: e.tensor_tensor(out=t2[:, 0, :], in0=banks[b2][:],
                                                                             in1=st_[:, hf * 512:(hf + 1) * 512], op=ALU.mult),
                             reads=[bk(b2), sk_], writes=[('t2', 0)])
                        S.op('dve', lambda e, hf=hf: e.tensor_tensor(out=dst[:, hf * 512:(hf + 1) * 512], in0=t1[:, 0, :],
                                                                      in1=t2[:, 0, :], op=ALU.add),
                             reads=[('t1', 0), ('t2', 0)], writes=[('dstrope', id(dst))])

                for h in range(16):
                    w = wh[0]
                    wk = ('wh', 0)
                    for sl in range(3):
                        src = a_w_in[:, sl * D + h * 128:sl * D + (h + 1) * 128].rearrange("(c p) n -> p c n", p=128)
                        dma('pool', w[:, :, sl, :], src, writes=[wk])
                    proj_fm_rope(w, wk, 0, QT, cosq, sinq, tabk[0], tabk[1])
                    proj_fm_rope(w, wk, 1, KT, cosk, sink_, tabk[2], tabk[3])
                    for tt in range(8):
                        b = next_bank()
                        for kc in range(16):
                            S.op('pe', lambda e, kc=kc, b=b, tt=tt: e.matmul(
                                banks[b][:, 0:256], lhsT=hT[:, kc, tt * 128:(tt + 1) * 128],
                                rhs=w[:, kc, 1:3, :].rearrange("p a b -> p (a b)"),
                                start=(kc == 0), stop=(kc == 15)), reads=[wk, ('hT', kc)], writes=[bk(b)])
                        S.op('act', lambda e, b=b, tt=tt: e.copy(out=kvo[:, :, tt % 2, :],
                                                                in_=banks[b][:, 0:256].rearrange("p (a b) -> p a b", a=2)),
                             reads=[bk(b)], writes=[('kvo', tt % 2)])
                        S.op('dve', lambda e, b=b, tt=tt: e.tensor_copy(out=Vaug[:, tt, 0:128], in_=banks[b][:, 128:256]),
                             reads=[bk(b)], writes=[('Vaug', tt)])
                        dma('sp', a_ko[tt * 128:(tt + 1) * 128, h * 128:(h + 1) * 128], kvo[:, 0, tt % 2, :],
                            reads=[('kvo', tt % 2)], pool='st')
                        dma('sp', a_vo[tt * 128:(tt + 1) * 128, h * 128:(h + 1) * 128], kvo[:, 1, tt % 2, :],
                            reads=[('kvo', tt % 2)], pool='st')
                    dma('pool', cks[:], a_ck[:, h * 128:(h + 1) * 128].rearrange("(t p) d -> p t d", p=128), writes=['cks'])
                    dma('pool', Vaug[:, 8:12, 0:128], a_cv[:, h * 128:(h + 1) * 128].rearrange("(t p) d -> p t d", p=128),
                        writes=[('Vaug', 8 + i) for i in range(4)])
                    b = next_bank()
                    pb = banks[b][:].bitcast(BF16)
                    for i in range(4):
                        S.op('pe', lambda e, i=i, pb=pb: e.transpose(out=pb[:, i * 128:(i + 1) * 128], in_=cks[:, i, :],
                                                                      identity=ident_b[:]),
                             reads=['cks', 'ident_b'], writes=[bk(b)])
                    S.op('act', lambda e, pb=pb: e.copy(out=KT[:, NT:NT + 512], in_=pb[:, 0:512]),
                         reads=[bk(b)], writes=['KTctx'])
                    for qp in range(4):
                        ob = [[next_bank() for _ in range(2)] for _ in range(2)]
                        its = [(j, kc) for j in range(2) for kc in range(12)]
                        sbk = {}

                        def st_score(n, qp=qp, ob=ob, its=its, sbk=sbk):
                            j, kc = its[n]
                            pr = slice(j * 64, (j + 1) * 64)
                            b = next_bank()
                            while any(b in r for r in ob):
                                b = next_bank()
                            sbk[n] = b
                            S.op('pe', lambda e: e.matmul(
                                banks[b][:, 0:256], lhsT=KT[pr, kc * 128:(kc + 1) * 128], rhs=QT[pr, qp * 256:(qp + 1) * 256],
                                start=True, stop=True),
                                reads=[('dstrope', id(KT)), ('dstrope', id(QT)), 'KTctx'], writes=[bk(b)])

                        def st_exp(n, qp=qp, its=its, sbk=sbk):
                            j, kc = its[n]
                            b = sbk[n]
                            es_ = n % 3
                            S.op('act', lambda e: e.activation(
                                out=Et[:, es_, :], in_=banks[b][:, 0:256], func=AF.Exp,
                                bias=maskA[:, kc * 4 + qp:kc * 4 + qp + 1], scale=1.0),
                                reads=[bk(b), 'maskA'], writes=[('Et', es_)])

                        def st_pv(n, ob=ob, its=its):
                            j, kc = its[n]
                            es_ = n % 3
                            for qq in range(2):
                                S.op('pe', lambda e: e.matmul(
                                    banks[ob[j][qq]][:, 0:129], lhsT=Et[:, es_, qq * 128:(qq + 1) * 128], rhs=Vaug[:, kc, 0:129],
                                    start=(kc == 0), stop=(kc == 11)),
                                    reads=[('Et', es_), ('Vaug', kc), 'Vaug_ones'], writes=[bk(ob[j][qq])])

                        LA = 2
                        for t in range(len(its) + LA):
                            if t < len(its):
                                st_score(t)
                            if 1 <= t <= len(its):
                                st_exp(t - 1)
                            if t >= LA:
                                st_pv(t - LA)
                        for qq in range(2):
                            tt = qp * 2 + qq
                            o1, o2 = banks[ob[0][qq]], banks[ob[1][qq]]
                            S.op('dve', lambda e, o1=o1: e.reciprocal(out=sst[:, 0:1], in_=o1[:, 128:129]),
                                 reads=[bk(ob[0][qq])], writes=[('sst', 0)])
                            S.op('dve', lambda e, o2=o2: e.reciprocal(out=sst[:, 1:2], in_=o2[:, 128:129]),
                                 reads=[bk(ob[1][qq])], writes=[('sst', 1)])
                            S.op('dve', lambda e, h=h: e.tensor_tensor(out=sst[:, 2:3], in0=sst[:, 1:2], in1=lam_s[:, 3, h:h + 1],
                                                                        op=ALU.mult),
                                 reads=[('sst', 1), ('lam_s', 3)], writes=[('sst', 2)])
                            S.op('dve', lambda e, o1=o1, qq=qq: e.tensor_scalar(out=osb[:, qq, :], in0=o1[:, 0:128], scalar1=sst[:, 0:1],
                                                                                 scalar2=None, op0=ALU.mult),
                                 reads=[bk(ob[0][qq]), ('sst', 0)], writes=[('osb', qq)])
                            S.op('dve', lambda e, o2=o2, qq=qq: e.scalar_tensor_tensor(
                                out=osb[:, qq, :], in0=o2[:, 0:128], scalar=sst[:, 2:3], in1=osb[:, qq, :], op0=ALU.mult, op1=ALU.add),
                                reads=[bk(ob[1][qq]), ('sst', 2), ('osb', qq)], writes=[('osb', qq)])
                            S.op('act', lambda e, qq=qq: e.activation(out=sq_junk[:, 0:128], in_=osb[:, qq, :], func=AF.Square,
                                                                       accum_out=sst[:, 3:4]),
                                 reads=[('osb', qq)], writes=['sq_junk', ('sst', 3)])
                            S.op('dve', lambda e: e.tensor_scalar(out=sst[:, 4:5], in0=sst[:, 3:4], scalar1=1.0 / 128, scalar2=EPS,
                                                                   op0=ALU.mult, op1=ALU.add), reads=[('sst', 3)], writes=[('sst', 4)])
                            S.op('act', lambda e: e.activation(out=sst[:, 6:7], in_=sst[:, 4:5], func=AF.Ln),
                                 reads=[('sst', 4)], writes=[('sst', 6)])
                            S.op('act', lambda e: e.activation(out=sst[:, 5:6], in_=sst[:, 6:7], func=AF.Exp, scale=-0.5),
                                 reads=[('sst', 6)], writes=[('sst', 5)])
                            S.op('dve', lambda e, qq=qq: e.scalar_tensor_tensor(
                                out=onb[:, qq, :], in0=osb[:, qq, :], scalar=sst[:, 5:6], in1=gsub[:], op0=ALU.mult, op1=ALU.mult),
                                reads=[('osb', qq), ('sst', 5), 'gsub'], writes=[('onb', qq)])
                            b = next_bank()
                            pb = banks[b][:].bitcast(BF16)
                            S.op('pe', lambda e, qq=qq, pb=pb: e.transpose(out=pb[:, 0:128], in_=onb[:, qq, :], identity=ident_b[:]),
                                 reads=[('onb', qq), 'ident_b'], writes=[bk(b)])
                            S.op('act', lambda e, pb=pb, h=h, tt=tt: e.copy(out=oT[:, h, tt * 128:(tt + 1) * 128], in_=pb[:, 0:128]),
                                 reads=[bk(b)], writes=[('oT', h)])
            S.barrier()
            out_proj(a_w_out, oT, st)
        S.barrier()

    NA_L = [[0, 1, 2, 3], [0, 1, 2, 3], [0, 1, 2, 3, 4], [1, 2, 3, 4, 5], [2, 3, 4, 5, 6], [3, 4, 5, 6, 7], [4, 5, 6, 7], [4, 5, 6, 7]]

    def mixer_cd(l, kind):
        isd = (kind == 'd')
        w_in_d, w_out_d = (d_w_in, d_w_out) if isd else (c_w_in, c_w_out)
        ck_d, cv_d = (d_ck, d_cv) if isd else (c_ck, c_cv)
        k_out, v_out = (d_ko, d_vo) if isd else (c_ko, c_vo)
        dv = 64 if isd else 128
        nslot = 4 if isd else 3
        with ExitStack() as st:
            oT = sb("oT", [128, 16, NT], BF16, st)
            with ExitStack() as st2:
                hT = sb("hT", [128, 16, NT], BF16, st2)
                w = sb("wcd", [128, 16, nslot, 128], BF16, st2)
                wk = 'wcd'
                QT = sb("QT", [128, 2 if isd else 1, NT], BF16, st2)
                KT = sb("KT", [128, NT + 512], BF16, st2)
                Vaug = sb("Vaug", [128, 12, dv + 4], BF16, st2)
                kvo = sb("kvo", [128, 2, 2, dv], F32, st2)
                cks = sb("cks", [128, 4, 128], BF16, st2)
                Et = sb("Et", [128, 4, 128], BF16, st2)
                sbs = sb("sbs", [128, 4, 128], F32, st2)
                onb = sb("onb", [128, 2, 256 if isd else 128], BF16, st2)
                sst = sb("sst", [128, 16], F32, st2)
                ctxb = sb("ctxb_s", [128, 1], F32, st2)
                dma('sp', ctxb[:], ctxb_d[:, :], writes=['ctxb'])
                if isd:
                    cosk = sb("cosk", [128, NT], F32, st2)
                    sink_ = sb("sink", [128, NT], F32, st2)
                    rot_f = sb("rot_f", [128, 128], F32, st2)
                    rot_b = sb("rot_b", [128, 128], BF16, st2)
                    raw = sb("raw", [128, 512], BF16, st2)
                    maskD = sb("maskD", [128, 2, 8, 128], F32, st2)
                    snk = sb("snk", [128, 32], F32, st2)
                    dma('sp', cosk[:], a_cosk[:, :], writes=['cosk'])
                    dma('sp', sink_[:], a_sink[:, :], writes=['sink'])
                    dma('sp', rot_f[:], rotm[:, :], writes=['rot_f'])
                    dma('sp', maskD[:], d_mask[:, :].rearrange("p (a b c) -> p a b c", a=2, b=8), writes=['maskD'])
                    dma('sp', snk[:], d_snk[:, :], writes=['snk0'])
                    S.op('dve', lambda e: e.tensor_copy(out=rot_b[:], in_=rot_f[:]), reads=['rot_f'], writes=['rot_b'])
                    S.op('act', lambda e: e.activation(out=snk[:], in_=snk[:], func=AF.Exp), reads=['snk0'], writes=['snk'])
                else:
                    biasC = sb("biasC", [128, 2, 5, 128], F32, st2)
                S.op('dve', lambda e: e.memset(Vaug[:, :, dv:dv + 4], 1.0), writes=['Vaug_ones'])
                adaln_in(0, list(range(8)), hT, 'hT')

                def proj_fm(slot, dst, dkey, rope, scale):
                    for hf in range(2):
                        b = next_bank()
                        for kc in range(16):
                            S.op('pe', lambda e, kc=kc, b=b, hf=hf: e.matmul(
                                banks[b][:], lhsT=w[:, kc, slot, :], rhs=hT[:, kc, hf * 512:(hf + 1) * 512],
                                start=(kc == 0), stop=(kc == 15)), reads=[wk, ('hT', kc)], writes=[bk(b)])
                        dsl = dst[:, hf * 512:(hf + 1) * 512]
                        if not rope:
                            S.op('act', lambda e, b=b: e.mul(out=dsl, in_=banks[b][:], mul=scale), reads=[bk(b)], writes=[dkey])
                            continue
                        S.op('act', lambda e, b=b: e.copy(out=raw[:], in_=banks[b][:]), reads=[bk(b)], writes=['raw'])
                        b2 = next_bank()
                        S.op('pe', lambda e, b2=b2: e.matmul(banks[b2][:], lhsT=rot_b[:], rhs=raw[:], start=True, stop=True),
                             reads=['rot_b', 'raw'], writes=[bk(b2)])
                        S.op('dve', lambda e, b=b, hf=hf: e.tensor_tensor(out=tmp_f[:, 0, :], in0=banks[b][:],
                                                                           in1=cosk[:, hf * 512:(hf + 1) * 512], op=ALU.mult),
                             reads=[bk(b), 'cosk'], writes=[('tmp_f', 0)])
                        S.op('dve', lambda e, b2=b2, hf=hf: e.tensor_tensor(out=tmp_f[:, 1, :], in0=banks[b2][:],
                                                                             in1=sink_[:, hf * 512:(hf + 1) * 512], op=ALU.mult),
                             reads=[bk(b2), 'sink'], writes=[('tmp_f', 1)])
                        S.op('dve', lambda e: e.tensor_tensor(out=dsl, in0=tmp_f[:, 0, :], in1=tmp_f[:, 1, :], op=ALU.add),
                             reads=[('tmp_f', 0), ('tmp_f', 1)], writes=[dkey])

                ngrp = 8 if isd else 16
                for g in range(ngrp):
                    rs = lambda a: a.rearrange("(c p) n -> p c n", p=128)
                    if isd:
                        for ci in range(2):
                            dma('pool', w[:, :, ci, :], rs(w_in_d[:, g * 256 + ci * 128:g * 256 + (ci + 1) * 128]), writes=[wk])
                        kcols = rs(w_in_d[:, 2048 + g * 64:2048 + (g + 1) * 64])
                        vcols = rs(w_in_d[:, 2560 + g * 64:2560 + (g + 1) * 64])
                        dma('pool', w[:, :, 2, 0:64], kcols, writes=[wk])
                        dma('pool', w[:, :, 2, 64:128], vcols, writes=[wk])
                        dma('pool', w[:, :, 3, 0:64], kcols, writes=[wk])
                        dma('pool', w[:, :, 3, 64:128], kcols, writes=[wk])
                        proj_fm(0, QT[:, 0, :], 'QT', True, 1.0)
                        proj_fm(1, QT[:, 1, :], 'QT', True, 1.0)
                        proj_fm(3, KT[:, 0:NT], 'KT', True, 1.0)
                        kvslot = w[:, :, 2, :]
                        nkv = 128
                    else:
                        for sl in range(3):
                            dma('pool', w[:, :, sl, :], rs(w_in_d[:, sl * D + g * 128:sl * D + (g + 1) * 128]), writes=[wk])
                        proj_fm(0, QT[:, 0, :], 'QT', False, 128 ** -0.5)
                        proj_fm(1, KT[:, 0:NT], 'KT', False, 1.0)
                        kvslot = w[:, :, 1:3, :].rearrange("p c a b -> p c (a b)")
                        nkv = 256
                    for tt in range(8):
                        b = next_bank()
                        for kc in range(16):
                            S.op('pe', lambda e, kc=kc, b=b, tt=tt: e.matmul(
                                banks[b][:, 0:nkv], lhsT=hT[:, kc, tt * 128:(tt + 1) * 128], rhs=kvslot[:, kc, :],
                                start=(kc == 0), stop=(kc == 15)), reads=[wk, ('hT', kc)], writes=[bk(b)])
                        S.op('act', lambda e, b=b, tt=tt: e.copy(out=kvo[:, :, tt % 2, :],
                                                                in_=banks[b][:, 0:nkv].rearrange("p (a b) -> p a b", a=2)),
                             reads=[bk(b)], writes=[('kvo', tt % 2)])
                        S.op('dve', lambda e, b=b, tt=tt: e.tensor_copy(out=Vaug[:, tt, 0:dv], in_=banks[b][:, dv:2 * dv]),
                             reads=[bk(b)], writes=[('Vaug', tt)])
                        dma('sp', k_out[tt * 128:(tt + 1) * 128, g * dv:(g + 1) * dv], kvo[:, 0, tt % 2, :],
                            reads=[('kvo', tt % 2)], pool='st')
                        dma('sp', v_out[tt * 128:(tt + 1) * 128, g * dv:(g + 1) * dv], kvo[:, 1, tt % 2, :],
                            reads=[('kvo', tt % 2)], pool='st')
                    csrc = ck_d[:, g * dv:(g + 1) * dv].rearrange("(t p) d -> p t d", p=128)
                    if isd:
                        dma('pool', cks[:, :, 0:64], csrc, writes=['cks'])
                        dma('pool', cks[:, :, 64:128], csrc, writes=['cks'])
                    else:
                        dma('pool', cks[:], csrc, writes=['cks'])
                    dma('pool', Vaug[:, 8:12, 0:dv], cv_d[:, g * dv:(g + 1) * dv].rearrange("(t p) d -> p t d", p=128),
                        writes=[('Vaug', 8 + i) for i in range(4)])
                    b = next_bank()
                    pb = banks[b][:].bitcast(BF16)
                    for i in range(4):
                        S.op('pe', lambda e, i=i, pb=pb: e.transpose(out=pb[:, i * 128:(i + 1) * 128], in_=cks[:, i, :],
                                                                      identity=ident_b[:]),
                             reads=['cks', 'ident_b'], writes=[bk(b)])
                    S.op('act', lambda e, pb=pb: e.copy(out=KT[:, NT:NT + 512], in_=pb[:, 0:512]), reads=[bk(b)], writes=['KTctx'])
                    items = []
                    for qt in range(8):
                        if isd:
                            chunks = [(qt + r, ri) for r, ri in ((-1, 0), (0, None), (1, 1)) if 0 <= qt + r < 8]
                        else:
                            chunks = [(kc, si) for si, kc in enumerate(NA_L[qt])]
                        chunks += [(8 + i, 'ctx') for i in range(4)]
                        for gi in range(4 if isd else 1):
                            for n, (kc, mk) in enumerate(chunks):
                                items.append((qt, gi, n, kc, mk, len(chunks)))
                    sbk = {}
                    obk = {}
                    live_ob = set()
                    esc = 0.125 if isd else 1.0
                    ngi = 4 if isd else 1

                    def prq(gi):
                        return slice((gi % 2) * 64, (gi % 2 + 1) * 64) if isd else slice(0, 128)

                    def st_score(t):
                        qt, gi, n, kc, mk, nch = items[t]
                        if n == 0:
                            if gi == 0 and not isd:
                                bslot = (g * 8 + qt) % 2
                                dma('sp', biasC[:, bslot, :, :], c_bias[(g * 8 + qt) * 128:(g * 8 + qt + 1) * 128, :].rearrange(
                                    "p (a b) -> p a b", a=5), writes=[('biasC', bslot)])
                            ob = next_bank()
                            while ob in live_ob:
                                ob = next_bank()
                            obk[(qt, gi)] = ob
                            live_ob.add(ob)
                        b = next_bank()
                        while b in live_ob:
                            b = next_bank()
                        sbk[t] = b
                        pr = prq(gi)
                        qsrc = QT[pr, gi // 2, qt * 128:(qt + 1) * 128]
                        S.op('pe', lambda e: e.matmul(
                            banks[b][:, 0:128], lhsT=KT[pr, kc * 128:(kc + 1) * 128], rhs=qsrc, start=True, stop=True),
                            reads=['KT', 'QT', 'KTctx'], writes=[bk(b)])

                    def st_exp(t):
                        qt, gi, n, kc, mk, nch = items[t]
                        b = sbk[t]
                        es_ = t % 4
                        if mk == 'ctx':
                            S.op('act', lambda e: e.activation(out=Et[:, es_, :], in_=banks[b][:, 0:128], func=AF.Exp,
                                                               bias=ctxb[:, 0:1], scale=esc),
                                 reads=[bk(b), 'ctxb'], writes=[('Et', es_)])
                        elif mk is None:
                            S.op('act', lambda e: e.activation(out=Et[:, es_, :], in_=banks[b][:, 0:128], func=AF.Exp, scale=esc),
                                 reads=[bk(b)], writes=[('Et', es_)])
                        else:
                            bslot = (g * 8 + qt) % 2
                            msrc = maskD[:, mk, qt, :] if isd else biasC[:, bslot, mk, :]
                            mkey = 'maskD' if isd else ('biasC', bslot)
                            S.op('dve', lambda e: e.tensor_tensor(out=sbs[:, es_, :], in0=banks[b][:, 0:128], in1=msrc, op=ALU.add),
                                 reads=[bk(b), mkey], writes=[('sbs', es_)])
                            S.op('act', lambda e: e.activation(out=Et[:, es_, :], in_=sbs[:, es_, :], func=AF.Exp, scale=esc),
                                 reads=[('sbs', es_)], writes=[('Et', es_)])

                    def st_pv(t):
                        qt, gi, n, kc, mk, nch = items[t]
                        es_ = t % 4
                        ob = obk[(qt, gi)]
                        S.op('pe', lambda e: e.matmul(
                            banks[ob][:, 0:dv + 1], lhsT=Et[:, es_, :], rhs=Vaug[:, kc, 0:dv + 1],
                            start=(n == 0), stop=(n == nch - 1)),
                            reads=[('Et', es_), ('Vaug', kc), 'Vaug_ones'], writes=[bk(ob)])
                        if n != nch - 1:
                            return
                        os_ = qt % 2
                        if isd:
                            hq = g * 4 + gi
                            S.op('dve', lambda e: e.tensor_tensor(out=sst[:, 0:1], in0=banks[ob][:, dv:dv + 1], in1=snk[:, hq:hq + 1],
                                                                  op=ALU.add), reads=[bk(ob), 'snk'], writes=[('sst', 0)])
                            S.op('dve', lambda e: e.reciprocal(out=sst[:, 1:2], in_=sst[:, 0:1]), reads=[('sst', 0)], writes=[('sst', 1)])
                        else:
                            S.op('dve', lambda e: e.reciprocal(out=sst[:, 1:2], in_=banks[ob][:, dv:dv + 1]),
                                 reads=[bk(ob)], writes=[('sst', 1)])
                        S.op('dve', lambda e: e.tensor_scalar(out=onb[:, os_, gi * dv:(gi + 1) * dv], in0=banks[ob][:, 0:dv],
                                                               scalar1=sst[:, 1:2], scalar2=None, op0=ALU.mult),
                             reads=[bk(ob), ('sst', 1)], writes=[('onb', os_)])
                        live_ob.discard(ob)
                        if gi != ngi - 1:
                            return
                        for ci in range(2 if isd else 1):
                            b = next_bank()
                            pb = banks[b][:].bitcast(BF16)
                            S.op('pe', lambda e: e.transpose(out=pb[:, 0:128], in_=onb[:, os_, ci * 128:(ci + 1) * 128],
                                                             identity=ident_b[:]),
                                 reads=[('onb', os_), 'ident_b'], writes=[bk(b)])
                            och = (2 * g + ci) if isd else g
                            S.op('act', lambda e: e.copy(out=oT[:, och, qt * 128:(qt + 1) * 128], in_=pb[:, 0:128]),
                                 reads=[bk(b)], writes=[('oT', och)])

                    LA = 2
                    for t in range(len(items) + LA):
                        if t < len(items):
                            st_score(t)
                        if 1 <= t <= len(items):
                            st_exp(t - 1)
                        if t >= LA:
                            st_pv(t - LA)
            S.barrier()
            out_proj(w_out_d, oT, st)
        S.barrier()

    def mixer_b(l):
        with ExitStack() as st:
            oT = sb("oT", [128, 16, NT], BF16, st)
            with ExitStack() as st2:
                hT = sb("hT", [128, 16, NT], BF16, st2)
                wt = [sb("wb%d" % i, [128, 16, 256], BF16, st2) for i in range(1)]
                wg = sb("wg", [128, 16, 32], BF16, st2)
                gb = sb("gb", [128, 32], F32, st2)
                nrm = sb("nrm", [128, 256], F32, st2)
                keep = sb("keep", [128, 1], F32, st2)
                m0e = sb("m0e", [128, 16], F32, st2)
                Um = sb("Um", [128, 2, 128], F32, st2)
                Mk = sb("Mk", [128, 2, 128], F32, st2)
                G = sb("G", [128, 256], F32, st2)
                LF = sb("LF", [128, 256], F32, st2)
                BB = tmp_f[:, 1, 0:256]
                AB = sb("AB", [128, 128], F32, st2)
                EE = sb("EE", [128, 128], F32, st2)
                WL = sb("WL", [128, 128], F32, st2)
                DT = sb("DT", [128, 128], F32, st2)
                BT8 = sb("BT8", [8, 16], F32, st2)
                EM8 = sb("EM8", [8, 16], F32, st2)
                M8 = sb("M8", [8, 8], F32, st2)
                T8 = sb("T8", [8, 2], F32, st2)
                SC8 = sb("SC8", [8, 8], F32, st2)
                d8 = sb("d8", [8, 8, 8], F32, st2)
                SCB = sb("SCB", [128, 64], F32, st2)
                QT = sb("QT", [128, NT], BF16, st2)
                KT = sb("KT", [128, NT], BF16, st2)
                Ktm = sb("Ktm", [128, 8, 128], BF16, st2)
                Vaug = sb("Vaug", [128, 8, 260], BF16, st2)
                SIGO = sb("SIGO", [128, 2, 256], F32, st2)
                Hs = sb("Hs", [128, 8, 256], F32, st2)
                S32 = sb("S32", [128, 2, 257], F32, st2)
                Sbf = sb("Sbf", [128, 2, 260], BF16, st2)
                outS = sb("outS", [128, 2, 257], F32, st2)
                lfbc = sb("lfbc", [128, 2, 128], F32, st2)
                tD = sb("tD", [128, 2, 128], F32, st2)
                Dt = sb("Dt", [128, 2, 128], F32, st2)
                EB = sb("EB", [128, 2, 128], F32, st2)
                Qs = sb("Qs", [128, 2, 128], BF16, st2)
                PT = sb("PT", [128, 2, 128], BF16, st2)
                Kw = sb("Kw", [128, 2, 128], BF16, st2)
                rr = sb("rr", [128, 2, 4], F32, st2)
                hn = tmp_f[:, 0, 0:256]
                onb = sb("onb", [128, 2, 256], BF16, st2)
                sst = sb("sst", [128, 8], F32, st2)

                dma('sp', gb[:], b_gb[:, :], writes=['gb'])
                dma('sp', keep[:], b_keep[:, :], writes=['keep'])
                dma('sp', m0e[:], b_m0[:, :], writes=['m0raw'])
                dma('sp', Um[:], b_um[:, :].rearrange("p (a b) -> p a b", a=2), writes=['Um'])
                dma('sp', Mk[:], b_mk[:, :].rearrange("p (a b) -> p a b", a=2), writes=['Mk'])
                dma('pool', wg[:], b_w_in[:, 6144:6176].rearrange("(c p) n -> p c n", p=128), writes=['wg'])
                S.op('act', lambda e: e.activation(out=m0e[:], in_=m0e[:], func=AF.Exp), reads=['m0raw'], writes=['m0e'])
                S.op('dve', lambda e: e.memset(Vaug[:, :, 256:260], 1.0), writes=['Vaug_ones'])
                adaln_in(0, list(range(8)), hT, 'hT')

                bg = next_bank()
                for tt in range(8):
                    for kc in range(16):
                        S.op('pe', lambda e, tt=tt, kc=kc: e.matmul(
                            banks[bg][:, tt * 32:(tt + 1) * 32], lhsT=hT[:, kc, tt * 128:(tt + 1) * 128], rhs=wg[:, kc, :],
                            start=(kc == 0), stop=(kc == 15)), reads=['wg', ('hT', kc)], writes=[bk(bg)])
                for tt in range(8):
                    S.op('dve', lambda e, tt=tt: e.tensor_tensor(out=G[:, tt * 32:(tt + 1) * 32], in0=banks[bg][:, tt * 32:(tt + 1) * 32],
                                                                  in1=gb[:], op=ALU.add), reads=[bk(bg), 'gb'], writes=['G'])
                S.op('act', lambda e: e.activation(out=LF[:], in_=G[:], func=AF.Exp, scale=-1.0), reads=['G'], writes=['LF'])
                S.op('dve', lambda e: e.tensor_scalar(out=LF[:], in0=LF[:], scalar1=1.0, scalar2=None, op0=ALU.add),
                     reads=['LF'], writes=['LF'])
                S.op('act', lambda e: e.activation(out=LF[:], in_=LF[:], func=AF.Ln), reads=['LF'], writes=['LF'])
                S.op('dve', lambda e: e.tensor_scalar(out=LF[:], in0=LF[:], scalar1=-1.0, scalar2=None, op0=ALU.mult),
                     reads=['LF'], writes=['LF'])
                bb_ = next_bank()
                for c in range(8):
                    for d in range(2):
                        cd = c * 2 + d
                        lf_cd = LF[:, c * 32 + (2 * d + 1) * 8:c * 32 + (2 * d + 1) * 8 + 8]
                        S.op('pe', lambda e, cd=cd, d=d, lf_cd=lf_cd: e.matmul(
                            banks[bb_][:, cd * 16:cd * 16 + 8], lhsT=Um[:, d, :], rhs=lf_cd, start=True, stop=True),
                            reads=['Um', 'LF'], writes=[bk(bb_)])
                        S.op('pe', lambda e, cd=cd, lf_cd=lf_cd: e.matmul(
                            banks[bb_][:, cd * 16 + 8:cd * 16 + 16], lhsT=ones_f[:], rhs=lf_cd, start=True, stop=True),
                            reads=['ones_f', 'LF'], writes=[bk(bb_)])
                S.op('act', lambda e: e.copy(out=BB, in_=banks[bb_][:, 0:256]), reads=[bk(bb_)], writes=['BB'])
                for c in range(8):
                    for d in range(2):
                        cd = c * 2 + d
                        ig = G[:, c * 32 + 2 * d * 8:c * 32 + 2 * d * 8 + 8]
                        S.op('dve', lambda e, cd=cd, ig=ig: e.tensor_tensor(out=AB[:, cd * 8:cd * 8 + 8], in0=ig,
                                                                             in1=BB[:, cd * 16:cd * 16 + 8], op=ALU.subtract),
                             reads=['G', 'BB'], writes=['AB'])
                        S.op('dve', lambda e, cd=cd: e.tensor_tensor(out=EE[:, cd * 8:cd * 8 + 8], in0=AB[:, cd * 8:cd * 8 + 8],
                                                                      in1=BB[:, cd * 16 + 8:cd * 16 + 16], op=ALU.add),
                             reads=['AB', 'BB'], writes=['EE'])
                        S.op('act', lambda e, cd=cd: e.activation(out=DT[:, cd * 8:cd * 8 + 8], in_=BB[:, cd * 16 + 8:cd * 16 + 16],
                                                                   func=AF.Exp), reads=['BB'], writes=['DT'])
                S.op('act', lambda e: e.activation(out=WL[:], in_=EE[:], func=AF.Exp), reads=['EE'], writes=['WL'])
                bt_ = next_bank()
                for cd in range(16):
                    c, d = divmod(cd, 2)
                    lf_cd = LF[:, c * 32 + (2 * d + 1) * 8:c * 32 + (2 * d + 1) * 8 + 8]
                    S.op('pe', lambda e, cd=cd, lf_cd=lf_cd: e.matmul(banks[bt_][0:8, cd:cd + 1], lhsT=lf_cd, rhs=ones_f[:, 0:1],
                                                                      start=True, stop=True),
                         reads=['LF', 'ones_f'], writes=[bk(bt_)])
                S.op('act', lambda e: e.copy(out=BT8[:], in_=banks[bt_][0:8, 0:16]), reads=[bk(bt_)], writes=['BT8'])
                for g4 in range(4):
                    be = next_bank()
                    for i in range(4):
                        cd = g4 * 4 + i
                        S.op('pe', lambda e, cd=cd, i=i, be=be: e.matmul(banks[be][0:8, i * 128:(i + 1) * 128],
                                                                        lhsT=EE[:, cd * 8:cd * 8 + 8], rhs=ident_f[:],
                                                                        start=True, stop=True),
                             reads=['EE', 'ident_f'], writes=[bk(be)])
                    S.op('dve', lambda e, g4=g4, be=be: e.tensor_reduce(
                        out=EM8[:, g4 * 4:(g4 + 1) * 4], in_=banks[be][0:8, :].rearrange("p (a b) -> p a b", a=4),
                        axis=mybir.AxisListType.X, op=ALU.max), reads=[bk(be)], writes=['EM8'])
                for j in range(4):
                    for d in range(2):
                        ca, cb = (2 * j, 2 * j + 1) if d == 0 else (2 * j + 1, 2 * j)
                        a_, b_ = ca * 2 + d, cb * 2 + d
                        jd = j * 2 + d
                        S.op('dve', lambda e, a_=a_: e.tensor_tensor(out=T8[:, 0:1], in0=BT8[:, a_:a_ + 1], in1=EM8[:, a_:a_ + 1],
                                                                      op=ALU.max), reads=['BT8', 'EM8'], writes=[('T8', 0)])
                        S.op('dve', lambda e, b_=b_: e.tensor_tensor(out=T8[:, 1:2], in0=T8[:, 0:1], in1=BT8[:, b_:b_ + 1],
                                                                      op=ALU.add), reads=['BT8', ('T8', 0)], writes=[('T8', 1)])
                        S.op('dve', lambda e, b_=b_, jd=jd: e.tensor_tensor(out=M8[:, jd:jd + 1], in0=T8[:, 1:2], in1=EM8[:, b_:b_ + 1],
                                                                             op=ALU.max), reads=['EM8', ('T8', 1)], writes=['M8'])
                S.op('act', lambda e: e.activation(out=SC8[:], in_=M8[:], func=AF.Exp, scale=-1.0), reads=['M8'], writes=['SC8'])
                dma('sp', b_mo[:, :], M8[:], reads=['M8'], pool='st')
                bs_ = next_bank()
                for jd in range(8):
                    S.op('dve', lambda e, jd=jd: e.tensor_scalar(out=d8[:, jd, :], in0=ident_f[0:8, 0:8], scalar1=SC8[:, jd:jd + 1],
                                                                  scalar2=None, op0=ALU.mult),
                         reads=['ident_f', 'SC8'], writes=[('d8', jd)])
                    S.op('pe', lambda e, jd=jd: e.matmul(banks[bs_][:, jd * 8:(jd + 1) * 8], lhsT=ones_f[0:8, :], rhs=d8[:, jd, :],
                                                         start=True, stop=True),
                         reads=['ones_f', ('d8', jd)], writes=[bk(bs_)])
                S.op('act', lambda e: e.copy(out=SCB[:], in_=banks[bs_][:, 0:64]), reads=[bk(bs_)], writes=['SCB'])

                wi = [0]
                hs_done = set()

                def record(fn):
                    saved = []
                    S.op = lambda *a, **k: saved.append((a, k))
                    try:
                        fn()
                    finally:
                        del S.op
                    return saved

                def loadw(col_ranges):
                    w = wt[0]
                    wk = ('wb', 0)
                    wi[0] += 1
                    o_ = 0
                    for c0, n_ in col_ranges:
                        dma('pool', w[:, :, o_:o_ + n_], b_w_in[:, c0:c0 + n_].rearrange("(c p) n -> p c n", p=128), writes=[wk])
                        o_ += n_
                    return w, wk

                def chunk(c, d, h, s_):
                    col = (c * 2 + d) * 8 + h
                    lfi = c * 32 + (2 * d + 1) * 8 + h
                    S.op('dve', lambda e: e.tensor_scalar(out=lfbc[:, s_, :], in0=ones_f[:], scalar1=LF[:, lfi:lfi + 1], scalar2=None,
                                                           op0=ALU.mult), reads=['LF', 'ones_f'], writes=[('lfbc', s_)])
                    br = next_bank()
                    S.op('pe', lambda e: e.matmul(banks[br][:, 0:128], lhsT=lfbc[:, s_, :], rhs=Um[:, d, :], start=True, stop=True),
                         reads=[('lfbc', s_), 'Um'], writes=[bk(br)])
                    S.op('dve', lambda e: e.tensor_tensor(out=tD[:, s_, :], in0=banks[br][:, 0:128], in1=Mk[:, d, :], op=ALU.add),
                         reads=[bk(br), 'Mk'], writes=[('tD', s_)])
                    S.op('act', lambda e: e.activation(out=Dt[:, s_, :], in_=tD[:, s_, :], func=AF.Exp, bias=AB[:, col:col + 1], scale=1.0),
                         reads=[('tD', s_), 'AB'], writes=[('Dt', s_)])
                    S.op('act', lambda e: e.activation(out=EB[:, s_, :], in_=banks[br][:, 0:128], func=AF.Exp),
                         reads=[bk(br)], writes=[('EB', s_)])
                    S.op('dve', lambda e: e.tensor_tensor(out=Qs[:, s_, :], in0=QT[:, c * 128:(c + 1) * 128], in1=EB[:, s_, :], op=ALU.mult),
                         reads=['QT', ('EB', s_)], writes=[('Qs', s_)])
                    bs2 = next_bank()
                    S.op('pe', lambda e: e.matmul(banks[bs2][:, 0:128], lhsT=KT[:, c * 128:(c + 1) * 128], rhs=QT[:, c * 128:(c + 1) * 128],
                                                  start=True, stop=True), reads=['KT', 'QT'], writes=[bk(bs2)])
                    S.op('dve', lambda e: e.tensor_tensor(out=PT[:, s_, :], in0=banks[bs2][:, 0:128], in1=Dt[:, s_, :], op=ALU.mult),
                         reads=[bk(bs2), ('Dt', s_)], writes=[('PT', s_)])
                    bn = next_bank()
                    S.op('pe', lambda e: e.matmul(banks[bn][:, 0:257], lhsT=PT[:, s_, :], rhs=Vaug[:, c, 0:257], start=True, stop=False),
                         reads=[('PT', s_), ('Vaug', c), 'Vaug_ones'], writes=[bk(bn)])
                    S.op('pe', lambda e: e.matmul(banks[bn][:, 0:257], lhsT=Qs[:, s_, :], rhs=Sbf[:, d, 0:257], start=False, stop=True),
                         reads=[('Qs', s_), ('Sbf', d)], writes=[bk(bn)])
                    S.op('dve', lambda e: e.tensor_scalar(out=rr[:, s_, 2:3], in0=banks[bn][:, 256:257], scalar1=-1.0, scalar2=None,
                                                           op0=ALU.mult), reads=[bk(bn)], writes=[('rr2', s_)])
                    S.op('dve', lambda e: e.scalar_tensor_tensor(out=rr[:, s_, 0:1], in0=banks[bn][:, 256:257], scalar=1.0,
                                                                  in1=rr[:, s_, 2:3], op0=ALU.max, op1=ALU.max),
                         reads=[bk(bn), ('rr2', s_)], writes=[('rr0', s_)])
                    S.op('dve', lambda e: e.reciprocal(out=rr[:, s_, 1:2], in_=rr[:, s_, 0:1]), reads=[('rr0', s_)], writes=[('rr1', s_)])
                    if c not in hs_done:
                        hs_done.add(c)
                        S.op('dve', lambda e: e.tensor_scalar(out=Hs[:, c, :], in0=banks[bn][:, 0:256], scalar1=rr[:, s_, 1:2], scalar2=None,
                                                               op0=ALU.mult), reads=[bk(bn), ('rr1', s_)], writes=[('Hs', c)])
                    else:
                        S.op('dve', lambda e: e.scalar_tensor_tensor(out=Hs[:, c, :], in0=banks[bn][:, 0:256], scalar=rr[:, s_, 1:2],
                                                                      in1=Hs[:, c, :], op0=ALU.mult, op1=ALU.add),
                             reads=[bk(bn), ('rr1', s_), ('Hs', c)], writes=[('Hs', c)])
                    S.op('dve', lambda e: e.tensor_scalar(out=Kw[:, s_, :], in0=Ktm[:, c, :], scalar1=WL[:, col:col + 1], scalar2=None,
                                                           op0=ALU.mult), reads=[('Ktm', c), 'WL'], writes=[('Kw', s_)])
                    bu = next_bank()
                    S.op('pe', lambda e: e.matmul(banks[bu][:, 0:257], lhsT=Kw[:, s_, :], rhs=Vaug[:, c, 0:257], start=True, stop=True),
                         reads=[('Kw', s_), ('Vaug', c), 'Vaug_ones'], writes=[bk(bu)])
                    S.op('dve', lambda e: e.scalar_tensor_tensor(out=S32[:, d, :], in0=S32[:, d, :], scalar=DT[:, col:col + 1],
                                                                  in1=banks[bu][:, 0:257], op0=ALU.mult, op1=ALU.add),
                         reads=[('S32', d), 'DT', bk(bu)], writes=[('S32', d)])
                    S.op('act', lambda e: e.copy(out=Sbf[:, d, 0:257], in_=S32[:, d, :]), reads=[('S32', d)], writes=[('Sbf', d)])

                for h in range(8):
                    w, wk = loadw([(h * 128, 128), (1024 + h * 128, 128)])
                    for slot, dst, dkey, scl in ((0, QT, 'QT', 128 ** -0.5), (1, KT, 'KT', 1.0)):
                        for hf in range(2):
                            b = next_bank()
                            for kc in range(16):
                                S.op('pe', lambda e, kc=kc, b=b, hf=hf, slot=slot, w=w: e.matmul(
                                    banks[b][:], lhsT=w[:, kc, slot * 128:(slot + 1) * 128], rhs=hT[:, kc, hf * 512:(hf + 1) * 512],
                                    start=(kc == 0), stop=(kc == 15)), reads=[wk, ('hT', kc)], writes=[bk(b)])
                            S.op('act', lambda e, b=b, hf=hf, dst=dst, scl=scl: e.mul(out=dst[:, hf * 512:(hf + 1) * 512], in_=banks[b][:],
                                                                                      mul=scl), reads=[bk(b)], writes=[dkey])
                    for tt in range(8):
                        b = next_bank()
                        for kc in range(16):
                            S.op('pe', lambda e, kc=kc, b=b, tt=tt, w=w: e.matmul(
                                banks[b][:, 0:128], lhsT=hT[:, kc, tt * 128:(tt + 1) * 128], rhs=w[:, kc, 128:256],
                                start=(kc == 0), stop=(kc == 15)), reads=[wk, ('hT', kc)], writes=[bk(b)])
                        S.op('act', lambda e, b=b, tt=tt: e.copy(out=Ktm[:, tt, :], in_=banks[b][:, 0:128]),
                             reads=[bk(b)], writes=[('Ktm', tt)])
                    w, wk = loadw([(2048 + h * 256, 256)])
                    for tt in range(8):
                        b = next_bank()
                        for kc in range(16):
                            S.op('pe', lambda e, kc=kc, b=b, tt=tt, w=w: e.matmul(
                                banks[b][:, 0:256], lhsT=hT[:, kc, tt * 128:(tt + 1) * 128], rhs=w[:, kc, :],
                                start=(kc == 0), stop=(kc == 15)), reads=[wk, ('hT', kc)], writes=[bk(b)])
                        S.op('dve', lambda e, b=b, tt=tt: e.tensor_copy(out=Vaug[:, tt, 0:256], in_=banks[b][:, 0:256]),
                             reads=[bk(b)], writes=[('Vaug', tt)])
                    for d in range(2):
                        r0 = (d * 8 + h) * 128
                        dma('sp', S32[:, d, :], b_S0[r0:r0 + 128, :], writes=[('S32', d)])
                        S.op('dve', lambda e, d=d: e.tensor_scalar(out=S32[:, d, :], in0=S32[:, d, :],
                                                                    scalar1=m0e[:, d * 8 + h:d * 8 + h + 1], scalar2=None, op0=ALU.mult),
                             reads=[('S32', d), 'm0e'], writes=[('S32', d)])
                        S.op('act', lambda e, d=d: e.copy(out=Sbf[:, d, 0:257], in_=S32[:, d, :]), reads=[('S32', d)], writes=[('Sbf', d)])
                    hs_done.clear()
                    for n_ in range(8):
                        recs = []
                        for d in range(2):
                            c = n_ if d == 0 else 7 - n_

                            def step(d=d, c=c, n_=n_):
                                if n_ > 0 and n_ % 2 == 0:
                                    S.op('dve', lambda e: e.tensor_scalar(out=S32[:, d, :], in0=S32[:, d, :], scalar1=keep[:, 0:1],
                                                                           scalar2=None, op0=ALU.mult),
                                         reads=[('S32', d), 'keep'], writes=[('S32', d)])
                                    S.op('act', lambda e: e.copy(out=Sbf[:, d, 0:257], in_=S32[:, d, :]),
                                         reads=[('S32', d)], writes=[('Sbf', d)])
                                chunk(c, d, h, d)
                                if n_ % 2 == 1:
                                    j = c // 2
                                    jd = j * 2 + d
                                    S.op('dve', lambda e: e.tensor_scalar(
                                        out=outS[:, d, :], in0=S32[:, d, :], scalar1=SCB[:, jd * 8 + h:jd * 8 + h + 1], scalar2=None,
                                        op0=ALU.mult), reads=[('S32', d), 'SCB'], writes=[('outS', d)])
                                    r0 = (j * 16 + d * 8 + h) * 128
                                    dma('sp', b_So[r0:r0 + 128, :], outS[:, d, :], reads=[('outS', d)], pool='st')
                            recs.append(record(step))
                        for i in range(max(len(r_) for r_ in recs)):
                            for r_ in recs:
                                if i < len(r_):
                                    S.op(*r_[i][0], **r_[i][1])
                    w, wk = loadw([(4096 + h * 256, 256)])
                    dma('sp', nrm[:], b_nrm[:, h * 256:(h + 1) * 256], writes=['nrm'])
                    for tt in range(8):
                        os_ = tt % 2
                        b = next_bank()
                        for kc in range(16):
                            S.op('pe', lambda e, kc=kc, b=b, tt=tt, w=w: e.matmul(
                                banks[b][:, 0:256], lhsT=hT[:, kc, tt * 128:(tt + 1) * 128], rhs=w[:, kc, :],
                                start=(kc == 0), stop=(kc == 15)), reads=[wk, ('hT', kc)], writes=[bk(b)])
                        S.op('act', lambda e, b=b, os_=os_: e.activation(out=SIGO[:, os_, :], in_=banks[b][:, 0:256], func=AF.Sigmoid),
                             reads=[bk(b)], writes=[('SIGO', os_)])
                        S.op('act', lambda e, tt=tt: e.activation(out=sq_junk[:, 0:256], in_=Hs[:, tt, :], func=AF.Square,
                                                                   accum_out=sst[:, 0:1]),
                             reads=[('Hs', tt)], writes=['sq_junk', ('sst', 0)])
                        S.op('dve', lambda e: e.tensor_scalar(out=sst[:, 1:2], in0=sst[:, 0:1], scalar1=1.0 / 256, scalar2=EPS,
                                                               op0=ALU.mult, op1=ALU.add), reads=[('sst', 0)], writes=[('sst', 1)])
                        S.op('act', lambda e: e.activation(out=sst[:, 2:3], in_=sst[:, 1:2], func=AF.Ln),
                             reads=[('sst', 1)], writes=[('sst', 2)])
                        S.op('act', lambda e: e.activation(out=sst[:, 3:4], in_=sst[:, 2:3], func=AF.Exp, scale=-0.5),
                             reads=[('sst', 2)], writes=[('sst', 3)])
                        S.op('dve', lambda e, tt=tt: e.scalar_tensor_tensor(
                            out=hn, in0=Hs[:, tt, :], scalar=sst[:, 3:4], in1=nrm[:],
                            op0=ALU.mult, op1=ALU.mult), reads=[('Hs', tt), ('sst', 3), 'nrm'], writes=[('tmp_f', 0)])
                        S.op('dve', lambda e, tt=tt, os_=os_: e.tensor_tensor(out=onb[:, os_, :], in0=hn, in1=SIGO[:, os_, :], op=ALU.mult),
                             reads=[('tmp_f', 0), ('SIGO', os_)], writes=[('onb', os_)])
                        for ci in range(2):
                            b = next_bank()
                            pb = banks[b][:].bitcast(BF16)
                            S.op('pe', lambda e, pb=pb, ci=ci, os_=os_: e.transpose(out=pb[:, 0:128], in_=onb[:, os_, ci * 128:(ci + 1) * 128],
                                                                                    identity=ident_b[:]),
                                 reads=[('onb', os_), 'ident_b'], writes=[bk(b)])
                            och = 2 * h + ci
                            S.op('act', lambda e, pb=pb, och=och, tt=tt: e.copy(out=oT[:, och, tt * 128:(tt + 1) * 128], in_=pb[:, 0:128]),
                                 reads=[bk(b)], writes=[('oT', och)])
            S.barrier()
            out_proj(b_w_out, oT, st)
        S.barrier()

    def out_proj(w_out_d, oT, st):
        with ExitStack() as st3:
            wo = sb("wo", [128, 16, D], BF16, st3)
            for c4 in range(4):
                dma('pool', wo[:, :, c4 * 512:(c4 + 1) * 512],
                    w_out_d[:, c4 * 512:(c4 + 1) * 512].rearrange("(c p) n -> p c n", p=128), writes=[('wo', c4)])
            for tt in range(8):
                yb = [next_bank() for _ in range(4)]
                for q in range(4):
                    for kc in range(16):
                        S.op('pe', lambda e, q=q, kc=kc, tt=tt, yb=yb: e.matmul(
                            banks[yb[q]][:], lhsT=oT[:, kc, tt * 128:(tt + 1) * 128], rhs=wo[:, kc, q * 512:(q + 1) * 512],
                            start=(kc == 0), stop=(kc == 15)), reads=[('wo', q), ('oT', kc)], writes=[bk(yb[q])])
                post_norm(0, tt, yb)

    for l in layers:
        with ExitStack() as wst:
            modulation(l, wst)
        S.barrier()
        if l % 4 == 0:
            mixer_a(l)
        elif l % 4 == 1:
            mixer_b(l)
        elif l % 4 == 2:
            mixer_cd(l, 'c')
        elif l % 4 == 3:
            mixer_cd(l, 'd')
        if dbg == ('mid', l):
            for tt in range(8):
                dma('sp', dbg_out[tt * 128:(tt + 1) * 128, :], xres[:, tt, :], reads=[('x', tt)], pool='st')
        mlp(l)

    for tt in range(8):
        dma('sp', y_out[tt * 128:(tt + 1) * 128, :], xres[:, tt, :], reads=[('x', tt)], pool='st')
    if max_ops is not None:
        S.truncate(max_ops)
    info = S.emit()
    es.close()
    return nc, info


def fm(v):
    v = np.asarray(v, np.float32)
    return np.ascontiguousarray(np.moveaxis(v.reshape(v.shape[:-1] + (-1, 128)), -1, 0))


def make_in_maps(inp, nlw=4):
    maps = []
    ident = np.eye(128, dtype=np.float32)
    rot = rot_matrix()
    shared = dict(
        w_mod=inp['w_mod'][:nlw].reshape(nlw * D, 6 * D), w_ff1=inp['w_ff1'][:nlw].reshape(nlw * D, DFF),
        w_ff2=inp['w_ff2'][:nlw].reshape(nlw * DFF, D), ident=ident, rotm=rot,
        bmod=fm(inp['b_mod']).reshape(128, 4 * 96),
        gfm=fm(inp['g_norm']).reshape(128, 4 * 4 * 16),
        a_w_in=inp['a_w_in'][0], a_w_out=inp['a_w_out'][0],
        a_lam=np.ascontiguousarray(np.broadcast_to(inp['a_lambda'][0].reshape(1, 4096), (128, 4096))),
        a_sub=np.ascontiguousarray(np.broadcast_to(inp['a_subln'][0].reshape(1, 128), (128, 128))),
    )
    NA_L = [[0, 1, 2, 3], [0, 1, 2, 3], [0, 1, 2, 3, 4], [1, 2, 3, 4, 5], [2, 3, 4, 5, 6], [3, 4, 5, 6, 7], [4, 5, 6, 7], [4, 5, 6, 7]]
    rpb = np.asarray(inp['c_rpb'][0], np.float32).reshape(16, 465)
    ext = np.concatenate([rpb, np.full((16, 1), NEG, np.float32), np.zeros((16, 1), np.float32)], 1)
    cbias = {}
    dmask = {}
    for sample in (True, False):
        idx = np.full((8, 128, 5, 128), 465, np.int64)
        for qt in range(8):
            for si, kc in enumerate(NA_L[qt]):
                k = kc * 128 + np.arange(128)[:, None]
                q = qt * 128 + np.arange(128)[None, :]
                if sample:
                    rk, ck_, rq, cq_ = k // 64, k % 64, q // 64, q % 64
                    r0 = np.clip(rq - 4, 0, 8)
                    cs = np.clip(cq_ - 8, 0, 48)
                    valid = (rk >= r0) & (rk < r0 + 8) & (ck_ >= cs) & (ck_ < cs + 16)
                    ii = np.where(valid, (rk - rq + 7) * 31 + np.clip(ck_ - cq_, -15, 15) + 15, 465)
                else:
                    ii = np.full((128, 128), 466 if kc // 2 == qt // 2 else 465)
                idx[qt, :, si, :] = ii
        cbias[sample] = np.ascontiguousarray(ext[:, idx].reshape(16 * 8 * 128, 5 * 128))
        dm = np.full((128, 2, 8, 128), NEG, np.float32)
        kp = np.arange(128)[:, None]
        qq = np.arange(128)[None, :]
        for qt in range(8):
            if sample:
                dm[:, 0, qt, :] = np.where(kp >= qq, 0.0, NEG)
                dm[:, 1, qt, :] = np.where(kp <= qq, 0.0, NEG)
            else:
                dm[:, 0, qt, :] = 0.0 if qt % 2 == 1 else NEG
                dm[:, 1, qt, :] = 0.0 if qt % 2 == 0 else NEG
        dmask[sample] = dm.reshape(128, 2 * 8 * 128)
    shared.update(
        c_w_in=inp['c_w_in'][0], c_w_out=inp['c_w_out'][0], d_w_in=inp['d_w_in'][0], d_w_out=inp['d_w_out'][0],
        d_snk=np.ascontiguousarray(np.broadcast_to(inp['d_sink'][0].reshape(1, 32), (128, 32))))
    pi = np.arange(128)[:, None]
    ti = np.arange(128)[None, :]
    um = np.concatenate([(pi <= ti), (pi >= ti)], 1).astype(np.float32)
    mkb = np.concatenate([np.where(pi <= ti, 0.0, NEG), np.where(pi >= ti, 0.0, NEG)], 1).astype(np.float32)
    shared.update(
        b_w_in=inp['b_w_in'][0], b_w_out=inp['b_w_out'][0], b_um=um, b_mk=mkb,
        b_gb=np.ascontiguousarray(np.broadcast_to(inp['b_gate_bias'][0].reshape(1, 32), (128, 32))),
        b_nrm=np.ascontiguousarray(np.broadcast_to(inp['b_norm'][0].reshape(1, D), (128, D))))
    for core in range(8):
        sample = core < 4
        m = dict(shared)
        cb = core if sample else 0
        if sample:
            ct = np.swapaxes(inp['state_b_C'][core, 0], -1, -2)
            m['b_S0'] = np.concatenate([ct, inp['state_b_n'][core, 0][..., None]], -1).reshape(16 * 128, 257)
            m['b_m0'] = np.ascontiguousarray(np.broadcast_to(inp['state_b_m'][core, 0].reshape(1, 16), (128, 16)))
            m['b_keep'] = np.ones((128, 1), np.float32)
        else:
            m['b_S0'] = np.zeros((16 * 128, 257), np.float32)
            m['b_m0'] = np.zeros((128, 16), np.float32)
            m['b_keep'] = np.zeros((128, 1), np.float32)
        m['c_ck'] = inp['cache_c_k'][cb, 0].reshape(512, D)
        m['c_cv'] = inp['cache_c_v'][cb, 0].reshape(512, D)
        m['d_ck'] = inp['cache_d_k'][cb, 0].reshape(512, 512)
        m['d_cv'] = inp['cache_d_v'][cb, 0].reshape(512, 512)
        m['c_bias'] = cbias[sample]
        m['d_mask'] = dmask[sample]
        m['ctxb'] = np.full((128, 1), 0.0 if sample else NEG, np.float32)
        if sample:
            m['x'] = inp['x_sample'][core]
            m['cvec'] = fm(inp['c'][core])
            m['a_ck'] = inp['cache_a_k'][core, 0].reshape(512, D)
            m['a_cv'] = inp['cache_a_v'][core, 0].reshape(512, D)
            mask = np.zeros((12, 4), np.float32)
        else:
            p0 = (core - 4) * 4
            m['x'] = inp['x_prompt'][p0:p0 + 4].reshape(NT, D)
            m['cvec'] = fm(inp['c_ctx'])
            m['a_ck'] = inp['cache_a_k'][0, 0].reshape(512, D)
            m['a_cv'] = inp['cache_a_v'][0, 0].reshape(512, D)
            mask = np.full((12, 4), NEG, np.float32)
            for qp in range(4):
                mask[2 * qp:2 * qp + 2, qp] = 0.0
        m['a_mask'] = np.ascontiguousarray(np.broadcast_to(mask.reshape(1, 48), (128, 48)))
        cq, sq = rope_tables(sample, 0.125)
        ck, sk = rope_tables(sample, 1.0)
        m['a_cosq'], m['a_sinq'], m['a_cosk'], m['a_sink'] = cq, sq, ck, sk
        maps.append({k: (v if (v.dtype == np.float32 and v.flags['C_CONTIGUOUS']) else np.ascontiguousarray(v, dtype=np.float32))
                     for k, v in m.items()})
    return maps


def kernel(**inputs):
    inp = {k: np.asarray(v) for k, v in inputs.items()}
    nc, info = build()
    maps = make_in_maps(inp)
    res = run_bass_kernel_spmd(nc, maps, core_ids=list(range(8)))
    r = res.results
    y_sample = np.stack([r[c]['y'] for c in range(4)], 0)
    y_prompt = np.concatenate([r[c]['y'].reshape(4, 256, D) for c in range(4, 8)], 0)

    def ctx(name, hh, dd):
        return np.concatenate([r[c][name].reshape(4, 1, 256, hh, dd) for c in range(4, 8)], 0)
    so = np.concatenate([r[c]['b_So'].reshape(4, 1, 2, 8, 128, 257) for c in range(4, 8)], 0)
    b_C = np.ascontiguousarray(np.swapaxes(so[..., 0:256], -1, -2))
    b_n = np.ascontiguousarray(so[..., 256])
    b_m = np.concatenate([np.transpose(r[c]['b_mo'].reshape(8, 4, 1, 2), (1, 2, 3, 0)) for c in range(4, 8)], 0)
    b_m = np.ascontiguousarray(b_m)
    return (y_prompt, y_sample, ctx('a_k', 16, 128), ctx('a_v', 16, 128), b_C, b_n, b_m,
            ctx('c_k', 16, 128), ctx('c_v', 16, 128), ctx('d_k', 8, 64), ctx('d_v', 8, 64))
```

```python
import math
import numpy as np
import concourse.bass as bass
import concourse.mybir as mybir
from concourse.bass_utils import run_bass_kernel_spmd
from contextlib import ExitStack

F32 = mybir.dt.float32
BF16 = mybir.dt.bfloat16
AF = mybir.ActivationFunctionType
ALU = mybir.AluOpType

D = 2048
NT = 1024
DFF = 8192
EPS = 1e-6
NEG = -30000.0


class _Rec:
    def __init__(self):
        self.call = None

    def __getattr__(self, name):
        def f(*a, **k):
            self.call = (name, a, k)
            return None
        return f


class Sched:
    COMPUTE = ('pe', 'act', 'dve', 'pool')

    def __init__(self, nc, es):
        self.nc = nc
        self.es = es
        self.ops = []
        self.last_w = {}
        self.readers = {}
        self.engs = ('pe', 'act', 'dve', 'pool', 'sp')
        self.psem = {e: es.enter_context(nc.semaphore("P_" + e)) for e in self.COMPUTE}
        self.dma_last = {}
        self.dsems = {}
        self.last_on = {}
        self.bar = None
        self.bar_done = set()

    def barrier(self):
        self.bar = set(self.last_on.values()) | set(self.dma_last.values())
        self.bar_done = set()

    def op(self, eng, fn, reads=(), writes=(), dma=None):
        deps = set()
        for r in reads:
            if r in self.last_w:
                deps.add(self.last_w[r])
            if isinstance(r, tuple) and r[0] == 'bank':
                for k_, v_ in (self.readers.get(r) or {}).items():
                    if k_ != eng:
                        deps.add(v_)
        for w in writes:
            if w in self.last_w:
                deps.add(self.last_w[w])
            rd = self.readers.get(w)
            if rd:
                deps.update(rd.values())
        if dma is not None:
            if dma not in self.dsems:
                self.dsems[dma] = self.es.enter_context(self.nc.semaphore("D_" + dma))
            if dma in self.dma_last:
                deps.add(self.dma_last[dma])
        if self.bar is not None and eng not in self.bar_done:
            deps.update(self.bar)
            self.bar_done.add(eng)
        oid = len(self.ops)
        rec = _Rec()
        fn(rec)
        name_, a_, k_ = rec.call
        fn = (lambda E, name_=name_, a_=a_, k_=k_: getattr(E, name_)(*a_, **k_))
        self.ops.append([eng, fn, deps, dma, False, None])
        if dma is not None:
            self.dma_last[dma] = oid
        else:
            self.last_on[eng] = oid
        for w in writes:
            self.last_w[w] = oid
            self.readers[w] = {}
        for r in reads:
            d = self.readers.setdefault(r, {})
            d[eng if dma is None else ('dma', oid)] = oid
        return oid

    def truncate(self, n):
        self.ops = self.ops[:n]

    def emit(self, final_wait_eng='sp'):
        ops = self.ops
        for o in ops:
            for d in o[2]:
                do = ops[d]
                if do[3] is None and not (do[0] == 'pe' and o[0] == 'pe' and o[3] is None):
                    do[4] = True
        cnt = {e: 0 for e in self.COMPUTE}
        dcnt = {}
        for o in ops:
            if o[3] is not None:
                dcnt[o[3]] = dcnt.get(o[3], 0) + 16
                o[5] = (o[3], dcnt[o[3]])
            elif o[4]:
                cnt[o[0]] += 1
                o[5] = (o[0], cnt[o[0]])
        waited = {e: {} for e in self.engs}
        streams = {e: [] for e in self.engs}
        for o in ops:
            eng, fn, deps, dma, sig, tok = o
            need = {}
            for d in deps:
                do = ops[d]
                if do[3] is None and do[0] == 'pe' and eng == 'pe' and dma is None:
                    continue
                s, v = do[5]
                if v > need.get(s, 0):
                    need[s] = v
            for s, v in need.items():
                if waited[eng].get(s, 0) >= v:
                    continue
                sem = self.psem[s] if s in self.psem else self.dsems[s]
                streams[eng].append(('w', sem, v))
                waited[eng][s] = v
            if dma is not None:
                streams[eng].append(('i', fn, self.dsems[dma], 16))
            elif sig:
                streams[eng].append(('i', fn, self.psem[eng], 1))
            else:
                streams[eng].append(('i', fn, None, 0))
        for s, v in dcnt.items():
            streams[final_wait_eng].append(('w', self.dsems[s], v))

        def runner(eng):
            def f(E):
                for it in streams[eng]:
                    if it[0] == 'w':
                        E.wait_ge(it[1], it[2])
                    else:
                        ins = it[1](E)
                        if it[2] is not None:
                            ins.then_inc(it[2], it[3])
            return f
        with self.nc.Block() as block:
            block.sync(runner('sp'))
            block.scalar(runner('act'))
            block.vector(runner('dve'))
            block.gpsimd(runner('pool'))
            block.tensor(runner('pe'))
        return dict(n_ops=len(ops), counts=cnt)


def rope_tables(sample, qscale):
    t = np.arange(NT)
    cos = np.ones((64, NT), np.float64)
    sin = np.zeros((64, NT), np.float64)
    if sample:
        inv = 10000.0 ** (-np.arange(16, dtype=np.float32) / 16)
        for grp, pos in ((0, t // 64), (1, t % 64)):
            ang = pos.astype(np.float32)[None, :] * inv[:, None].astype(np.float32)
            c, s = np.cos(ang), np.sin(ang)
            cos[grp * 32:grp * 32 + 16] = c
            cos[grp * 32 + 16:grp * 32 + 32] = c
            sin[grp * 32:grp * 32 + 16] = -s
            sin[grp * 32 + 16:grp * 32 + 32] = s
    cos = np.concatenate([cos, cos], 0) * qscale
    sin = np.concatenate([sin, sin], 0) * qscale
    return cos.astype(np.float32), sin.astype(np.float32)


def rot_matrix():
    P = np.zeros((128, 128), np.float32)
    for m in range(128):
        g, i = divmod(m, 32)
        partner = g * 32 + (i + 16) % 32
        P[partner, m] = 1.0
    return P


def build(layers=(0, 1, 2, 3), dbg=None, max_ops=None):
    NLW = max(layers) + 1
    nc = bass.Bass("TRN2", target_bir_lowering=False)
    es = ExitStack()
    S = Sched(nc, es)

    def din(name, shape):
        return nc.dram_tensor(name, list(shape), F32, kind="ExternalInput").ap()

    def dout(name, shape):
        return nc.dram_tensor(name, list(shape), F32, kind="ExternalOutput").ap()

    x_in = din("x", [NT, D])
    cvec = din("cvec", [128, 16])
    w_mod = din("w_mod", [NLW * D, 6 * D])
    bmod = din("bmod", [128, 4 * 96])
    gfm = din("gfm", [128, 4 * 4 * 16])
    w_ff1 = din("w_ff1", [NLW * D, DFF])
    w_ff2 = din("w_ff2", [NLW * DFF, D])
    ident_d = din("ident", [128, 128])
    a_w_in = din("a_w_in", [D, 3 * D])
    a_w_out = din("a_w_out", [D, D])
    a_ck = din("a_ck", [512, D])
    a_cv = din("a_cv", [512, D])
    a_lam = din("a_lam", [128, 4096])
    a_sub = din("a_sub", [128, 128])
    a_cosq = din("a_cosq", [128, NT])
    a_sinq = din("a_sinq", [128, NT])
    a_cosk = din("a_cosk", [128, NT])
    a_sink = din("a_sink", [128, NT])
    a_mask = din("a_mask", [128, 48])
    rotm = din("rotm", [128, 128])

    c_w_in = din("c_w_in", [D, 3 * D])
    c_w_out = din("c_w_out", [D, D])
    c_ck = din("c_ck", [512, D])
    c_cv = din("c_cv", [512, D])
    c_bias = din("c_bias", [16 * 8 * 128, 5 * 128])
    d_w_in = din("d_w_in", [D, 3072])
    d_w_out = din("d_w_out", [D, D])
    d_ck = din("d_ck", [512, 512])
    d_cv = din("d_cv", [512, 512])
    d_mask = din("d_mask", [128, 2 * 8 * 128])
    d_snk = din("d_snk", [128, 32])
    ctxb_d = din("ctxb", [128, 1])
    b_w_in = din("b_w_in", [D, 6176])
    b_w_out = din("b_w_out", [D, D])
    b_gb = din("b_gb", [128, 32])
    b_nrm = din("b_nrm", [128, D])
    b_keep = din("b_keep", [128, 1])
    b_m0 = din("b_m0", [128, 16])
    b_S0 = din("b_S0", [16 * 128, 257])
    b_um = din("b_um", [128, 256])
    b_mk = din("b_mk", [128, 256])
    b_So = dout("b_So", [4 * 16 * 128, 257])
    b_mo = dout("b_mo", [8, 8])
    c_ko = dout("c_k", [NT, D])
    c_vo = dout("c_v", [NT, D])
    d_ko = dout("d_k", [NT, 512])
    d_vo = dout("d_v", [NT, 512])
    y_out = dout("y", [NT, D])
    a_ko = dout("a_k", [NT, D])
    a_vo = dout("a_v", [NT, D])
    dbg_out = dout("dbg", [NT, D]) if dbg else None

    sb_ctr = [0]

    def sb(name, shape, dt, stack=es):
        sb_ctr[0] += 1
        return stack.enter_context(nc.sbuf_tensor("%s_%d" % (name, sb_ctr[0]), list(shape), dt))

    banks = [es.enter_context(nc.psum_tensor("bank%d" % i, [128, 512], F32)) for i in range(8)]
    bank_rr = [0]

    def next_bank():
        b = bank_rr[0]
        bank_rr[0] = (b + 1) % 8
        return b

    def bk(b):
        return ('bank', b)

    xres = sb("xres", [128, 8, D], F32)
    ident_f = sb("ident_f", [128, 128], F32)
    ident_b = sb("ident_b", [128, 128], BF16)
    ones_f = sb("ones_f", [128, 128], F32)
    cv_f = sb("cv_f", [128, 16], F32)
    cv_b = sb("cv_b", [128, 16], BF16)
    bmod_s = sb("bmod_s", [128, 4 * 96], F32)
    gfm_s = sb("gfm_s", [128, 4 * 4 * 16], F32)
    modv = sb("modv", [128, 96], F32)
    Acoef = sb("Acoef", [128, 2, 16], F32)
    Gfm = sb("Gfm", [128, 2, 16], F32)
    Gbc = sb("Gbc", [128, 2, D], F32)
    diag = sb("diag", [128, 2, 128], F32)
    stat = sb("stat", [128, 64], F32)
    sq_junk = sb("sq_junk", [128, 512], BF16)
    xn_b = sb("xn_b", [128, 1, D], BF16)
    tmp_f = sb("tmp_f", [128, 2, 512], F32)

    n_dma = [0]

    def dma(eng, out, in_, reads=(), writes=(), pool=None):
        if pool is None:
            pool = 'ld' if eng == 'sp' else 'wq'
        k = n_dma[0]
        n_dma[0] += 1
        name = "%s%d" % (pool, k % 6)
        S.op(eng, lambda e: e.dma_start(out=out, in_=in_), reads=reads, writes=writes, dma=name)

    for tt in range(8):
        dma('sp', xres[:, tt, :], x_in[tt * 128:(tt + 1) * 128, :], writes=[('x', tt)])
    dma('sp', ident_f[:], ident_d[:, :], writes=['ident_f'])
    dma('sp', cv_f[:], cvec[:, :], writes=['cv_f'])
    dma('sp', bmod_s[:], bmod[:, :], writes=['bmod_s'])
    dma('sp', gfm_s[:], gfm[:, :], writes=['gfm_s'])
    S.op('dve', lambda e: e.tensor_copy(out=ident_b[:], in_=ident_f[:]), reads=['ident_f'], writes=['ident_b'])
    S.op('dve', lambda e: e.memset(ones_f[:], 1.0), writes=['ones_f'])
    S.op('act', lambda e: e.activation(out=cv_b[:], in_=cv_f[:], func=AF.Silu), reads=['cv_f'], writes=['cv_b'])

    wctr = [0]

    def modulation(l, wst):
        wblk = [sb("wm%d" % i, [128, 16, 512], BF16, wst) for i in range(3)]
        b = next_bank()
        for nb in range(24):
            w = wblk[nb % 3]
            wk = ('wm', nb % 3)
            src = w_mod[l * D:(l + 1) * D, nb * 512:(nb + 1) * 512].rearrange("(c p) n -> p c n", p=128)
            dma('pool', w[:], src, writes=[wk])
            for j in range(4):
                col = nb * 4 + j
                for kc in range(16):
                    S.op('pe', lambda e, w=w, j=j, kc=kc, col=col, b=b: e.matmul(
                        banks[b][:, col:col + 1], lhsT=w[:, kc, j * 128:(j + 1) * 128], rhs=cv_b[:, kc:kc + 1],
                        start=(kc == 0), stop=(kc == 15)), reads=[wk, 'cv_b'], writes=[bk(b)])
        S.op('dve', lambda e, b=b: e.tensor_tensor(out=modv[:], in0=banks[b][:, 0:96], in1=bmod_s[:, l * 96:(l + 1) * 96],
                                                    op=ALU.add), reads=[bk(b), 'bmod_s'], writes=['modv'])
        g = lambda i: gfm_s[:, (l * 4 + i) * 16:(l * 4 + i + 1) * 16]
        for half in range(2):
            sc = modv[:, (3 * half + 1) * 16:(3 * half + 2) * 16]
            gt = modv[:, (3 * half + 2) * 16:(3 * half + 3) * 16]
            S.op('dve', lambda e, half=half, sc=sc: e.scalar_tensor_tensor(
                out=Acoef[:, half, :], in0=sc, scalar=1.0, in1=g(2 * half), op0=ALU.add, op1=ALU.mult),
                reads=['modv', 'gfm_s'], writes=[('Acoef', half)])
            S.op('dve', lambda e, half=half, gt=gt: e.tensor_tensor(
                out=Gfm[:, half, :], in0=gt, in1=g(2 * half + 1), op=ALU.mult),
                reads=['modv', 'gfm_s'], writes=[('Gfm', half)])
            for c4 in range(4):
                b2 = next_bank()
                for cc in range(4):
                    c = c4 * 4 + cc
                    S.op('dve', lambda e, c=c, half=half: e.tensor_scalar(
                        out=diag[:, c % 2, :], in0=ident_f[:], scalar1=Gfm[:, half, c:c + 1], scalar2=None, op0=ALU.mult),
                        reads=['ident_f', ('Gfm', half)], writes=[('diag', c % 2)])
                    S.op('pe', lambda e, c=c, cc=cc, b2=b2: e.matmul(
                        banks[b2][:, cc * 128:(cc + 1) * 128], lhsT=ones_f[:], rhs=diag[:, c % 2, :], start=True, stop=True),
                        reads=['ones_f', ('diag', c % 2)], writes=[bk(b2)])
                S.op('act', lambda e, c4=c4, half=half, b2=b2: e.copy(
                    out=Gbc[:, half, c4 * 512:(c4 + 1) * 512], in_=banks[b2][:]),
                    reads=[bk(b2)], writes=[('Gbc', half)])

    def adaln_in(half, tiles, hT, hkey, col0=0):
        shift = modv[:, (3 * half) * 16:(3 * half + 1) * 16]
        for i, tt in enumerate(tiles):
            s = tt % 2
            S.op('act', lambda e, tt=tt, s=s: e.activation(out=xn_b[:, 0, :], in_=xres[:, tt, :], func=AF.Square,
                                                            accum_out=stat[:, s:s + 1]),
                 reads=[('x', tt)], writes=[('xn_b', 0), ('stat', s)])
            S.op('dve', lambda e, s=s: e.tensor_scalar(out=stat[:, 2 + s:3 + s], in0=stat[:, s:s + 1], scalar1=1.0 / D,
                                                        scalar2=EPS, op0=ALU.mult, op1=ALU.add),
                 reads=[('stat', s)], writes=[('stat', 2 + s)])
            S.op('act', lambda e, s=s: e.activation(out=stat[:, 6 + s:7 + s], in_=stat[:, 2 + s:3 + s], func=AF.Ln),
                 reads=[('stat', 2 + s)], writes=[('stat', 6 + s)])
            S.op('act', lambda e, s=s: e.activation(out=stat[:, 4 + s:5 + s], in_=stat[:, 6 + s:7 + s], func=AF.Exp, scale=-0.5),
                 reads=[('stat', 6 + s)], writes=[('stat', 4 + s)])
            S.op('dve', lambda e, tt=tt, s=s: e.tensor_scalar(out=xn_b[:, 0, :], in0=xres[:, tt, :],
                                                               scalar1=stat[:, 4 + s:5 + s], scalar2=None, op0=ALU.mult),
                 reads=[('x', tt), ('stat', 4 + s)], writes=[('xn_b', 0)])
            for c8 in range(2):
                b = next_bank()
                pb = banks[b][:].bitcast(BF16)
                for cc in range(8):
                    c = c8 * 8 + cc
                    S.op('pe', lambda e, c=c, cc=cc, s=s, pb=pb: e.transpose(
                        out=pb[:, cc * 128:(cc + 1) * 128], in_=xn_b[:, 0, c * 128:(c + 1) * 128], identity=ident_b[:]),
                        reads=[('xn_b', 0), 'ident_b'], writes=[bk(b)])
                for cc in range(8):
                    c = c8 * 8 + cc
                    dst = hT[:, c, col0 + i * 128:col0 + (i + 1) * 128]
                    if True:
                        S.op('act', lambda e, c=c, cc=cc, pb=pb, dst=dst, half=half: e.activation(
                            out=dst, in_=pb[:, cc * 128:(cc + 1) * 128], func=AF.Identity,
                            bias=shift[:, c:c + 1], scale=Acoef[:, half, c:c + 1]),
                            reads=[bk(b), ('Acoef', half), 'modv'], writes=[(hkey, c)])
                    else:
                        S.op('dve', lambda e, c=c, cc=cc, pb=pb, dst=dst, half=half: e.tensor_scalar(
                            out=dst, in0=pb[:, cc * 128:(cc + 1) * 128], scalar1=Acoef[:, half, c:c + 1],
                            scalar2=shift[:, c:c + 1], op0=ALU.mult, op1=ALU.add),
                            reads=[bk(b), ('Acoef', half), 'modv'], writes=[(hkey, c)])

    def post_norm(half, tt, yb):
        for q in range(4):
            S.op('act', lambda e, q=q: e.activation(out=sq_junk[:, :], in_=banks[yb[q]][:],
                                                     func=AF.Square, accum_out=stat[:, 8 + q:9 + q]),
                 reads=[bk(yb[q])], writes=['sq_junk', ('stat', 8 + q)])
        S.op('dve', lambda e: e.tensor_reduce(out=stat[:, 12:13], in_=stat[:, 8:12], axis=mybir.AxisListType.X, op=ALU.add),
             reads=[('stat', 8 + q) for q in range(4)], writes=[('stat', 12)])
        S.op('dve', lambda e: e.tensor_scalar(out=stat[:, 13:14], in0=stat[:, 12:13], scalar1=1.0 / D, scalar2=EPS,
                                               op0=ALU.mult, op1=ALU.add), reads=[('stat', 12)], writes=[('stat', 13)])
        S.op('act', lambda e: e.activation(out=stat[:, 15:16], in_=stat[:, 13:14], func=AF.Ln),
             reads=[('stat', 13)], writes=[('stat', 15)])
        S.op('act', lambda e: e.activation(out=stat[:, 14:15], in_=stat[:, 15:16], func=AF.Exp, scale=-0.5),
             reads=[('stat', 15)], writes=[('stat', 14)])
        for q in range(4):
            s = q % 2
            S.op('dve', lambda e, q=q, s=s: e.scalar_tensor_tensor(
                out=tmp_f[:, s, :], in0=banks[yb[q]][:], scalar=stat[:, 14:15], in1=Gbc[:, half, q * 512:(q + 1) * 512],
                op0=ALU.mult, op1=ALU.mult), reads=[bk(yb[q]), ('stat', 14), ('Gbc', half)], writes=[('tmp_f', s)])
            S.op('dve', lambda e, q=q, s=s, tt=tt: e.tensor_tensor(
                out=xres[:, tt, q * 512:(q + 1) * 512], in0=xres[:, tt, q * 512:(q + 1) * 512], in1=tmp_f[:, s, :],
                op=ALU.add), reads=[('x', tt), ('tmp_f', s)], writes=[('x', tt)])

    def mlp(l):
        with ExitStack() as st:
            hTq = sb("hTq", [128, 16, 512], BF16, st)
            ystash = hTq[:].rearrange("p c n -> p (c n)").bitcast(F32).rearrange("p (t n) -> p t n", t=4)
            uTq = sb("uTq", [128, 64, 512], BF16, st)
            w1 = [sb("w1_%d" % i, [128, 16, 128], BF16, st) for i in range(4)]
            w2 = [sb("w2_%d" % i, [128, 1024], BF16, st) for i in range(6)]
            rl = sb("rl", [128, 2, 512], F32, st)
            hkeys = [('hTq', kc) for kc in range(16)]
            w2n = 0
            for hq in range(2):
                adaln_in(1, [4 * hq + i for i in range(4)], hTq, 'hTq')
                for nb in range(64):
                    w = w1[nb % 4]
                    wk = ('w1', nb % 4)
                    src = w_ff1[l * D:(l + 1) * D, nb * 128:(nb + 1) * 128].rearrange("(c p) n -> p c n", p=128)
                    dma('pool', w[:], src, writes=[wk])
                    b = next_bank()
                    fc = nb
                    for kc in range(16):
                        S.op('pe', lambda e, w=w, kc=kc, b=b: e.matmul(
                            banks[b][:], lhsT=w[:, kc, :], rhs=hTq[:, kc, :],
                            start=(kc == 0), stop=(kc == 15)), reads=[wk, ('hTq', kc)], writes=[bk(b)])
                    s = fc % 2
                    S.op('act', lambda e, b=b, s=s: e.activation(out=rl[:, s, :], in_=banks[b][:], func=AF.Relu),
                         reads=[bk(b)], writes=[('rl', s)])
                    S.op('dve', lambda e, s=s, fc=fc: e.tensor_tensor(out=uTq[:, fc, :], in0=rl[:, s, :], in1=rl[:, s, :],
                                                                       op=ALU.mult),
                         reads=[('rl', s)], writes=[('uTq', fc)])
                for ch in range(2):
                    yb = [[next_bank() for _ in range(2)] for _ in range(4)]
                    for kc in range(64):
                        w = w2[w2n % 6]
                        wk = ("w2", w2n % 6)
                        w2n += 1
                        dma('pool', w[:], w_ff2[l * DFF + kc * 128:l * DFF + (kc + 1) * 128, ch * 1024:(ch + 1) * 1024], writes=[wk])
                        for t in range(4):
                            for nq in range(2):
                                S.op('pe', lambda e, w=w, kc=kc, t=t, nq=nq, yb=yb: e.matmul(
                                    banks[yb[t][nq]][:], lhsT=uTq[:, kc, t * 128:(t + 1) * 128], rhs=w[:, nq * 512:(nq + 1) * 512],
                                    start=(kc == 0), stop=(kc == 63)), reads=[wk, ('uTq', kc)], writes=[bk(yb[t][nq])])
                    for t in range(4):
                        tt = 4 * hq + t
                        for nq in range(2):
                            sc = 16 + t * 4 + ch * 2 + nq
                            S.op('act', lambda e, t=t, nq=nq, sc=sc, yb=yb: e.activation(
                                out=sq_junk[:, :], in_=banks[yb[t][nq]][:], func=AF.Square, accum_out=stat[:, sc:sc + 1]),
                                reads=[bk(yb[t][nq])], writes=['sq_junk', ('stat', sc)])
                            if ch == 0:
                                S.op('act', lambda e, t=t, nq=nq, yb=yb: e.copy(out=ystash[:, t, nq * 512:(nq + 1) * 512],
                                                                              in_=banks[yb[t][nq]][:]),
                                     reads=[bk(yb[t][nq])], writes=[('ystash', t, nq)] + hkeys)
                        if ch == 0:
                            continue
                        S.op('dve', lambda e, t=t: e.tensor_reduce(out=stat[:, 12:13], in_=stat[:, 16 + t * 4:20 + t * 4],
                                                                    axis=mybir.AxisListType.X, op=ALU.add),
                             reads=[('stat', 16 + t * 4 + i) for i in range(4)], writes=[('stat', 12)])
                        S.op('dve', lambda e: e.tensor_scalar(out=stat[:, 13:14], in0=stat[:, 12:13], scalar1=1.0 / D, scalar2=EPS,
                                                               op0=ALU.mult, op1=ALU.add), reads=[('stat', 12)], writes=[('stat', 13)])
                        S.op('act', lambda e: e.activation(out=stat[:, 15:16], in_=stat[:, 13:14], func=AF.Ln),
                             reads=[('stat', 13)], writes=[('stat', 15)])
                        S.op('act', lambda e: e.activation(out=stat[:, 14:15], in_=stat[:, 15:16], func=AF.Exp, scale=-0.5),
                             reads=[('stat', 15)], writes=[('stat', 14)])
                        for q in range(4):
                            s = q % 2
                            if q < 2:
                                src, rk = ystash[:, t, q * 512:(q + 1) * 512], [('ystash', t, q)] + hkeys
                            else:
                                src, rk = banks[yb[t][q - 2]][:], [bk(yb[t][q - 2])]
                            S.op('dve', lambda e, q=q, s=s, src=src: e.scalar_tensor_tensor(
                                out=tmp_f[:, s, :], in0=src, scalar=stat[:, 14:15], in1=Gbc[:, 1, q * 512:(q + 1) * 512],
                                op0=ALU.mult, op1=ALU.mult), reads=rk + [('stat', 14), ('Gbc', 1)], writes=[('tmp_f', s)])
                            S.op('dve', lambda e, q=q, s=s, tt=tt: e.tensor_tensor(
                                out=xres[:, tt, q * 512:(q + 1) * 512], in0=xres[:, tt, q * 512:(q + 1) * 512], in1=tmp_f[:, s, :],
                                op=ALU.add), reads=[('x', tt), ('tmp_f', s)], writes=[('x', tt)])
        S.barrier()

    def mixer_a(l):
        lam_init = 0.8 - 0.6 * math.exp(-0.3 * l)
        with ExitStack() as st:
            oT = sb("oT", [128, 16, NT], BF16, st)
            with ExitStack() as st2:
                lam_s = sb("lam_s", [128, 4, 16], F32, st2)
                gsub = sb("gsub", [128, 128], F32, st2)
                st_lam = ExitStack()
                lam_t = sb("lam_t", [128, 4096], F32, st_lam)
                lam_p = sb("lam_p", [128, 2, 1024], F32, st_lam)
                dma('sp', lam_t[:], a_lam[:, :], writes=['lam_t'])
                dma('sp', gsub[:], a_sub[:, :], writes=['gsub0'])
                for i in range(2):
                    S.op('dve', lambda e, i=i: e.tensor_tensor(out=lam_p[:, i, :], in0=lam_t[:, (2 * i) * 1024:(2 * i + 1) * 1024],
                                                                in1=lam_t[:, (2 * i + 1) * 1024:(2 * i + 2) * 1024], op=ALU.mult),
                         reads=['lam_t'], writes=[('lam_p', i)])
                    S.op('dve', lambda e, i=i: e.tensor_reduce(out=lam_s[:, i, :], in_=lam_p[:, i, :].rearrange("p (h d) -> p h d", d=64),
                                                                axis=mybir.AxisListType.X, op=ALU.add),
                         reads=[('lam_p', i)], writes=[('lam_s', i)])
                    S.op('act', lambda e, i=i: e.activation(out=lam_s[:, i, :], in_=lam_s[:, i, :], func=AF.Exp),
                         reads=[('lam_s', i)], writes=[('lam_s', i)])
                S.op('dve', lambda e: e.tensor_tensor(out=lam_s[:, 2, :], in0=lam_s[:, 0, :], in1=lam_s[:, 1, :], op=ALU.subtract),
                     reads=[('lam_s', 0), ('lam_s', 1)], writes=[('lam_s', 2)])
                S.op('dve', lambda e: e.tensor_scalar(out=lam_s[:, 3, :], in0=lam_s[:, 2, :], scalar1=lam_init, scalar2=-1.0,
                                                       op0=ALU.add, op1=ALU.mult), reads=[('lam_s', 2)], writes=[('lam_s', 3)])
                S.op('dve', lambda e: e.tensor_scalar(out=gsub[:], in0=gsub[:], scalar1=1.0 - lam_init, scalar2=None, op0=ALU.mult),
                     reads=['gsub0'], writes=['gsub'])
                S.barrier()
                st_lam.close()
                hT = sb("hT", [128, 16, NT], BF16, st2)
                wh = [sb("wh%d" % i, [128, 16, 3, 128], BF16, st2) for i in range(1)]
                cosq = sb("cosq", [128, NT], F32, st2)
                sinq = sb("sinq", [128, NT], F32, st2)
                cosk = sb("cosk", [128, NT], F32, st2)
                sink_ = sb("sink", [128, NT], F32, st2)
                rot_f = sb("rot_f", [128, 128], F32, st2)
                rot_b = sb("rot_b", [128, 128], BF16, st2)
                maskA = sb("maskA", [128, 48], F32, st2)
                QT = sb("QT", [128, NT], BF16, st2)
                KT = sb("KT", [128, NT + 512], BF16, st2)
                raw = sb("raw", [128, 1, 512], BF16, st2)
                t1 = sb("t1", [128, 1, 512], F32, st2)
                t2 = sb("t2", [128, 1, 512], F32, st2)
                Vaug = sb("Vaug", [128, 12, 132], BF16, st2)
                kvo = sb("kvo", [128, 2, 2, 128], F32, st2)
                cks = sb("cks", [128, 4, 128], BF16, st2)
                Et = sb("Et", [128, 3, 256], BF16, st2)
                osb = sb("osb", [128, 2, 128], F32, st2)
                onb = sb("onb", [128, 2, 128], BF16, st2)
                sst = sb("sst", [128, 16], F32, st2)

                tabk = ['cosq', 'sinq', 'cosk', 'sink']
                for t_, d_, k_ in ((cosq, a_cosq, 'cosq'), (sinq, a_sinq, 'sinq'), (cosk, a_cosk, 'cosk'), (sink_, a_sink, 'sink')):
                    dma('sp', t_[:], d_[:, :], writes=[k_])
                dma('sp', rot_f[:], rotm[:, :], writes=['rot_f'])
                dma('sp', maskA[:], a_mask[:, :], writes=['maskA'])
                S.op('dve', lambda e: e.tensor_copy(out=rot_b[:], in_=rot_f[:]), reads=['rot_f'], writes=['rot_b'])
                S.op('dve', lambda e: e.memset(Vaug[:, :, 128:132], 1.0), writes=['Vaug_ones'])

                adaln_in(0, list(range(8)), hT, 'hT')

                def proj_fm_rope(w, wk, slot, dst, ct, st_, ck_, sk_):
                    for hf in range(2):
                        b = next_bank()
                        for kc in range(16):
                            S.op('pe', lambda e, kc=kc, b=b, hf=hf: e.matmul(
                                banks[b][:], lhsT=w[:, kc, slot, :], rhs=hT[:, kc, hf * 512:(hf + 1) * 512],
                                start=(kc == 0), stop=(kc == 15)), reads=[wk, ('hT', kc)], writes=[bk(b)])
                        S.op('act', lambda e, b=b, hf=hf: e.copy(out=raw[:, 0, :], in_=banks[b][:]),
                             reads=[bk(b)], writes=[('raw', 0)])
                        b2 = next_bank()
                        S.op('pe', lambda e, b2=b2, hf=hf: e.matmul(banks[b2][:], lhsT=rot_b[:], rhs=raw[:, 0, :],
                                                                    start=True, stop=True),
                             reads=['rot_b', ('raw', 0)], writes=[bk(b2)])
                        S.op('dve', lambda e, b=b, hf=hf: e.tensor_tensor(out=t1[:, 0, :], in0=banks[b][:],
                                                                           in1=ct[:, hf * 512:(hf + 1) * 512], op=ALU.mult),
                             reads=[bk(b), ck_], writes=[('t1', 0)])
                        S.op('dve', lambda e, b2=b2, hf=hf: e.tensor_tensor(out=t2[:, 0, :], in0=banks[b2][:],
                                                                             in1=st_[:, hf * 512:(hf + 1) * 512], op=ALU.mult),
                             reads=[bk(b2), sk_], writes=[('t2', 0)])
                        S.op('dve', lambda e, hf=hf: e.tensor_tensor(out=dst[:, hf * 512:(hf + 1) * 512], in0=t1[:, 0, :],
                                                                      in1=t2[:, 0, :], op=ALU.add),
                             reads=[('t1', 0), ('t2', 0)], writes=[('dstrope', id(dst))])

                for h in range(16):
                    w = wh[0]
                    wk = ('wh', 0)
                    for sl in range(3):
                        src = a_w_in[:, sl * D + h * 128:sl * D + (h + 1) * 128].rearrange("(c p) n -> p c n", p=128)
                        dma('pool', w[:, :, sl, :], src, writes=[wk])
                    proj_fm_rope(w, wk, 0, QT, cosq, sinq, tabk[0], tabk[1])
                    proj_fm_rope(w, wk, 1, KT, cosk, sink_, tabk[2], tabk[3])
                    for tt in range(8):
                        b = next_bank()
                        for kc in range(16):
                            S.op('pe', lambda e, kc=kc, b=b, tt=tt: e.matmul(
                                banks[b][:, 0:256], lhsT=hT[:, kc, tt * 128:(tt + 1) * 128],
                                rhs=w[:, kc, 1:3, :].rearrange("p a b -> p (a b)"),
                                start=(kc == 0), stop=(kc == 15)), reads=[wk, ('hT', kc)], writes=[bk(b)])
                        S.op('act', lambda e, b=b, tt=tt: e.copy(out=kvo[:, :, tt % 2, :],
                                                                in_=banks[b][:, 0:256].rearrange("p (a b) -> p a b", a=2)),
                             reads=[bk(b)], writes=[('kvo', tt % 2)])
                        S.op('dve', lambda e, b=b, tt=tt: e.tensor_copy(out=Vaug[:, tt, 0:128], in_=banks[b][:, 128:256]),
                             reads=[bk(b)], writes=[('Vaug', tt)])
                        dma('sp', a_ko[tt * 128:(tt + 1) * 128, h * 128:(h + 1) * 128], kvo[:, 0, tt % 2, :],
                            reads=[('kvo', tt % 2)], pool='st')
                        dma('sp', a_vo[tt * 128:(tt + 1) * 128, h * 128:(h + 1) * 128], kvo[:, 1, tt % 2, :],
                            reads=[('kvo', tt % 2)], pool='st')
                    dma('pool', cks[:], a_ck[:, h * 128:(h + 1) * 128].rearrange("(t p) d -> p t d", p=128), writes=['cks'])
                    dma('pool', Vaug[:, 8:12, 0:128], a_cv[:, h * 128:(h + 1) * 128].rearrange("(t p) d -> p t d", p=128),
                        writes=[('Vaug', 8 + i) for i in range(4)])
                    b = next_bank()
                    pb = banks[b][:].bitcast(BF16)
                    for i in range(4):
                        S.op('pe', lambda e, i=i, pb=pb: e.transpose(out=pb[:, i * 128:(i + 1) * 128], in_=cks[:, i, :],
                                                                      identity=ident_b[:]),
                             reads=['cks', 'ident_b'], writes=[bk(b)])
                    S.op('act', lambda e, pb=pb: e.copy(out=KT[:, NT:NT + 512], in_=pb[:, 0:512]),
                         reads=[bk(b)], writes=['KTctx'])
                    for qp in range(4):
                        ob = [[next_bank() for _ in range(2)] for _ in range(2)]
                        its = [(j, kc) for j in range(2) for kc in range(12)]
                        sbk = {}

                        def st_score(n, qp=qp, ob=ob, its=its, sbk=sbk):
                            j, kc = its[n]
                            pr = slice(j * 64, (j + 1) * 64)
                            b = next_bank()
                            while any(b in r for r in ob):
                                b = next_bank()
                            sbk[n] = b
                            S.op('pe', lambda e: e.matmul(
                                banks[b][:, 0:256], lhsT=KT[pr, kc * 128:(kc + 1) * 128], rhs=QT[pr, qp * 256:(qp + 1) * 256],
                                start=True, stop=True),
                                reads=[('dstrope', id(KT)), ('dstrope', id(QT)), 'KTctx'], writes=[bk(b)])

                        def st_exp(n, qp=qp, its=its, sbk=sbk):
                            j, kc = its[n]
                            b = sbk[n]
                            es_ = n % 3
                            S.op('act', lambda e: e.activation(
                                out=Et[:, es_, :], in_=banks[b][:, 0:256], func=AF.Exp,
                                bias=maskA[:, kc * 4 + qp:kc * 4 + qp + 1], scale=1.0),
                                reads=[bk(b), 'maskA'], writes=[('Et', es_)])

                        def st_pv(n, ob=ob, its=its):
                            j, kc = its[n]
                            es_ = n % 3
                            for qq in range(2):
                                S.op('pe', lambda e: e.matmul(
                                    banks[ob[j][qq]][:, 0:129], lhsT=Et[:, es_, qq * 128:(qq + 1) * 128], rhs=Vaug[:, kc, 0:129],
                                    start=(kc == 0), stop=(kc == 11)),
                                    reads=[('Et', es_), ('Vaug', kc), 'Vaug_ones'], writes=[bk(ob[j][qq])])

                        LA = 2
                        for t in range(len(its) + LA):
                            if t < len(its):
                                st_score(t)
                            if 1 <= t <= len(its):
                                st_exp(t - 1)
                            if t >= LA:
                                st_pv(t - LA)
                        for qq in range(2):
                            tt = qp * 2 + qq
                            o1, o2 = banks[ob[0][qq]], banks[ob[1][qq]]
                            S.op('dve', lambda e, o1=o1: e.reciprocal(out=sst[:, 0:1], in_=o1[:, 128:129]),
                                 reads=[bk(ob[0][qq])], writes=[('sst', 0)])
                            S.op('dve', lambda e, o2=o2: e.reciprocal(out=sst[:, 1:2], in_=o2[:, 128:129]),
                                 reads=[bk(ob[1][qq])], writes=[('sst', 1)])
                            S.op('dve', lambda e, h=h: e.tensor_tensor(out=sst[:, 2:3], in0=sst[:, 1:2], in1=lam_s[:, 3, h:h + 1],
                                                                        op=ALU.mult),
                                 reads=[('sst', 1), ('lam_s', 3)], writes=[('sst', 2)])
                            S.op('dve', lambda e, o1=o1, qq=qq: e.tensor_scalar(out=osb[:, qq, :], in0=o1[:, 0:128], scalar1=sst[:, 0:1],
                                                                                 scalar2=None, op0=ALU.mult),
                                 reads=[bk(ob[0][qq]), ('sst', 0)], writes=[('osb', qq)])
                            S.op('dve', lambda e, o2=o2, qq=qq: e.scalar_tensor_tensor(
                                out=osb[:, qq, :], in0=o2[:, 0:128], scalar=sst[:, 2:3], in1=osb[:, qq, :], op0=ALU.mult, op1=ALU.add),
                                reads=[bk(ob[1][qq]), ('sst', 2), ('osb', qq)], writes=[('osb', qq)])
                            S.op('act', lambda e, qq=qq: e.activation(out=sq_junk[:, 0:128], in_=osb[:, qq, :], func=AF.Square,
                                                                       accum_out=sst[:, 3:4]),
                                 reads=[('osb', qq)], writes=['sq_junk', ('sst', 3)])
                            S.op('dve', lambda e: e.tensor_scalar(out=sst[:, 4:5], in0=sst[:, 3:4], scalar1=1.0 / 128, scalar2=EPS,
                                                                   op0=ALU.mult, op1=ALU.add), reads=[('sst', 3)], writes=[('sst', 4)])
                            S.op('act', lambda e: e.activation(out=sst[:, 6:7], in_=sst[:, 4:5], func=AF.Ln),
                                 reads=[('sst', 4)], writes=[('sst', 6)])
                            S.op('act', lambda e: e.activation(out=sst[:, 5:6], in_=sst[:, 6:7], func=AF.Exp, scale=-0.5),
                                 reads=[('sst', 6)], writes=[('sst', 5)])
                            S.op('dve', lambda e, qq=qq: e.scalar_tensor_tensor(
                                out=onb[:, qq, :], in0=osb[:, qq, :], scalar=sst[:, 5:6], in1=gsub[:], op0=ALU.mult, op1=ALU.mult),
                                reads=[('osb', qq), ('sst', 5), 'gsub'], writes=[('onb', qq)])
                            b = next_bank()
                            pb = banks[b][:].bitcast(BF16)
                            S.op('pe', lambda e, qq=qq, pb=pb: e.transpose(out=pb[:, 0:128], in_=onb[:, qq, :], identity=ident_b[:]),
                                 reads=[('onb', qq), 'ident_b'], writes=[bk(b)])
                            S.op('act', lambda e, pb=pb, h=h, tt=tt: e.copy(out=oT[:, h, tt * 128:(tt + 1) * 128], in_=pb[:, 0:128]),
                                 reads=[bk(b)], writes=[('oT', h)])
            S.barrier()
            out_proj(a_w_out, oT, st)
        S.barrier()

    NA_L = [[0, 1, 2, 3], [0, 1, 2, 3], [0, 1, 2, 3, 4], [1, 2, 3, 4, 5], [2, 3, 4, 5, 6], [3, 4, 5, 6, 7], [4, 5, 6, 7], [4, 5, 6, 7]]

    def mixer_cd(l, kind):
        isd = (kind == 'd')
        w_in_d, w_out_d = (d_w_in, d_w_out) if isd else (c_w_in, c_w_out)
        ck_d, cv_d = (d_ck, d_cv) if isd else (c_ck, c_cv)
        k_out, v_out = (d_ko, d_vo) if isd else (c_ko, c_vo)
        dv = 64 if isd else 128
        nslot = 4 if isd else 3
        with ExitStack() as st:
            oT = sb("oT", [128, 16, NT], BF16, st)
            with ExitStack() as st2:
                hT = sb("hT", [128, 16, NT], BF16, st2)
                w = sb("wcd", [128, 16, nslot, 128], BF16, st2)
                wk = 'wcd'
                QT = sb("QT", [128, 2 if isd else 1, NT], BF16, st2)
                KT = sb("KT", [128, NT + 512], BF16, st2)
                Vaug = sb("Vaug", [128, 12, dv + 4], BF16, st2)
                kvo = sb("kvo", [128, 2, 2, dv], F32, st2)
                cks = sb("cks", [128, 4, 128], BF16, st2)
                Et = sb("Et", [128, 4, 128], BF16, st2)
                sbs = sb("sbs", [128, 4, 128], F32, st2)
                onb = sb("onb", [128, 2, 256 if isd else 128], BF16, st2)
                sst = sb("sst", [128, 16], F32, st2)
                ctxb = sb("ctxb_s", [128, 1], F32, st2)
                dma('sp', ctxb[:], ctxb_d[:, :], writes=['ctxb'])
                if isd:
                    cosk = sb("cosk", [128, NT], F32, st2)
                    sink_ = sb("sink", [128, NT], F32, st2)
                    rot_f = sb("rot_f", [128, 128], F32, st2)
                    rot_b = sb("rot_b", [128, 128], BF16, st2)
                    raw = sb("raw", [128, 512], BF16, st2)
                    maskD = sb("maskD", [128, 2, 8, 128], F32, st2)
                    snk = sb("snk", [128, 32], F32, st2)
                    dma('sp', cosk[:], a_cosk[:, :], writes=['cosk'])
                    dma('sp', sink_[:], a_sink[:, :], writes=['sink'])
                    dma('sp', rot_f[:], rotm[:, :], writes=['rot_f'])
                    dma('sp', maskD[:], d_mask[:, :].rearrange("p (a b c) -> p a b c", a=2, b=8), writes=['maskD'])
                    dma('sp', snk[:], d_snk[:, :], writes=['snk0'])
                    S.op('dve', lambda e: e.tensor_copy(out=rot_b[:], in_=rot_f[:]), reads=['rot_f'], writes=['rot_b'])
                    S.op('act', lambda e: e.activation(out=snk[:], in_=snk[:], func=AF.Exp), reads=['snk0'], writes=['snk'])
                else:
                    biasC = sb("biasC", [128, 2, 5, 128], F32, st2)
                S.op('dve', lambda e: e.memset(Vaug[:, :, dv:dv + 4], 1.0), writes=['Vaug_ones'])
                adaln_in(0, list(range(8)), hT, 'hT')

                def proj_fm(slot, dst, dkey, rope, scale):
                    for hf in range(2):
                        b = next_bank()
                        for kc in range(16):
                            S.op('pe', lambda e, kc=kc, b=b, hf=hf: e.matmul(
                                banks[b][:], lhsT=w[:, kc, slot, :], rhs=hT[:, kc, hf * 512:(hf + 1) * 512],
                                start=(kc == 0), stop=(kc == 15)), reads=[wk, ('hT', kc)], writes=[bk(b)])
                        dsl = dst[:, hf * 512:(hf + 1) * 512]
                        if not rope:
                            S.op('act', lambda e, b=b: e.mul(out=dsl, in_=banks[b][:], mul=scale), reads=[bk(b)], writes=[dkey])
                            continue
                        S.op('act', lambda e, b=b: e.copy(out=raw[:], in_=banks[b][:]), reads=[bk(b)], writes=['raw'])
                        b2 = next_bank()
                        S.op('pe', lambda e, b2=b2: e.matmul(banks[b2][:], lhsT=rot_b[:], rhs=raw[:], start=True, stop=True),
                             reads=['rot_b', 'raw'], writes=[bk(b2)])
                        S.op('dve', lambda e, b=b, hf=hf: e.tensor_tensor(out=tmp_f[:, 0, :], in0=banks[b][:],
                                                                           in1=cosk[:, hf * 512:(hf + 1) * 512], op=ALU.mult),
                             reads=[bk(b), 'cosk'], writes=[('tmp_f', 0)])
                        S.op('dve', lambda e, b2=b2, hf=hf: e.tensor_tensor(out=tmp_f[:, 1, :], in0=banks[b2][:],
                                                                             in1=sink_[:, hf * 512:(hf + 1) * 512], op=ALU.mult),
                             reads=[bk(b2), 'sink'], writes=[('tmp_f', 1)])
                        S.op('dve', lambda e: e.tensor_tensor(out=dsl, in0=tmp_f[:, 0, :], in1=tmp_f[:, 1, :], op=ALU.add),
                             reads=[('tmp_f', 0), ('tmp_f', 1)], writes=[dkey])

                ngrp = 8 if isd else 16
                for g in range(ngrp):
                    rs = lambda a: a.rearrange("(c p) n -> p c n", p=128)
                    if isd:
                        for ci in range(2):
                            dma('pool', w[:, :, ci, :], rs(w_in_d[:, g * 256 + ci * 128:g * 256 + (ci + 1) * 128]), writes=[wk])
                        kcols = rs(w_in_d[:, 2048 + g * 64:2048 + (g + 1) * 64])
                        vcols = rs(w_in_d[:, 2560 + g * 64:2560 + (g + 1) * 64])
                        dma('pool', w[:, :, 2, 0:64], kcols, writes=[wk])
                        dma('pool', w[:, :, 2, 64:128], vcols, writes=[wk])
                        dma('pool', w[:, :, 3, 0:64], kcols, writes=[wk])
                        dma('pool', w[:, :, 3, 64:128], kcols, writes=[wk])
                        proj_fm(0, QT[:, 0, :], 'QT', True, 1.0)
                        proj_fm(1, QT[:, 1, :], 'QT', True, 1.0)
                        proj_fm(3, KT[:, 0:NT], 'KT', True, 1.0)
                        kvslot = w[:, :, 2, :]
                        nkv = 128
                    else:
                        for sl in range(3):
                            dma('pool', w[:, :, sl, :], rs(w_in_d[:, sl * D + g * 128:sl * D + (g + 1) * 128]), writes=[wk])
                        proj_fm(0, QT[:, 0, :], 'QT', False, 128 ** -0.5)
                        proj_fm(1, KT[:, 0:NT], 'KT', False, 1.0)
                        kvslot = w[:, :, 1:3, :].rearrange("p c a b -> p c (a b)")
                        nkv = 256
                    for tt in range(8):
                        b = next_bank()
                        for kc in range(16):
                            S.op('pe', lambda e, kc=kc, b=b, tt=tt: e.matmul(
                                banks[b][:, 0:nkv], lhsT=hT[:, kc, tt * 128:(tt + 1) * 128], rhs=kvslot[:, kc, :],
                                start=(kc == 0), stop=(kc == 15)), reads=[wk, ('hT', kc)], writes=[bk(b)])
                        S.op('act', lambda e, b=b, tt=tt: e.copy(out=kvo[:, :, tt % 2, :],
                                                                in_=banks[b][:, 0:nkv].rearrange("p (a b) -> p a b", a=2)),
                             reads=[bk(b)], writes=[('kvo', tt % 2)])
                        S.op('dve', lambda e, b=b, tt=tt: e.tensor_copy(out=Vaug[:, tt, 0:dv], in_=banks[b][:, dv:2 * dv]),
                             reads=[bk(b)], writes=[('Vaug', tt)])
                        dma('sp', k_out[tt * 128:(tt + 1) * 128, g * dv:(g + 1) * dv], kvo[:, 0, tt % 2, :],
                            reads=[('kvo', tt % 2)], pool='st')
                        dma('sp', v_out[tt * 128:(tt + 1) * 128, g * dv:(g + 1) * dv], kvo[:, 1, tt % 2, :],
                            reads=[('kvo', tt % 2)], pool='st')
                    csrc = ck_d[:, g * dv:(g + 1) * dv].rearrange("(t p) d -> p t d", p=128)
                    if isd:
                        dma('pool', cks[:, :, 0:64], csrc, writes=['cks'])
                        dma('pool', cks[:, :, 64:128], csrc, writes=['cks'])
                    else:
                        dma('pool', cks[:], csrc, writes=['cks'])
                    dma('pool', Vaug[:, 8:12, 0:dv], cv_d[:, g * dv:(g + 1) * dv].rearrange("(t p) d -> p t d", p=128),
                        writes=[('Vaug', 8 + i) for i in range(4)])
                    b = next_bank()
                    pb = banks[b][:].bitcast(BF16)
                    for i in range(4):
                        S.op('pe', lambda e, i=i, pb=pb: e.transpose(out=pb[:, i * 128:(i + 1) * 128], in_=cks[:, i, :],
                                                                      identity=ident_b[:]),
                             reads=['cks', 'ident_b'], writes=[bk(b)])
                    S.op('act', lambda e, pb=pb: e.copy(out=KT[:, NT:NT + 512], in_=pb[:, 0:512]), reads=[bk(b)], writes=['KTctx'])
                    items = []
                    for qt in range(8):
                        if isd:
                            chunks = [(qt + r, ri) for r, ri in ((-1, 0), (0, None), (1, 1)) if 0 <= qt + r < 8]
                        else:
                            chunks = [(kc, si) for si, kc in enumerate(NA_L[qt])]
                        chunks += [(8 + i, 'ctx') for i in range(4)]
                        for gi in range(4 if isd else 1):
                            for n, (kc, mk) in enumerate(chunks):
                                items.append((qt, gi, n, kc, mk, len(chunks)))
                    sbk = {}
                    obk = {}
                    live_ob = set()
                    esc = 0.125 if isd else 1.0
                    ngi = 4 if isd else 1

                    def prq(gi):
                        return slice((gi % 2) * 64, (gi % 2 + 1) * 64) if isd else slice(0, 128)

                    def st_score(t):
                        qt, gi, n, kc, mk, nch = items[t]
                        if n == 0:
                            if gi == 0 and not isd:
                                bslot = (g * 8 + qt) % 2
                                dma('sp', biasC[:, bslot, :, :], c_bias[(g * 8 + qt) * 128:(g * 8 + qt + 1) * 128, :].rearrange(
                                    "p (a b) -> p a b", a=5), writes=[('biasC', bslot)])
                            ob = next_bank()
                            while ob in live_ob:
                                ob = next_bank()
                            obk[(qt, gi)] = ob
                            live_ob.add(ob)
                        b = next_bank()
                        while b in live_ob:
                            b = next_bank()
                        sbk[t] = b
                        pr = prq(gi)
                        qsrc = QT[pr, gi // 2, qt * 128:(qt + 1) * 128]
                        S.op('pe', lambda e: e.matmul(
                            banks[b][:, 0:128], lhsT=KT[pr, kc * 128:(kc + 1) * 128], rhs=qsrc, start=True, stop=True),
                            reads=['KT', 'QT', 'KTctx'], writes=[bk(b)])

                    def st_exp(t):
                        qt, gi, n, kc, mk, nch = items[t]
                        b = sbk[t]
                        es_ = t % 4
                        if mk == 'ctx':
                            S.op('act', lambda e: e.activation(out=Et[:, es_, :], in_=banks[b][:, 0:128], func=AF.Exp,
                                                               bias=ctxb[:, 0:1], scale=esc),
                                 reads=[bk(b), 'ctxb'], writes=[('Et', es_)])
                        elif mk is None:
                            S.op('act', lambda e: e.activation(out=Et[:, es_, :], in_=banks[b][:, 0:128], func=AF.Exp, scale=esc),
                                 reads=[bk(b)], writes=[('Et', es_)])
                        else:
                            bslot = (g * 8 + qt) % 2
                            msrc = maskD[:, mk, qt, :] if isd else biasC[:, bslot, mk, :]
                            mkey = 'maskD' if isd else ('biasC', bslot)
                            S.op('dve', lambda e: e.tensor_tensor(out=sbs[:, es_, :], in0=banks[b][:, 0:128], in1=msrc, op=ALU.add),
                                 reads=[bk(b), mkey], writes=[('sbs', es_)])
                            S.op('act', lambda e: e.activation(out=Et[:, es_, :], in_=sbs[:, es_, :], func=AF.Exp, scale=esc),
                                 reads=[('sbs', es_)], writes=[('Et', es_)])

                    def st_pv(t):
                        qt, gi, n, kc, mk, nch = items[t]
                        es_ = t % 4
                        ob = obk[(qt, gi)]
                        S.op('pe', lambda e: e.matmul(
                            banks[ob][:, 0:dv + 1], lhsT=Et[:, es_, :], rhs=Vaug[:, kc, 0:dv + 1],
                            start=(n == 0), stop=(n == nch - 1)),
                            reads=[('Et', es_), ('Vaug', kc), 'Vaug_ones'], writes=[bk(ob)])
                        if n != nch - 1:
                            return
                        os_ = qt % 2
                        if isd:
                            hq = g * 4 + gi
                            S.op('dve', lambda e: e.tensor_tensor(out=sst[:, 0:1], in0=banks[ob][:, dv:dv + 1], in1=snk[:, hq:hq + 1],
                                                                  op=ALU.add), reads=[bk(ob), 'snk'], writes=[('sst', 0)])
                            S.op('dve', lambda e: e.reciprocal(out=sst[:, 1:2], in_=sst[:, 0:1]), reads=[('sst', 0)], writes=[('sst', 1)])
                        else:
                            S.op('dve', lambda e: e.reciprocal(out=sst[:, 1:2], in_=banks[ob][:, dv:dv + 1]),
                                 reads=[bk(ob)], writes=[('sst', 1)])
                        S.op('dve', lambda e: e.tensor_scalar(out=onb[:, os_, gi * dv:(gi + 1) * dv], in0=banks[ob][:, 0:dv],
                                                               scalar1=sst[:, 1:2], scalar2=None, op0=ALU.mult),
                             reads=[bk(ob), ('sst', 1)], writes=[('onb', os_)])
                        live_ob.discard(ob)
                        if gi != ngi - 1:
                            return
                        for ci in range(2 if isd else 1):
                            b = next_bank()
                            pb = banks[b][:].bitcast(BF16)
                            S.op('pe', lambda e: e.transpose(out=pb[:, 0:128], in_=onb[:, os_, ci * 128:(ci + 1) * 128],
                                                             identity=ident_b[:]),
                                 reads=[('onb', os_), 'ident_b'], writes=[bk(b)])
                            och = (2 * g + ci) if isd else g
                            S.op('act', lambda e: e.copy(out=oT[:, och, qt * 128:(qt + 1) * 128], in_=pb[:, 0:128]),
                                 reads=[bk(b)], writes=[('oT', och)])

                    LA = 2
                    for t in range(len(items) + LA):
                        if t < len(items):
                            st_score(t)
                        if 1 <= t <= len(items):
                            st_exp(t - 1)
                        if t >= LA:
                            st_pv(t - LA)
            S.barrier()
            out_proj(w_out_d, oT, st)
        S.barrier()

    def mixer_b(l):
        with ExitStack() as st:
            oT = sb("oT", [128, 16, NT], BF16, st)
            with ExitStack() as st2:
                hT = sb("hT", [128, 16, NT], BF16, st2)
                wt = [sb("wb%d" % i, [128, 16, 256], BF16, st2) for i in range(1)]
                wg = sb("wg", [128, 16, 32], BF16, st2)
                gb = sb("gb", [128, 32], F32, st2)
                nrm = sb("nrm", [128, 256], F32, st2)
                keep = sb("keep", [128, 1], F32, st2)
                m0e = sb("m0e", [128, 16], F32, st2)
                Um = sb("Um", [128, 2, 128], F32, st2)
                Mk = sb("Mk", [128, 2, 128], F32, st2)
                G = sb("G", [128, 256], F32, st2)
                LF = sb("LF", [128, 256], F32, st2)
                BB = tmp_f[:, 1, 0:256]
                AB = sb("AB", [128, 128], F32, st2)
                EE = sb("EE", [128, 128], F32, st2)
                WL = sb("WL", [128, 128], F32, st2)
                DT = sb("DT", [128, 128], F32, st2)
                BT8 = sb("BT8", [8, 16], F32, st2)
                EM8 = sb("EM8", [8, 16], F32, st2)
                M8 = sb("M8", [8, 8], F32, st2)
                T8 = sb("T8", [8, 2], F32, st2)
                SC8 = sb("SC8", [8, 8], F32, st2)
                d8 = sb("d8", [8, 8, 8], F32, st2)
                SCB = sb("SCB", [128, 64], F32, st2)
                QT = sb("QT", [128, NT], BF16, st2)
                KT = sb("KT", [128, NT], BF16, st2)
                Ktm = sb("Ktm", [128, 8, 128], BF16, st2)
                Vaug = sb("Vaug", [128, 8, 260], BF16, st2)
                SIGO = sb("SIGO", [128, 2, 256], F32, st2)
                Hs = sb("Hs", [128, 8, 256], F32, st2)
                S32 = sb("S32", [128, 2, 257], F32, st2)
                Sbf = sb("Sbf", [128, 2, 260], BF16, st2)
                outS = sb("outS", [128, 2, 257], F32, st2)
                lfbc = sb("lfbc", [128, 2, 128], F32, st2)
                tD = sb("tD", [128, 2, 128], F32, st2)
                Dt = sb("Dt", [128, 2, 128], F32, st2)
                EB = sb("EB", [128, 2, 128], F32, st2)
                Qs = sb("Qs", [128, 2, 128], BF16, st2)
                PT = sb("PT", [128, 2, 128], BF16, st2)
                Kw = sb("Kw", [128, 2, 128], BF16, st2)
                rr = sb("rr", [128, 2, 4], F32, st2)
                hn = tmp_f[:, 0, 0:256]
                onb = sb("onb", [128, 2, 256], BF16, st2)
                sst = sb("sst", [128, 8], F32, st2)

                dma('sp', gb[:], b_gb[:, :], writes=['gb'])
                dma('sp', keep[:], b_keep[:, :], writes=['keep'])
                dma('sp', m0e[:], b_m0[:, :], writes=['m0raw'])
                dma('sp', Um[:], b_um[:, :].rearrange("p (a b) -> p a b", a=2), writes=['Um'])
                dma('sp', Mk[:], b_mk[:, :].rearrange("p (a b) -> p a b", a=2), writes=['Mk'])
                dma('pool', wg[:], b_w_in[:, 6144:6176].rearrange("(c p) n -> p c n", p=128), writes=['wg'])
                S.op('act', lambda e: e.activation(out=m0e[:], in_=m0e[:], func=AF.Exp), reads=['m0raw'], writes=['m0e'])
                S.op('dve', lambda e: e.memset(Vaug[:, :, 256:260], 1.0), writes=['Vaug_ones'])
                adaln_in(0, list(range(8)), hT, 'hT')

                bg = next_bank()
                for tt in range(8):
                    for kc in range(16):
                        S.op('pe', lambda e, tt=tt, kc=kc: e.matmul(
                            banks[bg][:, tt * 32:(tt + 1) * 32], lhsT=hT[:, kc, tt * 128:(tt + 1) * 128], rhs=wg[:, kc, :],
                            start=(kc == 0), stop=(kc == 15)), reads=['wg', ('hT', kc)], writes=[bk(bg)])
                for tt in range(8):
                    S.op('dve', lambda e, tt=tt: e.tensor_tensor(out=G[:, tt * 32:(tt + 1) * 32], in0=banks[bg][:, tt * 32:(tt + 1) * 32],
                                                                  in1=gb[:], op=ALU.add), reads=[bk(bg), 'gb'], writes=['G'])
                S.op('act', lambda e: e.activation(out=LF[:], in_=G[:], func=AF.Exp, scale=-1.0), reads=['G'], writes=['LF'])
                S.op('dve', lambda e: e.tensor_scalar(out=LF[:], in0=LF[:], scalar1=1.0, scalar2=None, op0=ALU.add),
                     reads=['LF'], writes=['LF'])
                S.op('act', lambda e: e.activation(out=LF[:], in_=LF[:], func=AF.Ln), reads=['LF'], writes=['LF'])
                S.op('dve', lambda e: e.tensor_scalar(out=LF[:], in0=LF[:], scalar1=-1.0, scalar2=None, op0=ALU.mult),
                     reads=['LF'], writes=['LF'])
                bb_ = next_bank()
                for c in range(8):
                    for d in range(2):
                        cd = c * 2 + d
                        lf_cd = LF[:, c * 32 + (2 * d + 1) * 8:c * 32 + (2 * d + 1) * 8 + 8]
                        S.op('pe', lambda e, cd=cd, d=d, lf_cd=lf_cd: e.matmul(
                            banks[bb_][:, cd * 16:cd * 16 + 8], lhsT=Um[:, d, :], rhs=lf_cd, start=True, stop=True),
                            reads=['Um', 'LF'], writes=[bk(bb_)])
                        S.op('pe', lambda e, cd=cd, lf_cd=lf_cd: e.matmul(
                            banks[bb_][:, cd * 16 + 8:cd * 16 + 16], lhsT=ones_f[:], rhs=lf_cd, start=True, stop=True),
                            reads=['ones_f', 'LF'], writes=[bk(bb_)])
                S.op('act', lambda e: e.copy(out=BB, in_=banks[bb_][:, 0:256]), reads=[bk(bb_)], writes=['BB'])
                for c in range(8):
                    for d in range(2):
                        cd = c * 2 + d
                        ig = G[:, c * 32 + 2 * d * 8:c * 32 + 2 * d * 8 + 8]
                        S.op('dve', lambda e, cd=cd, ig=ig: e.tensor_tensor(out=AB[:, cd * 8:cd * 8 + 8], in0=ig,
                                                                             in1=BB[:, cd * 16:cd * 16 + 8], op=ALU.subtract),
                             reads=['G', 'BB'], writes=['AB'])
                        S.op('dve', lambda e, cd=cd: e.tensor_tensor(out=EE[:, cd * 8:cd * 8 + 8], in0=AB[:, cd * 8:cd * 8 + 8],
                                                                      in1=BB[:, cd * 16 + 8:cd * 16 + 16], op=ALU.add),
                             reads=['AB', 'BB'], writes=['EE'])
                        S.op('act', lambda e, cd=cd: e.activation(out=DT[:, cd * 8:cd * 8 + 8], in_=BB[:, cd * 16 + 8:cd * 16 + 16],
                                                                   func=AF.Exp), reads=['BB'], writes=['DT'])
                S.op('act', lambda e: e.activation(out=WL[:], in_=EE[:], func=AF.Exp), reads=['EE'], writes=['WL'])
                bt_ = next_bank()
                for cd in range(16):
                    c, d = divmod(cd, 2)
                    lf_cd = LF[:, c * 32 + (2 * d + 1) * 8:c * 32 + (2 * d + 1) * 8 + 8]
                    S.op('pe', lambda e, cd=cd, lf_cd=lf_cd: e.matmul(banks[bt_][0:8, cd:cd + 1], lhsT=lf_cd, rhs=ones_f[:, 0:1],
                                                                      start=True, stop=True),
                         reads=['LF', 'ones_f'], writes=[bk(bt_)])
                S.op('act', lambda e: e.copy(out=BT8[:], in_=banks[bt_][0:8, 0:16]), reads=[bk(bt_)], writes=['BT8'])
                for g4 in range(4):
                    be = next_bank()
                    for i in range(4):
                        cd = g4 * 4 + i
                        S.op('pe', lambda e, cd=cd, i=i, be=be: e.matmul(banks[be][0:8, i * 128:(i + 1) * 128],
                                                                        lhsT=EE[:, cd * 8:cd * 8 + 8], rhs=ident_f[:],
                                                                        start=True, stop=True),
                             reads=['EE', 'ident_f'], writes=[bk(be)])
                    S.op('dve', lambda e, g4=g4, be=be: e.tensor_reduce(
                        out=EM8[:, g4 * 4:(g4 + 1) * 4], in_=banks[be][0:8, :].rearrange("p (a b) -> p a b", a=4),
                        axis=mybir.AxisListType.X, op=ALU.max), reads=[bk(be)], writes=['EM8'])
                for j in range(4):
                    for d in range(2):
                        ca, cb = (2 * j, 2 * j + 1) if d == 0 else (2 * j + 1, 2 * j)
                        a_, b_ = ca * 2 + d, cb * 2 + d
                        jd = j * 2 + d
                        S.op('dve', lambda e, a_=a_: e.tensor_tensor(out=T8[:, 0:1], in0=BT8[:, a_:a_ + 1], in1=EM8[:, a_:a_ + 1],
                                                                      op=ALU.max), reads=['BT8', 'EM8'], writes=[('T8', 0)])
                        S.op('dve', lambda e, b_=b_: e.tensor_tensor(out=T8[:, 1:2], in0=T8[:, 0:1], in1=BT8[:, b_:b_ + 1],
                                                                      op=ALU.add), reads=['BT8', ('T8', 0)], writes=[('T8', 1)])
                        S.op('dve', lambda e, b_=b_, jd=jd: e.tensor_tensor(out=M8[:, jd:jd + 1], in0=T8[:, 1:2], in1=EM8[:, b_:b_ + 1],
                                                                             op=ALU.max), reads=['EM8', ('T8', 1)], writes=['M8'])
                S.op('act', lambda e: e.activation(out=SC8[:], in_=M8[:], func=AF.Exp, scale=-1.0), reads=['M8'], writes=['SC8'])
                dma('sp', b_mo[:, :], M8[:], reads=['M8'], pool='st')
                bs_ = next_bank()
                for jd in range(8):
                    S.op('dve', lambda e, jd=jd: e.tensor_scalar(out=d8[:, jd, :], in0=ident_f[0:8, 0:8], scalar1=SC8[:, jd:jd + 1],
                                                                  scalar2=None, op0=ALU.mult),
                         reads=['ident_f', 'SC8'], writes=[('d8', jd)])
                    S.op('pe', lambda e, jd=jd: e.matmul(banks[bs_][:, jd * 8:(jd + 1) * 8], lhsT=ones_f[0:8, :], rhs=d8[:, jd, :],
                                                         start=True, stop=True),
                         reads=['ones_f', ('d8', jd)], writes=[bk(bs_)])
                S.op('act', lambda e: e.copy(out=SCB[:], in_=banks[bs_][:, 0:64]), reads=[bk(bs_)], writes=['SCB'])

                wi = [0]
                hs_done = set()

                def record(fn):
                    saved = []
                    S.op = lambda *a, **k: saved.append((a, k))
                    try:
                        fn()
                    finally:
                        del S.op
                    return saved

                def loadw(col_ranges):
                    w = wt[0]
                    wk = ('wb', 0)
                    wi[0] += 1
                    o_ = 0
                    for c0, n_ in col_ranges:
                        dma('pool', w[:, :, o_:o_ + n_], b_w_in[:, c0:c0 + n_].rearrange("(c p) n -> p c n", p=128), writes=[wk])
                        o_ += n_
                    return w, wk

                def chunk(c, d, h, s_):
                    col = (c * 2 + d) * 8 + h
                    lfi = c * 32 + (2 * d + 1) * 8 + h
                    S.op('dve', lambda e: e.tensor_scalar(out=lfbc[:, s_, :], in0=ones_f[:], scalar1=LF[:, lfi:lfi + 1], scalar2=None,
                                                           op0=ALU.mult), reads=['LF', 'ones_f'], writes=[('lfbc', s_)])
                    br = next_bank()
                    S.op('pe', lambda e: e.matmul(banks[br][:, 0:128], lhsT=lfbc[:, s_, :], rhs=Um[:, d, :], start=True, stop=True),
                         reads=[('lfbc', s_), 'Um'], writes=[bk(br)])
                    S.op('dve', lambda e: e.tensor_tensor(out=tD[:, s_, :], in0=banks[br][:, 0:128], in1=Mk[:, d, :], op=ALU.add),
                         reads=[bk(br), 'Mk'], writes=[('tD', s_)])
                    S.op('act', lambda e: e.activation(out=Dt[:, s_, :], in_=tD[:, s_, :], func=AF.Exp, bias=AB[:, col:col + 1], scale=1.0),
                         reads=[('tD', s_), 'AB'], writes=[('Dt', s_)])
                    S.op('act', lambda e: e.activation(out=EB[:, s_, :], in_=banks[br][:, 0:128], func=AF.Exp),
                         reads=[bk(br)], writes=[('EB', s_)])
                    S.op('dve', lambda e: e.tensor_tensor(out=Qs[:, s_, :], in0=QT[:, c * 128:(c + 1) * 128], in1=EB[:, s_, :], op=ALU.mult),
                         reads=['QT', ('EB', s_)], writes=[('Qs', s_)])
                    bs2 = next_bank()
                    S.op('pe', lambda e: e.matmul(banks[bs2][:, 0:128], lhsT=KT[:, c * 128:(c + 1) * 128], rhs=QT[:, c * 128:(c + 1) * 128],
                                                  start=True, stop=True), reads=['KT', 'QT'], writes=[bk(bs2)])
                    S.op('dve', lambda e: e.tensor_tensor(out=PT[:, s_, :], in0=banks[bs2][:, 0:128], in1=Dt[:, s_, :], op=ALU.mult),
                         reads=[bk(bs2), ('Dt', s_)], writes=[('PT', s_)])
                    bn = next_bank()
                    S.op('pe', lambda e: e.matmul(banks[bn][:, 0:257], lhsT=PT[:, s_, :], rhs=Vaug[:, c, 0:257], start=True, stop=False),
                         reads=[('PT', s_), ('Vaug', c), 'Vaug_ones'], writes=[bk(bn)])
                    S.op('pe', lambda e: e.matmul(banks[bn][:, 0:257], lhsT=Qs[:, s_, :], rhs=Sbf[:, d, 0:257], start=False, stop=True),
                         reads=[('Qs', s_), ('Sbf', d)], writes=[bk(bn)])
                    S.op('dve', lambda e: e.tensor_scalar(out=rr[:, s_, 2:3], in0=banks[bn][:, 256:257], scalar1=-1.0, scalar2=None,
                                                           op0=ALU.mult), reads=[bk(bn)], writes=[('rr2', s_)])
                    S.op('dve', lambda e: e.scalar_tensor_tensor(out=rr[:, s_, 0:1], in0=banks[bn][:, 256:257], scalar=1.0,
                                                                  in1=rr[:, s_, 2:3], op0=ALU.max, op1=ALU.max),
                         reads=[bk(bn), ('rr2', s_)], writes=[('rr0', s_)])
                    S.op('dve', lambda e: e.reciprocal(out=rr[:, s_, 1:2], in_=rr[:, s_, 0:1]), reads=[('rr0', s_)], writes=[('rr1', s_)])
                    if c not in hs_done:
                        hs_done.add(c)
                        S.op('dve', lambda e: e.tensor_scalar(out=Hs[:, c, :], in0=banks[bn][:, 0:256], scalar1=rr[:, s_, 1:2], scalar2=None,
                                                               op0=ALU.mult), reads=[bk(bn), ('rr1', s_)], writes=[('Hs', c)])
                    else:
                        S.op('dve', lambda e: e.scalar_tensor_tensor(out=Hs[:, c, :], in0=banks[bn][:, 0:256], scalar=rr[:, s_, 1:2],
                                                                      in1=Hs[:, c, :], op0=ALU.mult, op1=ALU.add),
                             reads=[bk(bn), ('rr1', s_), ('Hs', c)], writes=[('Hs', c)])
                    S.op('dve', lambda e: e.tensor_scalar(out=Kw[:, s_, :], in0=Ktm[:, c, :], scalar1=WL[:, col:col + 1], scalar2=None,
                                                           op0=ALU.mult), reads=[('Ktm', c), 'WL'], writes=[('Kw', s_)])
                    bu = next_bank()
                    S.op('pe', lambda e: e.matmul(banks[bu][:, 0:257], lhsT=Kw[:, s_, :], rhs=Vaug[:, c, 0:257], start=True, stop=True),
                         reads=[('Kw', s_), ('Vaug', c), 'Vaug_ones'], writes=[bk(bu)])
                    S.op('dve', lambda e: e.scalar_tensor_tensor(out=S32[:, d, :], in0=S32[:, d, :], scalar=DT[:, col:col + 1],
                                                                  in1=banks[bu][:, 0:257], op0=ALU.mult, op1=ALU.add),
                         reads=[('S32', d), 'DT', bk(bu)], writes=[('S32', d)])
                    S.op('act', lambda e: e.copy(out=Sbf[:, d, 0:257], in_=S32[:, d, :]), reads=[('S32', d)], writes=[('Sbf', d)])

                for h in range(8):
                    w, wk = loadw([(h * 128, 128), (1024 + h * 128, 128)])
                    for slot, dst, dkey, scl in ((0, QT, 'QT', 128 ** -0.5), (1, KT, 'KT', 1.0)):
                        for hf in range(2):
                            b = next_bank()
                            for kc in range(16):
                                S.op('pe', lambda e, kc=kc, b=b, hf=hf, slot=slot, w=w: e.matmul(
                                    banks[b][:], lhsT=w[:, kc, slot * 128:(slot + 1) * 128], rhs=hT[:, kc, hf * 512:(hf + 1) * 512],
                                    start=(kc == 0), stop=(kc == 15)), reads=[wk, ('hT', kc)], writes=[bk(b)])
                            S.op('act', lambda e, b=b, hf=hf, dst=dst, scl=scl: e.mul(out=dst[:, hf * 512:(hf + 1) * 512], in_=banks[b][:],
                                                                                      mul=scl), reads=[bk(b)], writes=[dkey])
                    for tt in range(8):
                        b = next_bank()
                        for kc in range(16):
                            S.op('pe', lambda e, kc=kc, b=b, tt=tt, w=w: e.matmul(
                                banks[b][:, 0:128], lhsT=hT[:, kc, tt * 128:(tt + 1) * 128], rhs=w[:, kc, 128:256],
                                start=(kc == 0), stop=(kc == 15)), reads=[wk, ('hT', kc)], writes=[bk(b)])
                        S.op('act', lambda e, b=b, tt=tt: e.copy(out=Ktm[:, tt, :], in_=banks[b][:, 0:128]),
                             reads=[bk(b)], writes=[('Ktm', tt)])
                    w, wk = loadw([(2048 + h * 256, 256)])
                    for tt in range(8):
                        b = next_bank()
                        for kc in range(16):
                            S.op('pe', lambda e, kc=kc, b=b, tt=tt, w=w: e.matmul(
                                banks[b][:, 0:256], lhsT=hT[:, kc, tt * 128:(tt + 1) * 128], rhs=w[:, kc, :],
                                start=(kc == 0), stop=(kc == 15)), reads=[wk, ('hT', kc)], writes=[bk(b)])
                        S.op('dve', lambda e, b=b, tt=tt: e.tensor_copy(out=Vaug[:, tt, 0:256], in_=banks[b][:, 0:256]),
                             reads=[bk(b)], writes=[('Vaug', tt)])
                    for d in range(2):
                        r0 = (d * 8 + h) * 128
                        dma('sp', S32[:, d, :], b_S0[r0:r0 + 128, :], writes=[('S32', d)])
                        S.op('dve', lambda e, d=d: e.tensor_scalar(out=S32[:, d, :], in0=S32[:, d, :],
                                                                    scalar1=m0e[:, d * 8 + h:d * 8 + h + 1], scalar2=None, op0=ALU.mult),
                             reads=[('S32', d), 'm0e'], writes=[('S32', d)])
                        S.op('act', lambda e, d=d: e.copy(out=Sbf[:, d, 0:257], in_=S32[:, d, :]), reads=[('S32', d)], writes=[('Sbf', d)])
                    hs_done.clear()
                    for n_ in range(8):
                        recs = []
                        for d in range(2):
                            c = n_ if d == 0 else 7 - n_

                            def step(d=d, c=c, n_=n_):
                                if n_ > 0 and n_ % 2 == 0:
                                    S.op('dve', lambda e: e.tensor_scalar(out=S32[:, d, :], in0=S32[:, d, :], scalar1=keep[:, 0:1],
                                                                           scalar2=None, op0=ALU.mult),
                                         reads=[('S32', d), 'keep'], writes=[('S32', d)])
                                    S.op('act', lambda e: e.copy(out=Sbf[:, d, 0:257], in_=S32[:, d, :]),
                                         reads=[('S32', d)], writes=[('Sbf', d)])
                                chunk(c, d, h, d)
                                if n_ % 2 == 1:
                                    j = c // 2
                                    jd = j * 2 + d
                                    S.op('dve', lambda e: e.tensor_scalar(
                                        out=outS[:, d, :], in0=S32[:, d, :], scalar1=SCB[:, jd * 8 + h:jd * 8 + h + 1], scalar2=None,
                                        op0=ALU.mult), reads=[('S32', d), 'SCB'], writes=[('outS', d)])
                                    r0 = (j * 16 + d * 8 + h) * 128
                                    dma('sp', b_So[r0:r0 + 128, :], outS[:, d, :], reads=[('outS', d)], pool='st')
                            recs.append(record(step))
                        for i in range(max(len(r_) for r_ in recs)):
                            for r_ in recs:
                                if i < len(r_):
                                    S.op(*r_[i][0], **r_[i][1])
                    w, wk = loadw([(4096 + h * 256, 256)])
                    dma('sp', nrm[:], b_nrm[:, h * 256:(h + 1) * 256], writes=['nrm'])
                    for tt in range(8):
                        os_ = tt % 2
                        b = next_bank()
                        for kc in range(16):
                            S.op('pe', lambda e, kc=kc, b=b, tt=tt, w=w: e.matmul(
                                banks[b][:, 0:256], lhsT=hT[:, kc, tt * 128:(tt + 1) * 128], rhs=w[:, kc, :],
                                start=(kc == 0), stop=(kc == 15)), reads=[wk, ('hT', kc)], writes=[bk(b)])
                        S.op('act', lambda e, b=b, os_=os_: e.activation(out=SIGO[:, os_, :], in_=banks[b][:, 0:256], func=AF.Sigmoid),
                             reads=[bk(b)], writes=[('SIGO', os_)])
                        S.op('act', lambda e, tt=tt: e.activation(out=sq_junk[:, 0:256], in_=Hs[:, tt, :], func=AF.Square,
                                                                   accum_out=sst[:, 0:1]),
                             reads=[('Hs', tt)], writes=['sq_junk', ('sst', 0)])
                        S.op('dve', lambda e: e.tensor_scalar(out=sst[:, 1:2], in0=sst[:, 0:1], scalar1=1.0 / 256, scalar2=EPS,
                                                               op0=ALU.mult, op1=ALU.add), reads=[('sst', 0)], writes=[('sst', 1)])
                        S.op('act', lambda e: e.activation(out=sst[:, 2:3], in_=sst[:, 1:2], func=AF.Ln),
                             reads=[('sst', 1)], writes=[('sst', 2)])
                        S.op('act', lambda e: e.activation(out=sst[:, 3:4], in_=sst[:, 2:3], func=AF.Exp, scale=-0.5),
                             reads=[('sst', 2)], writes=[('sst', 3)])
                        S.op('dve', lambda e, tt=tt: e.scalar_tensor_tensor(
                            out=hn, in0=Hs[:, tt, :], scalar=sst[:, 3:4], in1=nrm[:],
                            op0=ALU.mult, op1=ALU.mult), reads=[('Hs', tt), ('sst', 3), 'nrm'], writes=[('tmp_f', 0)])
                        S.op('dve', lambda e, tt=tt, os_=os_: e.tensor_tensor(out=onb[:, os_, :], in0=hn, in1=SIGO[:, os_, :], op=ALU.mult),
                             reads=[('tmp_f', 0), ('SIGO', os_)], writes=[('onb', os_)])
                        for ci in range(2):
                            b = next_bank()
                            pb = banks[b][:].bitcast(BF16)
                            S.op('pe', lambda e, pb=pb, ci=ci, os_=os_: e.transpose(out=pb[:, 0:128], in_=onb[:, os_, ci * 128:(ci + 1) * 128],
                                                                                    identity=ident_b[:]),
                                 reads=[('onb', os_), 'ident_b'], writes=[bk(b)])
                            och = 2 * h + ci
                            S.op('act', lambda e, pb=pb, och=och, tt=tt: e.copy(out=oT[:, och, tt * 128:(tt + 1) * 128], in_=pb[:, 0:128]),
                                 reads=[bk(b)], writes=[('oT', och)])
            S.barrier()
            out_proj(b_w_out, oT, st)
        S.barrier()

    def out_proj(w_out_d, oT, st):
        with ExitStack() as st3:
            wo = sb("wo", [128, 16, D], BF16, st3)
            for c4 in range(4):
                dma('pool', wo[:, :, c4 * 512:(c4 + 1) * 512],
                    w_out_d[:, c4 * 512:(c4 + 1) * 512].rearrange("(c p) n -> p c n", p=128), writes=[('wo', c4)])
            for tt in range(8):
                yb = [next_bank() for _ in range(4)]
                for q in range(4):
                    for kc in range(16):
                        S.op('pe', lambda e, q=q, kc=kc, tt=tt, yb=yb: e.matmul(
                            banks[yb[q]][:], lhsT=oT[:, kc, tt * 128:(tt + 1) * 128], rhs=wo[:, kc, q * 512:(q + 1) * 512],
                            start=(kc == 0), stop=(kc == 15)), reads=[('wo', q), ('oT', kc)], writes=[bk(yb[q])])
                post_norm(0, tt, yb)

    for l in layers:
        with ExitStack() as wst:
            modulation(l, wst)
        S.barrier()
        if l % 4 == 0:
            mixer_a(l)
        elif l % 4 == 1:
            mixer_b(l)
        elif l % 4 == 2:
            mixer_cd(l, 'c')
        elif l % 4 == 3:
            mixer_cd(l, 'd')
        if dbg == ('mid', l):
            for tt in range(8):
                dma('sp', dbg_out[tt * 128:(tt + 1) * 128, :], xres[:, tt, :], reads=[('x', tt)], pool='st')
        mlp(l)

    for tt in range(8):
        dma('sp', y_out[tt * 128:(tt + 1) * 128, :], xres[:, tt, :], reads=[('x', tt)], pool='st')
    if max_ops is not None:
        S.truncate(max_ops)
    info = S.emit()
    es.close()
    return nc, info


def fm(v):
    v = np.asarray(v, np.float32)
    return np.ascontiguousarray(np.moveaxis(v.reshape(v.shape[:-1] + (-1, 128)), -1, 0))


def make_in_maps(inp, nlw=4):
    maps = []
    ident = np.eye(128, dtype=np.float32)
    rot = rot_matrix()
    shared = dict(
        w_mod=inp['w_mod'][:nlw].reshape(nlw * D, 6 * D), w_ff1=inp['w_ff1'][:nlw].reshape(nlw * D, DFF),
        w_ff2=inp['w_ff2'][:nlw].reshape(nlw * DFF, D), ident=ident, rotm=rot,
        bmod=fm(inp['b_mod']).reshape(128, 4 * 96),
        gfm=fm(inp['g_norm']).reshape(128, 4 * 4 * 16),
        a_w_in=inp['a_w_in'][0], a_w_out=inp['a_w_out'][0],
        a_lam=np.ascontiguousarray(np.broadcast_to(inp['a_lambda'][0].reshape(1, 4096), (128, 4096))),
        a_sub=np.ascontiguousarray(np.broadcast_to(inp['a_subln'][0].reshape(1, 128), (128, 128))),
    )
    NA_L = [[0, 1, 2, 3], [0, 1, 2, 3], [0, 1, 2, 3, 4], [1, 2, 3, 4, 5], [2, 3, 4, 5, 6], [3, 4, 5, 6, 7], [4, 5, 6, 7], [4, 5, 6, 7]]
    rpb = np.asarray(inp['c_rpb'][0], np.float32).reshape(16, 465)
    ext = np.concatenate([rpb, np.full((16, 1), NEG, np.float32), np.zeros((16, 1), np.float32)], 1)
    cbias = {}
    dmask = {}
    for sample in (True, False):
        idx = np.full((8, 128, 5, 128), 465, np.int64)
        for qt in range(8):
            for si, kc in enumerate(NA_L[qt]):
                k = kc * 128 + np.arange(128)[:, None]
                q = qt * 128 + np.arange(128)[None, :]
                if sample:
                    rk, ck_, rq, cq_ = k // 64, k % 64, q // 64, q % 64
                    r0 = np.clip(rq - 4, 0, 8)
                    cs = np.clip(cq_ - 8, 0, 48)
                    valid = (rk >= r0) & (rk < r0 + 8) & (ck_ >= cs) & (ck_ < cs + 16)
                    ii = np.where(valid, (rk - rq + 7) * 31 + np.clip(ck_ - cq_, -15, 15) + 15, 465)
                else:
                    ii = np.full((128, 128), 466 if kc // 2 == qt // 2 else 465)
                idx[qt, :, si, :] = ii
        cbias[sample] = np.ascontiguousarray(ext[:, idx].reshape(16 * 8 * 128, 5 * 128))
        dm = np.full((128, 2, 8, 128), NEG, np.float32)
        kp = np.arange(128)[:, None]
        qq = np.arange(128)[None, :]
        for qt in range(8):
            if sample:
                dm[:, 0, qt, :] = np.where(kp >= qq, 0.0, NEG)
                dm[:, 1, qt, :] = np.where(kp <= qq, 0.0, NEG)
            else:
                dm[:, 0, qt, :] = 0.0 if qt % 2 == 1 else NEG
                dm[:, 1, qt, :] = 0.0 if qt % 2 == 0 else NEG
        dmask[sample] = dm.reshape(128, 2 * 8 * 128)
    shared.update(
        c_w_in=inp['c_w_in'][0], c_w_out=inp['c_w_out'][0], d_w_in=inp['d_w_in'][0], d_w_out=inp['d_w_out'][0],
        d_snk=np.ascontiguousarray(np.broadcast_to(inp['d_sink'][0].reshape(1, 32), (128, 32))))
    pi = np.arange(128)[:, None]
    ti = np.arange(128)[None, :]
    um = np.concatenate([(pi <= ti), (pi >= ti)], 1).astype(np.float32)
    mkb = np.concatenate([np.where(pi <= ti, 0.0, NEG), np.where(pi >= ti, 0.0, NEG)], 1).astype(np.float32)
    shared.update(
        b_w_in=inp['b_w_in'][0], b_w_out=inp['b_w_out'][0], b_um=um, b_mk=mkb,
        b_gb=np.ascontiguousarray(np.broadcast_to(inp['b_gate_bias'][0].reshape(1, 32), (128, 32))),
        b_nrm=np.ascontiguousarray(np.broadcast_to(inp['b_norm'][0].reshape(1, D), (128, D))))
    for core in range(8):
        sample = core < 4
        m = dict(shared)
        cb = core if sample else 0
        if sample:
            ct = np.swapaxes(inp['state_b_C'][core, 0], -1, -2)
            m['b_S0'] = np.concatenate([ct, inp['state_b_n'][core, 0][..., None]], -1).reshape(16 * 128, 257)
            m['b_m0'] = np.ascontiguousarray(np.broadcast_to(inp['state_b_m'][core, 0].reshape(1, 16), (128, 16)))
            m['b_keep'] = np.ones((128, 1), np.float32)
        else:
            m['b_S0'] = np.zeros((16 * 128, 257), np.float32)
            m['b_m0'] = np.zeros((128, 16), np.float32)
            m['b_keep'] = np.zeros((128, 1), np.float32)
        m['c_ck'] = inp['cache_c_k'][cb, 0].reshape(512, D)
        m['c_cv'] = inp['cache_c_v'][cb, 0].reshape(512, D)
        m['d_ck'] = inp['cache_d_k'][cb, 0].reshape(512, 512)
        m['d_cv'] = inp['cache_d_v'][cb, 0].reshape(512, 512)
        m['c_bias'] = cbias[sample]
        m['d_mask'] = dmask[sample]
        m['ctxb'] = np.full((128, 1), 0.0 if sample else NEG, np.float32)
        if sample:
            m['x'] = inp['x_sample'][core]
            m['cvec'] = fm(inp['c'][core])
            m['a_ck'] = inp['cache_a_k'][core, 0].reshape(512, D)
            m['a_cv'] = inp['cache_a_v'][core, 0].reshape(512, D)
            mask = np.zeros((12, 4), np.float32)
        else:
            p0 = (core - 4) * 4
            m['x'] = inp['x_prompt'][p0:p0 + 4].reshape(NT, D)
            m['cvec'] = fm(inp['c_ctx'])
            m['a_ck'] = inp['cache_a_k'][0, 0].reshape(512, D)
            m['a_cv'] = inp['cache_a_v'][0, 0].reshape(512, D)
            mask = np.full((12, 4), NEG, np.float32)
            for qp in range(4):
                mask[2 * qp:2 * qp + 2, qp] = 0.0
        m['a_mask'] = np.ascontiguousarray(np.broadcast_to(mask.reshape(1, 48), (128, 48)))
        cq, sq = rope_tables(sample, 0.125)
        ck, sk = rope_tables(sample, 1.0)
        m['a_cosq'], m['a_sinq'], m['a_cosk'], m['a_sink'] = cq, sq, ck, sk
        maps.append({k: (v if (v.dtype == np.float32 and v.flags['C_CONTIGUOUS']) else np.ascontiguousarray(v, dtype=np.float32))
                     for k, v in m.items()})
    return maps


def kernel(**inputs):
    inp = {k: np.asarray(v) for k, v in inputs.items()}
    nc, info = build()
    maps = make_in_maps(inp)
    res = run_bass_kernel_spmd(nc, maps, core_ids=list(range(8)))
    r = res.results
    y_sample = np.stack([r[c]['y'] for c in range(4)], 0)
    y_prompt = np.concatenate([r[c]['y'].reshape(4, 256, D) for c in range(4, 8)], 0)

    def ctx(name, hh, dd):
        return np.concatenate([r[c][name].reshape(4, 1, 256, hh, dd) for c in range(4, 8)], 0)
    so = np.concatenate([r[c]['b_So'].reshape(4, 1, 2, 8, 128, 257) for c in range(4, 8)], 0)
    b_C = np.ascontiguousarray(np.swapaxes(so[..., 0:256], -1, -2))
    b_n = np.ascontiguousarray(so[..., 256])
    b_m = np.concatenate([np.transpose(r[c]['b_mo'].reshape(8, 4, 1, 2), (1, 2, 3, 0)) for c in range(4, 8)], 0)
    b_m = np.ascontiguousarray(b_m)
    return (y_prompt, y_sample, ctx('a_k', 16, 128), ctx('a_v', 16, 128), b_C, b_n, b_m,
            ctx('c_k', 16, 128), ctx('c_v', 16, 128), ctx('d_k', 8, 64), ctx('d_v', 8, 64))
```
